# Optimizing a Trainium2 kernel written in Bass

```python
import math
import jax, jax.numpy as jnp
from jax import lax
import numpy as np

D_MODEL = 1024
BATCH = 2
SEQ = 8192
DEPTH = 2
DEC_BATCH = 128
DEC_SEQ = 1
PAST_LEN = 2048
PAGE_SIZE = 128

N_A_LAYERS = DEPTH // 2
N_B_LAYERS = DEPTH - N_A_LAYERS
D_RNN = D_MODEL
N_LRU_BLOCKS = 4
LRU_BLOCK = D_RNN // N_LRU_BLOCKS
CONV_W = 4
LRU_C = 8.0
N_HEADS = 16
HEAD_DIM = D_MODEL // N_HEADS
MOBA_BLOCK = 256
MOBA_TOPK = 3
Q_CHUNK = 64
D_FF = 4 * D_MODEL
EPS = 1e-6
NEG = -1e30

kernel_name = "yoco_rglru_moba_decoder_step"


def rmsnorm(x, g):
    xf = x.astype(jnp.float32)
    y = xf * lax.rsqrt(jnp.mean(xf * xf, axis=-1, keepdims=True) + EPS) * g.astype(jnp.float32)
    return y.astype(x.dtype)


def sqrelu_mlp(x, w1, w2):
    h = jax.nn.relu(x @ w1)
    return (h * h) @ w2


def causal_conv(u, w, b):
    T = u.shape[1] - (CONV_W - 1)
    out = b + u[:, 0:T] * w[0]
    for k in range(1, CONV_W):
        out = out + u[:, k:k + T] * w[k]
    return out


def rg_lru(x, h0, w_a, b_a, w_i, b_i, lam):
    Bn, T, _ = x.shape
    xb = x.reshape(Bn, T, N_LRU_BLOCKS, LRU_BLOCK)
    r = jax.nn.sigmoid(jnp.einsum('btnc,ncd->btnd', xb, w_a) + b_a).reshape(Bn, T, D_RNN)
    i = jax.nn.sigmoid(jnp.einsum('btnc,ncd->btnd', xb, w_i) + b_i).reshape(Bn, T, D_RNN)
    log_a = -LRU_C * r.astype(jnp.float32) * jax.nn.softplus(-lam.astype(jnp.float32))
    a = jnp.exp(log_a)
    mult = jnp.sqrt(-jnp.expm1(2.0 * log_a))
    bterm = mult * (i.astype(jnp.float32) * x.astype(jnp.float32))

    def step(h, ab):
        a_t, b_t = ab
        h = a_t * h + b_t
        return h, h

    h_last, hs = lax.scan(step, h0.astype(jnp.float32),
                          (jnp.swapaxes(a, 0, 1), jnp.swapaxes(bterm, 0, 1)))
    return jnp.swapaxes(hs, 0, 1).astype(x.dtype), h_last.astype(x.dtype)


def recurrent_block(xn, conv_buf, h0, w_in, b_in, conv_w, conv_b,
                    w_ga, b_ga, w_gi, b_gi, lam, w_out, b_out):
    proj = xn @ w_in + b_in
    gate, u = proj[..., :D_RNN], proj[..., D_RNN:]
    gate = jax.nn.gelu(gate)
    full = jnp.concatenate([conv_buf.astype(u.dtype), u], axis=1)
    new_buf = full[:, -(CONV_W - 1):]
    c = causal_conv(full, conv_w, conv_b)
    h, h_last = rg_lru(c, h0, w_ga, b_ga, w_gi, b_gi, lam)
    return (h * gate) @ w_out + b_out, new_buf, h_last


def moba_attention(q, k, v, q_pos):
    Bn, Lq = q.shape[:2]
    Lk = k.shape[1]
    pad = (-Lk) % MOBA_BLOCK
    k = jnp.pad(k, ((0, 0), (0, pad), (0, 0), (0, 0)))
    v = jnp.pad(v, ((0, 0), (0, pad), (0, 0), (0, 0)))
    NB = (Lk + pad) // MOBA_BLOCK
    kb = k.reshape(Bn, NB, MOBA_BLOCK, N_HEADS, HEAD_DIM).transpose(0, 3, 1, 2, 4)
    vb = v.reshape(Bn, NB, MOBA_BLOCK, N_HEADS, HEAD_DIM).transpose(0, 3, 1, 2, 4)
    k_mean = jnp.mean(kb.astype(jnp.float32), axis=3)
    chunk = math.gcd(Lq, Q_CHUNK)
    n_chunks = Lq // chunk
    qc = q.reshape(Bn, n_chunks, chunk, N_HEADS, HEAD_DIM).transpose(1, 0, 2, 3, 4)
    pc = q_pos.reshape(n_chunks, chunk)
    bi = jnp.arange(Bn)[:, None, None, None]
    hi = jnp.arange(N_HEADS)[None, None, :, None]
    scale = HEAD_DIM ** -0.5

    def one_chunk(args):
        qi, pos = args
        qblk = pos // MOBA_BLOCK
        gate = jnp.einsum('bchd,bhnd->bchn', qi.astype(jnp.float32), k_mean)
        past = jnp.arange(NB)[None, :] < qblk[:, None]
        gate = jnp.where(past[None, :, None, :], gate, NEG)
        if NB < MOBA_TOPK:
            gate = jnp.pad(gate, ((0, 0), (0, 0), (0, 0), (0, MOBA_TOPK - NB)), constant_values=NEG)
        _, top = lax.top_k(gate, MOBA_TOPK)
        top = jnp.minimum(top, NB - 1)
        own = jnp.broadcast_to(qblk[None, :, None, None], top.shape[:3] + (1,))
        idx = jnp.concatenate([top, own.astype(top.dtype)], axis=-1)
        slot_ok = jnp.concatenate([jnp.arange(MOBA_TOPK)[None, :] < qblk[:, None],
                                   jnp.ones((qblk.shape[0], 1), bool)], axis=-1)
        kg = kb[bi, hi, idx]
        vg = vb[bi, hi, idx]
        key_pos = idx[..., None] * MOBA_BLOCK + jnp.arange(MOBA_BLOCK)
        mask = slot_ok[None, :, None, :, None] & (key_pos <= pos[None, :, None, None, None])
        s = jnp.einsum('bchd,bchskd->bchsk', qi, kg).astype(jnp.float32) * scale
        s = jnp.where(mask, s, NEG)
        sh = s.shape
        p = jax.nn.softmax(s.reshape(sh[:3] + (-1,)), axis=-1).reshape(sh).astype(vg.dtype)
        return jnp.einsum('bchsk,bchskd->bchd', p, vg)

    out = lax.map(one_chunk, (qc, pc))
    return out.transpose(1, 0, 2, 3, 4).reshape(Bn, Lq, N_HEADS * HEAD_DIM)


def setup_inputs(seed: int = 0) -> dict:
    key = jax.random.key(seed)
    ks = jax.random.split(key, 32)
    f32 = jnp.float32
    n_pages = PAST_LEN // PAGE_SIZE
    n_used = DEC_BATCH * n_pages
    n_phys = n_used + (n_used + 3) // 4

    def nrm(k, shape, scale):
        return jax.random.normal(k, shape, f32) * scale

    u = jax.random.uniform(ks[20], (N_A_LAYERS, D_RNN), f32, 0.9, 0.999)
    s = u ** (1.0 / LRU_C)
    lru_lambda = jnp.log(s) - jnp.log1p(-s)
    return {
        "x_prompt": nrm(ks[0], (BATCH, SEQ, D_MODEL), 1.0),
        "x_sample": nrm(ks[1], (DEC_BATCH, DEC_SEQ, D_MODEL), 1.0),
        "state_conv": nrm(ks[2], (N_A_LAYERS, DEC_BATCH, CONV_W - 1, D_RNN), 1.0),
        "state_h": nrm(ks[3], (N_A_LAYERS, DEC_BATCH, D_RNN), 0.5),
        "cache_k": nrm(ks[4], (n_phys, PAGE_SIZE, N_HEADS, HEAD_DIM), 1.0),
        "cache_v": nrm(ks[5], (n_phys, PAGE_SIZE, N_HEADS, HEAD_DIM), 1.0),
        "page_table": jax.random.permutation(ks[6], n_phys)[:n_used].reshape(DEC_BATCH, n_pages).astype(jnp.int32),
        "norm_mix": 1.0 + nrm(ks[7], (DEPTH, D_MODEL), 0.05),
        "norm_mlp": 1.0 + nrm(ks[8], (DEPTH, D_MODEL), 0.05),
        "w_ff1": nrm(ks[9], (DEPTH, D_MODEL, D_FF), D_MODEL ** -0.5),
        "w_ff2": nrm(ks[10], (DEPTH, D_FF, D_MODEL), D_FF ** -0.5),
        "w_rg_in": nrm(ks[11], (N_A_LAYERS, D_MODEL, 2 * D_RNN), D_MODEL ** -0.5),
        "b_rg_in": nrm(ks[12], (N_A_LAYERS, 2 * D_RNN), 0.01),
        "conv_w": nrm(ks[13], (N_A_LAYERS, CONV_W, D_RNN), CONV_W ** -0.5),
        "conv_b": nrm(ks[14], (N_A_LAYERS, D_RNN), 0.01),
        "w_gate_a": nrm(ks[15], (N_A_LAYERS, N_LRU_BLOCKS, LRU_BLOCK, LRU_BLOCK), LRU_BLOCK ** -0.5),
        "b_gate_a": nrm(ks[16], (N_A_LAYERS, N_LRU_BLOCKS, LRU_BLOCK), 0.01),
        "w_gate_i": nrm(ks[17], (N_A_LAYERS, N_LRU_BLOCKS, LRU_BLOCK, LRU_BLOCK), LRU_BLOCK ** -0.5),
        "b_gate_i": nrm(ks[18], (N_A_LAYERS, N_LRU_BLOCKS, LRU_BLOCK), 0.01),
        "lru_lambda": lru_lambda,
        "w_rg_out": nrm(ks[21], (N_A_LAYERS, D_RNN, D_MODEL), D_RNN ** -0.5),
        "b_rg_out": nrm(ks[22], (N_A_LAYERS, D_MODEL), 0.01),
        "norm_kv": 1.0 + nrm(ks[23], (D_MODEL,), 0.05),
        "w_kv": nrm(ks[24], (D_MODEL, 2 * N_HEADS * HEAD_DIM), D_MODEL ** -0.5),
        "w_q": nrm(ks[25], (N_B_LAYERS, D_MODEL, N_HEADS * HEAD_DIM), D_MODEL ** -0.5),
        "w_o": nrm(ks[26], (N_B_LAYERS, N_HEADS * HEAD_DIM, D_MODEL), D_MODEL ** -0.5),
        "norm_out": 1.0 + nrm(ks[27], (D_MODEL,), 0.05),
    }


def reference(x_prompt, x_sample, state_conv, state_h, cache_k, cache_v, page_table,
              norm_mix, norm_mlp, w_ff1, w_ff2, w_rg_in, b_rg_in, conv_w, conv_b,
              w_gate_a, b_gate_a, w_gate_i, b_gate_i, lru_lambda, w_rg_out, b_rg_out,
              norm_kv, w_kv, w_q, w_o, norm_out):
    HD = N_HEADS * HEAD_DIM

    def trunk(x, conv0, h0, past_k, past_v):
        Bn, L, _ = x.shape
        q_pos = past_k.shape[1] + jnp.arange(L, dtype=jnp.int32)
        new_conv, new_h = [], []
        k_all = v_all = k_new = v_new = None
        for layer in range(DEPTH):
            hn = rmsnorm(x, norm_mix[layer])
            if layer < N_A_LAYERS:
                a = layer
                out, buf, hl = recurrent_block(hn, conv0[a], h0[a], w_rg_in[a], b_rg_in[a],
                                               conv_w[a], conv_b[a], w_gate_a[a], b_gate_a[a],
                                               w_gate_i[a], b_gate_i[a], lru_lambda[a],
                                               w_rg_out[a], b_rg_out[a])
                new_conv.append(buf)
                new_h.append(hl)
            else:
                b = layer - N_A_LAYERS
                q = (hn @ w_q[b]).reshape(Bn, L, N_HEADS, HEAD_DIM)
                out = moba_attention(q, k_all, v_all, q_pos) @ w_o[b]
            x = x + out
            x = x + sqrelu_mlp(rmsnorm(x, norm_mlp[layer]), w_ff1[layer], w_ff2[layer])
            if layer == N_A_LAYERS - 1:
                kv = rmsnorm(x, norm_kv) @ w_kv
                k_new = kv[..., :HD].reshape(Bn, L, N_HEADS, HEAD_DIM)
                v_new = kv[..., HD:].reshape(Bn, L, N_HEADS, HEAD_DIM)
                k_all = jnp.concatenate([past_k.astype(k_new.dtype), k_new], axis=1)
                v_all = jnp.concatenate([past_v.astype(v_new.dtype), v_new], axis=1)
        return rmsnorm(x, norm_out), jnp.stack(new_conv), jnp.stack(new_h), k_new, v_new

    dt = x_prompt.dtype
    empty = jnp.zeros((BATCH, 0, N_HEADS, HEAD_DIM), dt)
    y_prompt, conv_p, h_p, k_p, v_p = trunk(
        x_prompt,
        jnp.zeros((N_A_LAYERS, BATCH, CONV_W - 1, D_RNN), dt),
        jnp.zeros((N_A_LAYERS, BATCH, D_RNN), dt),
        empty, empty)

    n_pages = PAST_LEN // PAGE_SIZE
    past_k = cache_k[page_table].reshape(DEC_BATCH, n_pages * PAGE_SIZE, N_HEADS, HEAD_DIM)
    past_v = cache_v[page_table].reshape(DEC_BATCH, n_pages * PAGE_SIZE, N_HEADS, HEAD_DIM)
    y_sample, conv_s, h_s, k_s, v_s = trunk(x_sample, state_conv, state_h, past_k, past_v)

    return (y_prompt, y_sample, conv_p, h_p, k_p, v_p, conv_s, h_s, k_s, v_s)
```

```python
import contextlib
import numpy as np
import ml_dtypes
import concourse.bass as bass
import concourse.mybir as mybir
from concourse.bass_utils import run_bass_kernel_spmd

F32 = mybir.dt.float32
BF16 = mybir.dt.bfloat16
I32 = mybir.dt.int32
AF = mybir.ActivationFunctionType
ALU = mybir.AluOpType
AX = mybir.AxisListType

D = 1024
DFF = 4096
NH = 16
HD = 64
EPS = 1e-6
NS = 16
NPG = 16
NEGB = -30000.0
NV = 80


class Prog:
    def __init__(self, nc):
        self.nc = nc
        self.E = dict(pe=nc.tensor, act=nc.scalar, dve=nc.vector, pool=nc.gpsimd, sp=nc.sync)
        self.sem = {}
        self.cnt = {}
        for e in self.E:
            self.sem[e] = nc.alloc_semaphore("s_" + e)
            self.cnt[e] = 0
        self.known = {e: {} for e in self.E}
        self.lastw = {}
        self.readers = {}
        self.dk = {}
        self.nps = 0

    def _wait(self, eng, deps):
        best = {}
        for t in deps:
            if t is None:
                continue
            k, v = t
            if eng == 'pe' and k == 'pe':
                continue
            if v > best.get(k, 0):
                best[k] = v
        for k, v in best.items():
            if self.known[eng].get(k, 0) >= v:
                continue
            sem = self.sem[k] if k in self.sem else self.dk[k][0]
            self.E[eng].wait_ge(sem, v)
            self.known[eng][k] = v

    def _deps(self, reads, writes):
        deps = []
        for b in reads:
            deps.append(self.lastw.get(b))
        for b in writes:
            deps.append(self.lastw.get(b))
            deps.extend(self.readers.get(b, {}).items())
        return deps

    def _record(self, tok, reads, writes):
        k, v = tok
        for b in reads:
            self.readers.setdefault(b, {})[k] = v
        for b in writes:
            self.lastw[b] = tok
            self.readers[b] = {}

    def op(self, eng, fn, reads=(), writes=()):
        self._wait(eng, self._deps(reads, writes))
        ins = fn()
        self.cnt[eng] += 1
        ins.then_inc(self.sem[eng], 1)
        self._record((eng, self.cnt[eng]), reads, writes)

    def dma(self, q, key, fn, reads=(), writes=()):
        self._wait(q, self._deps(reads, writes))
        ins = fn()
        if key not in self.dk:
            self.dk[key] = [self.nc.alloc_semaphore("d_" + key), 0]
        self.dk[key][1] += 16
        ins.then_inc(self.dk[key][0], 16)
        self._record((key, self.dk[key][1]), reads, writes)

    def barrier(self):
        for e in self.E:
            deps = [(k, self.cnt[k]) for k in self.E if self.cnt[k] > 0]
            deps += [(k, v[1]) for k, v in self.dk.items()]
            self._wait(e, deps)
        self.lastw = {}
        self.readers = {}


def build(G, NPHYS=2560):
    T = 2048 * G
    NCH = T // 512
    nc = bass.Bass("TRN2", target_bir_lowering=False)
    P = Prog(nc)

    def din(name, shape, dt=F32):
        return nc.dram_tensor(name, list(shape), dt, kind="ExternalInput").ap()

    def dout(name, shape, dt=F32):
        return nc.dram_tensor(name, list(shape), dt, kind="ExternalOutput").ap()

    def dscr(name, shape, dt=F32):
        return nc.dram_tensor(name, list(shape), dt, kind="Internal").ap()

    xp = din("xp", [T, D])
    w_ff1 = din("w_ff1", [2, D, DFF])
    w_ff2 = din("w_ff2", [2, DFF, D])
    w_rg_in = din("w_rg_in", [D, 2 * D])
    w_ga = din("w_ga", [4, 256, 256])
    w_gi = din("w_gi", [4, 256, 256])
    w_rg_out = din("w_rg_out", [D, D])
    w_kv = din("w_kv", [D, 2 * D])
    w_q = din("w_q", [D, D])
    w_o = din("w_o", [D, D])
    vecs_d = din("vecs", [128, NV])
    rows_d = din("rows", [8, 128, D])
    qidx_d = din("qidx", [128, 4 * G], I32)
    ktidx_d = din("ktidx", [128, 16 * G], I32)
    pastb_d = din("pastb", [128, 4 * G, 32])
    past01_d = din("past01", [128, 4 * G, 32])
    ind_d = din("ind", [32, T], BF16)
    cmask_d = din("cmask", [128, 2, 256], BF16)
    ident_d = din("ident", [128, 128], BF16)
    xs_d = din("xs", [128, D])
    sconv_d = din("sconv", [128, 3, D])
    sh_d = din("sh", [128, D])
    ck_d = din("ck", [NPHYS * 128, D])
    cv_d = din("cv", [NPHYS * 128, D])
    ptab_d = din("ptab", [NS, NPG], I32)
    ownidx_d = din("ownidx", [128, 1], I32)
    ownbc_d = din("ownbc", [128, NS], I32)
    iop_d = din("iop", [128, 1])
    y_out = dout("y", [512 * G, D])
    k_out = dout("k", [T, D])
    v_out = dout("v", [T, D])
    conv_out = dout("convp", [3, D])
    h_out = dout("hp", [D])
    ys_out = dout("ys", [128, D])
    convs_out = dout("convs", [128, 3, D])
    hs_out = dout("hs", [128, D])
    ks_out = dout("ks", [128, D])
    vs_out = dout("vs", [128, D])
    XS1 = dscr("XS1", [128, D])
    XS2 = dscr("XS2", [128, D])
    XS3 = dscr("XS3", [128, D])
    KVS = dscr("KVS", [128, 2 * D])
    Qd = dscr("Qd", [128, D])
    Dn = dscr("Dn", [NS, 16])
    X1 = dscr("X1", [T, D])
    X2 = dscr("X2", [T, D])
    X3 = dscr("X3", [512 * G, D])
    KTd = dscr("KTd", [NCH * NH * 128, 512], BF16)
    Vd = dscr("Vd", [NH, T, 65], BF16)
    Vd2 = dscr("Vd2", [T, NH * 65], BF16)

    es = contextlib.ExitStack()

    uniq = [0]

    def sb(stack, name, shape, dt):
        uniq[0] += 1
        return stack.enter_context(nc.sbuf_tensor("sb%d_%s" % (uniq[0], name), list(shape), dt))

    def pst(stack, name, shape, dt):
        return stack.enter_context(nc.psum_tensor(name, list(shape), dt))

    with es:
        ident = sb(es, "ident", [128, 128], BF16)
        vecs = sb(es, "vecs", [128, NV], F32)
        grow = sb(es, "grow", [128, D], F32)
        ssq = sb(es, "ssq", [128, 4], F32)
        msq = sb(es, "msq", [128, 4], F32)
        rstd = sb(es, "rstd", [128, 4], F32)
        junk = sb(es, "junk", [128, D], BF16)
        P.dma('sp', 'c_ident', lambda: nc.sync.dma_start(out=ident[:, :], in_=ident_d[:, :]), [], ['ident'])
        P.dma('sp', 'c_vecs', lambda: nc.sync.dma_start(out=vecs[:, :], in_=vecs_d[:, :]), [], ['vecs'])

        psT = [pst(es, "psT%d" % i, [128, 1024], BF16) for i in range(2)]
        psM = [pst(es, "psM%d" % i, [128, 512], F32) for i in range(6)]
        st = dict(t=0, m=0)

        def next_psT():
            st['t'] += 1
            i = st['t'] % 2
            return psT[i], "psT%d" % i

        def next_psM(nm=4, base=0):
            st['m'] += 1
            i = base + st['m'] % nm
            return psM[i], "psM%d" % i

        def load_grow(idx):
            P.dma('sp', 'c_grow', lambda: nc.sync.dma_start(out=grow[:, :], in_=rows_d[idx, :, :]), [], ['grow'])

        def norm_T(xsub, xname, nsub, nt, xn, xnT, do_T=True, gname='grow', gt=None, xnn='xn', xnTn='xnT'):
            g_t = grow if gt is None else gt
            for j in range(nsub):
                P.op('act', lambda: nc.scalar.activation(out=junk[:nt, :], in_=xsub(j), func=AF.Square,
                                                          accum_out=ssq[:nt, j:j + 1]), [xname], ['junk', 'ssq'])
            P.op('dve', lambda: nc.vector.tensor_scalar(out=msq[:nt, :nsub], in0=ssq[:nt, :nsub], scalar1=1.0 / D,
                                                        scalar2=EPS, op0=ALU.mult, op1=ALU.add), ['ssq'], ['msq'])
            P.op('act', lambda: nc.scalar.activation(out=msq[:nt, :nsub], in_=msq[:nt, :nsub], func=AF.Sqrt),
                 ['msq'], ['msq'])
            P.op('dve', lambda: nc.vector.reciprocal(out=rstd[:nt, :nsub], in_=msq[:nt, :nsub]), ['msq'], ['rstd'])
            for j in range(nsub):
                P.op('dve', lambda: nc.vector.scalar_tensor_tensor(out=xn[:nt, j, :], in0=xsub(j),
                                                                    scalar=rstd[:nt, j:j + 1], in1=g_t[:nt, :],
                                                                    op0=ALU.mult, op1=ALU.mult),
                     [xname, 'rstd', gname], [xnn])
            if not do_T:
                return
            for c in range(8):
                ps, pn = next_psT()
                for j in range(nsub):
                    P.op('pe', lambda: nc.tensor.transpose(out=ps[:, j * nt:(j + 1) * nt],
                                                           in_=xn[:nt, j, c * 128:(c + 1) * 128],
                                                           identity=ident[:nt, :nt]), [xnn, 'ident'], [pn])
                if c % 2 == 0:
                    P.op('act', lambda: nc.scalar.copy(out=xnT[:, c, :nsub * nt], in_=ps[:, :nsub * nt]), [pn], [xnTn])
                else:
                    P.op('dve', lambda: nc.vector.tensor_copy(out=xnT[:, c, :nsub * nt], in_=ps[:, :nsub * nt]),
                         [pn], [xnTn])

        def load_w(stack_w, name, src, shape_kc, ncols, splits=1):
            t = sb(stack_w, name, [128, shape_kc, ncols], BF16)
            w = ncols // splits
            for s_ in range(splits):
                P.dma('pool', 'w_' + name, lambda: nc.gpsimd.dma_start(
                    out=t[:, :, s_ * w:(s_ + 1) * w],
                    in_=src[:, s_ * w:(s_ + 1) * w].rearrange("(kc p) f -> p kc f", p=128)), [], [name])
            return t

        TA = 256
        with contextlib.ExitStack() as sa:
            w_in_sb = load_w(sa, "w_in", w_rg_in, 8, 2048)
            wga = sb(sa, "wga", [128, 4, 2, 256], BF16)
            wgi = sb(sa, "wgi", [128, 4, 2, 256], BF16)
            P.dma('pool', 'w_wga', lambda: nc.gpsimd.dma_start(
                out=wga[:, :, :, :], in_=w_ga.rearrange("n (kc p) f -> p n kc f", p=128)), [], ['wga'])
            P.dma('pool', 'w_wgi', lambda: nc.gpsimd.dma_start(
                out=wgi[:, :, :, :], in_=w_gi.rearrange("n (kc p) f -> p n kc f", p=128)), [], ['wgi'])
            wout_sb = load_w(sa, "w_out", w_rg_out, 8, 1024)
            brow = sb(sa, "brow", [128, D], F32)
            P.dma('sp', 'c_brow', lambda: nc.sync.dma_start(out=brow[:, :], in_=rows_d[6, :, :]), [], ['brow'])
            load_grow(0)
            sc = sb(sa, "sc", [128, 8], F32)
            sc2 = sb(sa, "sc2", [128, 8], F32)
            P.op('act', lambda: nc.scalar.activation(out=sc[:, :], in_=vecs[:, 72:80], func=AF.Exp, scale=-1.0),
                 ['vecs'], ['sc'])
            P.op('act', lambda: nc.scalar.activation(out=sc[:, :], in_=sc[:, :], func=AF.Ln, bias=1.0), ['sc'], ['sc'])
            P.op('dve', lambda: nc.vector.tensor_scalar(out=sc2[:, :], in0=sc[:, :], scalar1=-16.0, scalar2=None,
                                                        op0=ALU.mult), ['sc'], ['sc2'])
            P.op('dve', lambda: nc.vector.tensor_scalar(out=sc[:, :], in0=sc[:, :], scalar1=-8.0, scalar2=None,
                                                        op0=ALU.mult), ['sc'], ['sc'])
            xt3 = [sb(sa, "xtA%d" % i, [128, 2, D], F32) for i in range(3)]
            xn2 = [sb(sa, "xnA%d" % i, [128, 2, D], BF16) for i in range(2)]
            xnT2 = [sb(sa, "xnTA%d" % i, [128, 8, TA], BF16) for i in range(2)]
            uT2 = [sb(sa, "uT%d" % i, [128, 8, TA + 3], F32) for i in range(2)]
            GX2 = [sb(sa, "GX%d" % i, [128, 8, TA], F32) for i in range(2)]
            Z = sb(sa, "Z", [128, 8, TA], F32)
            C = sb(sa, "C", [128, 8, TA], F32)
            Cbf = sb(sa, "Cbf", [128, 8, TA], BF16)
            R = sb(sa, "R", [128, 8, TA], F32)
            I_ = sb(sa, "I", [128, 8, TA], F32)
            E_ = sb(sa, "E", [128, 8, TA], F32)
            H = sb(sa, "H", [128, 8, TA], F32)
            HG = sb(sa, "HG", [128, 8, TA], BF16)
            hst = sb(sa, "hst", [128, 8], F32)
            UTM = sb(sa, "UTM", [128, D], F32)
            urow = sb(sa, "urow", [128, D], F32)
            id32 = sb(sa, "id32", [128, 128], F32)
            P.dma('sp', 'c_urow', lambda: nc.sync.dma_start(out=urow[:, :], in_=rows_d[7, :, :]), [], ['urow'])
            P.op('dve', lambda: nc.vector.tensor_copy(out=id32[:, :], in_=ident[:, :]), ['ident'], ['id32'])
            P.op('dve', lambda: nc.vector.memset(uT2[0][:, :, 0:3], 0.0), [], ['uTh0'])
            P.op('dve', lambda: nc.vector.memset(hst[:, :], 0.0), [], ['hst'])

            jobs = [('p', i) for i in range(T // TA)] + [('s', 0)]

            def jvars(k):
                kind, i = jobs[k]
                smp = (kind == 's')
                NCl = 128 if smp else TA
                nsub = 1 if smp else 2
                t0 = i * TA
                sl2 = k % 2
                xs = xt3[k % 3]
                xsn = "xtA%d" % (k % 3)
                return kind, i, smp, NCl, nsub, t0, sl2, xs, xsn

            def front(k):
                kind, i, smp, NCl, nsub, t0, sl2, xs, xsn = jvars(k)
                xn, xnT, uT, GX = xn2[sl2], xnT2[sl2], uT2[sl2], GX2[sl2]
                XN, XNT, UT, GXn = 'xn%d' % sl2, 'xnT%d' % sl2, 'uT%d' % sl2, 'GX%d' % sl2
                if smp:
                    P.dma('sp', xsn, lambda: nc.sync.dma_start(out=xs[:, 0, :], in_=xs_d[:, :]), [], [xsn])
                    for k3 in range(2):
                        P.dma('sp', 'o_cs', lambda: nc.sync.dma_start(out=convs_out[:, k3, :],
                                                                       in_=sconv_d[:, k3 + 1, :]), [], [])
                else:
                    P.dma('sp', xsn, lambda: nc.sync.dma_start(
                        out=xs[:, :, :], in_=xp[t0:t0 + TA, :].rearrange("(j p) f -> p j f", p=128)), [], [xsn])
                norm_T(lambda j: xs[:, j, :], xsn, nsub, 128, xn, xnT, xnn=XN, xnTn=XNT)
                for oc in range(16):
                    ps, pn = next_psM()
                    for kc in range(8):
                        P.op('pe', lambda: nc.tensor.matmul(ps[:, :NCl], lhsT=w_in_sb[:, kc, oc * 128:(oc + 1) * 128],
                                                            rhs=xnT[:, kc, :NCl], start=(kc == 0), stop=(kc == 7)),
                             ['w_in', XNT], [pn])
                    if oc < 8:
                        P.op('act', lambda: nc.scalar.activation(out=GX[:, oc, :NCl], in_=ps[:, :NCl], func=AF.Identity,
                                                                  bias=vecs[:, oc:oc + 1]), [pn, 'vecs'], [GXn])
                    else:
                        c = oc - 8
                        P.op('dve', lambda: nc.vector.tensor_scalar(out=uT[:, c, 3:3 + NCl], in0=ps[:, :NCl],
                                                                    scalar1=vecs[:, 8 + c:9 + c], scalar2=None,
                                                                    op0=ALU.add), [pn, 'vecs'], [UT])
                if smp:
                    for hf in range(2):
                        ps, pn = next_psM()
                        for kc in range(8):
                            P.op('pe', lambda: nc.tensor.matmul(ps[:, :], lhsT=xnT[:, kc, 0:128],
                                                                rhs=w_in_sb[:, kc, D + hf * 512:D + (hf + 1) * 512],
                                                                start=(kc == 0), stop=(kc == 7)), ['w_in', XNT], [pn])
                        P.op('dve', lambda: nc.vector.tensor_tensor(out=UTM[:, hf * 512:(hf + 1) * 512], in0=ps[:, :],
                                                                    in1=urow[:, hf * 512:(hf + 1) * 512], op=ALU.add),
                             [pn, 'urow'], ['UTM'])
                    P.dma('sp', 'o_cs2', lambda: nc.sync.dma_start(out=convs_out[:, 2, :], in_=UTM[:, :]), ['UTM'], [])
                Zg = Z
                Zn = 'Z'
                P.op('pool', lambda: nc.gpsimd.tensor_tensor(out=Zg[:, :, :NCl], in0=GX[:, :, :NCl], in1=GX[:, :, :NCl],
                                                             op=ALU.mult), [GXn], [Zn])
                P.op('dve', lambda: nc.vector.tensor_scalar(out=Zg[:, :, :NCl], in0=Zg[:, :, :NCl], scalar1=0.044715,
                                                            scalar2=1.0, op0=ALU.mult, op1=ALU.add), [Zn], [Zn])
                P.op('pool', lambda: nc.gpsimd.tensor_tensor(out=Zg[:, :, :NCl], in0=Zg[:, :, :NCl], in1=GX[:, :, :NCl],
                                                             op=ALU.mult), [Zn, GXn], [Zn])
                P.op('act', lambda: nc.scalar.activation(out=Zg[:, :, :NCl], in_=Zg[:, :, :NCl], func=AF.Sigmoid,
                                                          scale=1.5957691216057308), [Zn], [Zn])
                P.op('pool', lambda: nc.gpsimd.tensor_tensor(out=GX[:, :, :NCl], in0=GX[:, :, :NCl], in1=Zg[:, :, :NCl],
                                                             op=ALU.mult), [Zn, GXn], [GXn])

            def back(k):
                kind, i, smp, NCl, nsub, t0, sl2, xs, xsn = jvars(k)
                xn, xnT, uT, GX = xn2[sl2], xnT2[sl2], uT2[sl2], GX2[sl2]
                XN, XNT, UT, GXn = 'xn%d' % sl2, 'xnT%d' % sl2, 'uT%d' % sl2, 'GX%d' % sl2
                UTH = 'uTh%d' % sl2
                uTn = uT2[1 - sl2]
                UTHn = 'uTh%d' % (1 - sl2)
                if smp:
                    for k3 in range(4):
                        srcs = xs[:, 1, :]
                        if k3 < 3:
                            P.dma('sp', xsn, lambda: nc.sync.dma_start(out=xs[:, 1, :], in_=sconv_d[:, k3, :]),
                                  [], [xsn])
                        else:
                            P.dma('sp', xsn, lambda: nc.sync.dma_start(out=xs[:, 1, :], in_=sh_d[:, :]), [], [xsn])
                        for c in range(8):
                            ps, pn = next_psM()
                            P.op('pe', lambda: nc.tensor.transpose(out=ps[:, 0:128], in_=xs[:, 1, c * 128:(c + 1) * 128],
                                                                   identity=id32[:, :]), [xsn, 'id32'], [pn])
                            dstt, dn, o_ = ((Z, 'Z', 0), (Z, 'Z', 128), (E_, 'E', 0), (E_, 'E', 128))[k3]
                            P.op('act', lambda: nc.scalar.copy(out=dstt[:, c, o_:o_ + 128], in_=ps[:, 0:128]),
                                 [pn], [dn])
                        if k3 == 0:
                            pass
                for c in range(8):
                    if smp:
                        taps = [Z[:, c, 0:128], Z[:, c, 128:256], E_[:, c, 0:128], uT[:, c, 3:3 + 128]]
                    else:
                        taps = [uT[:, c, k:k + TA] for k in range(4)]
                    P.op('dve', lambda: nc.vector.tensor_scalar(out=C[:, c, :NCl], in0=taps[0],
                                                                scalar1=vecs[:, 16 + c:17 + c],
                                                                scalar2=vecs[:, 48 + c:49 + c],
                                                                op0=ALU.mult, op1=ALU.add), [UT, UTH, 'vecs', 'Z', 'E'], ['C'])
                    for k in range(1, 4):
                        P.op('dve', lambda: nc.vector.scalar_tensor_tensor(
                            out=C[:, c, :NCl], in0=taps[k], scalar=vecs[:, 16 + 8 * k + c:17 + 8 * k + c],
                            in1=C[:, c, :NCl], op0=ALU.mult, op1=ALU.add), [UT, UTH, 'vecs', 'C', 'Z', 'E'], ['C'])
                P.op('act', lambda: nc.scalar.copy(out=Cbf[:, :, :NCl], in_=C[:, :, :NCl]), ['C'], ['Cbf'])
                if kind == 'p' and i == T // TA - 1:
                    with nc.allow_non_contiguous_dma(reason="tiny state output"):
                        for k3 in range(3):
                            P.dma('sp', 'o_conv', lambda: nc.sync.dma_start(
                                out=conv_out[k3].rearrange("(c p) -> p c", p=128), in_=uT[:, :, TA + k3]),
                                [UT], [])
                if not smp:
                    P.op('pool', lambda: nc.gpsimd.tensor_copy(out=uTn[:, :, 0:3], in_=uT[:, :, TA:TA + 3]),
                         [UT], [UTHn])
                for c in range(8):
                    n, dc = c // 2, c % 2
                    for (wg, dst, dn, bo) in ((wga, R, 'R', 56), (wgi, I_, 'I', 64)):
                        ps, pn = next_psM()
                        for kc in range(2):
                            P.op('pe', lambda: nc.tensor.matmul(ps[:, :NCl], lhsT=wg[:, n, kc, dc * 128:(dc + 1) * 128],
                                                                rhs=Cbf[:, 2 * n + kc, :NCl], start=(kc == 0),
                                                                stop=(kc == 1)), ['wga', 'wgi', 'Cbf'], [pn])
                        P.op('act', lambda: nc.scalar.activation(out=dst[:, c, :NCl], in_=ps[:, :NCl], func=AF.Sigmoid,
                                                                  bias=vecs[:, bo + c:bo + c + 1]), [pn, 'vecs'], [dn])
                E2 = H if smp else E_
                E2n = 'H' if smp else 'E'
                for c in range(8):
                    P.op('act', lambda: nc.scalar.activation(out=E2[:, c, :NCl], in_=R[:, c, :NCl], func=AF.Exp,
                                                              scale=sc2[:, c:c + 1]), ['R', 'sc2'], [E2n])
                for c in range(8):
                    P.op('act', lambda: nc.scalar.activation(out=R[:, c, :NCl], in_=R[:, c, :NCl], func=AF.Exp,
                                                              scale=sc[:, c:c + 1]), ['R', 'sc'], ['R'])
                P.op('act', lambda: nc.scalar.activation(out=E2[:, :, :NCl], in_=E2[:, :, :NCl], func=AF.Sqrt,
                                                          scale=-1.0, bias=1.0), [E2n], [E2n])
                P.op('pool', lambda: nc.gpsimd.tensor_tensor(out=I_[:, :, :NCl], in0=I_[:, :, :NCl], in1=E2[:, :, :NCl],
                                                             op=ALU.mult), ['I', E2n], ['I'])
                P.op('pool', lambda: nc.gpsimd.tensor_tensor(out=I_[:, :, :NCl], in0=I_[:, :, :NCl], in1=C[:, :, :NCl],
                                                             op=ALU.mult), ['I', 'C'], ['I'])
                if smp:
                    P.op('dve', lambda: nc.vector.tensor_tensor(out=H[:, :, :128], in0=R[:, :, :128],
                                                                in1=E_[:, :, 128:256], op=ALU.mult),
                         ['R', 'E', 'I'], ['H'])
                    P.op('dve', lambda: nc.vector.tensor_tensor(out=H[:, :, :128], in0=H[:, :, :128],
                                                                in1=I_[:, :, :128], op=ALU.add), ['H', 'I'], ['H'])
                    for c in range(8):
                        ps, pn = next_psM()
                        P.op('pe', lambda: nc.tensor.transpose(out=ps[:, 0:128], in_=H[:, c, 0:128],
                                                               identity=id32[:, :]), ['H', 'id32'], [pn])
                        P.op('act', lambda: nc.scalar.copy(out=UTM[:, c * 128:(c + 1) * 128], in_=ps[:, 0:128]),
                             [pn], ['UTM'])
                    P.dma('sp', 'o_hs', lambda: nc.sync.dma_start(out=hs_out[:, :], in_=UTM[:, :]), ['UTM'], [])
                else:
                    for c in range(8):
                        P.op('dve', lambda: nc.vector.tensor_tensor_scan(out=H[:, c, :], data0=R[:, c, :],
                                                                         data1=I_[:, c, :], initial=hst[:, c:c + 1],
                                                                         op0=ALU.mult, op1=ALU.add),
                             ['R', 'I', 'hst'], ['H'])
                    P.op('dve', lambda: nc.vector.tensor_copy(out=hst[:, :], in_=H[:, :, TA - 1]), ['H'], ['hst'])
                P.op('pool', lambda: nc.gpsimd.tensor_tensor(out=HG[:, :, :NCl], in0=H[:, :, :NCl], in1=GX[:, :, :NCl],
                                                             op=ALU.mult), ['H', GXn], ['HG'])
                for j in range(nsub):
                    for hf in range(2):
                        ps, pn = next_psM()
                        for kc in range(8):
                            P.op('pe', lambda: nc.tensor.matmul(ps[:, :], lhsT=HG[:, kc, j * 128:(j + 1) * 128],
                                                                rhs=wout_sb[:, kc, hf * 512:(hf + 1) * 512],
                                                                start=(kc == 0), stop=(kc == 7)), ['HG', 'w_out'], [pn])
                        P.op('dve', lambda: nc.vector.tensor_tensor(out=xs[:, j, hf * 512:(hf + 1) * 512],
                                                                    in0=ps[:, :], in1=xs[:, j, hf * 512:(hf + 1) * 512],
                                                                    op=ALU.add), [pn, xsn], [xsn])
                    P.op('pool', lambda: nc.gpsimd.tensor_tensor(out=xs[:, j, :], in0=xs[:, j, :], in1=brow[:, :],
                                                                 op=ALU.add), [xsn, 'brow'], [xsn])
                if smp:
                    P.dma('sp', xsn + 's', lambda: nc.sync.dma_start(out=XS1[:, :], in_=xs[:, 0, :]), [xsn], [])
                else:
                    P.dma('sp', xsn + 's', lambda: nc.sync.dma_start(
                        out=X1[t0:t0 + TA, :].rearrange("(j p) f -> p j f", p=128), in_=xs[:, :, :]), [xsn], [])
                    if i == T // TA - 1:
                        with nc.allow_non_contiguous_dma(reason="tiny state output"):
                            P.dma('sp', 'o_h', lambda: nc.sync.dma_start(out=h_out.rearrange("(c p) -> p c", p=128),
                                                                         in_=hst[:, :]), ['hst'], [])

            front(0)
            for k in range(len(jobs)):
                if k + 1 < len(jobs):
                    front(k + 1)
                back(k)
            P.barrier()

        def phase_mlp(layer, jobs, grow_idx, final_idx=None):
            TBm = 256
            with contextlib.ExitStack() as sm:
                ff1 = load_w(sm, "ff1", w_ff1[layer], 8, DFF, splits=2)
                ff2 = load_w(sm, "ff2", w_ff2[layer], 32, D)
                load_grow(grow_idx)
                if final_idx is not None:
                    grow2 = sb(sm, "grow2", [128, D], F32)
                    P.dma('sp', 'c_grow2', lambda: nc.sync.dma_start(out=grow2[:, :], in_=rows_d[final_idx, :, :]),
                          [], ['grow2'])
                    yt = sb(sm, "yt", [128, 2, D], F32)
                xt = [sb(sm, "xtB%d" % i, [128, 2, D], F32) for i in range(2)]
                xn = sb(sm, "xnB", [128, 2, D], BF16)
                xnT = sb(sm, "xnTB", [128, 8, TBm], BF16)
                RL = [sb(sm, "RL%d" % i, [128, TBm], F32) for i in range(2)]
                HT = sb(sm, "HT", [128, 32, TBm], BF16)
                ti = 0
                for (src, dst, ntok, TB) in jobs:
                    nsub = TB // 128
                    for i in range(ntok // TB):
                        t0 = i * TB
                        xs = xt[ti % 2]
                        xsn = "xtB%d" % (ti % 2)
                        ti += 1
                        P.dma('sp', xsn, lambda: nc.sync.dma_start(
                            out=xs[:, 0:nsub, :], in_=src[t0:t0 + TB, :].rearrange("(j p) f -> p j f", p=128)),
                            [], [xsn])
                        norm_T(lambda j: xs[:, j, :], xsn, nsub, 128, xn, xnT)
                        for oc in range(32):
                            ps, pn = next_psM()
                            for kc in range(8):
                                P.op('pe', lambda: nc.tensor.matmul(ps[:, :TB], lhsT=ff1[:, kc, oc * 128:(oc + 1) * 128],
                                                                    rhs=xnT[:, kc, :TB], start=(kc == 0),
                                                                    stop=(kc == 7)), ['ff1', 'xnT'], [pn])
                            rl = RL[oc % 2]
                            rln = "RL%d" % (oc % 2)
                            P.op('act', lambda: nc.scalar.activation(out=rl[:, :TB], in_=ps[:, :TB], func=AF.Relu),
                                 [pn], [rln])
                            P.op('pool', lambda: nc.gpsimd.tensor_tensor(out=HT[:, oc, :TB], in0=rl[:, :TB],
                                                                         in1=rl[:, :TB], op=ALU.mult), [rln], ['HT'])
                        for j in range(nsub):
                            for hf in range(2):
                                ps, pn = next_psM()
                                for kc in range(32):
                                    P.op('pe', lambda: nc.tensor.matmul(ps[:, :], lhsT=HT[:, kc, j * 128:(j + 1) * 128],
                                                                        rhs=ff2[:, kc, hf * 512:(hf + 1) * 512],
                                                                        start=(kc == 0), stop=(kc == 31)),
                                         ['HT', 'ff2'], [pn])
                                P.op('dve', lambda: nc.vector.tensor_tensor(out=xs[:, j, hf * 512:(hf + 1) * 512],
                                                                            in0=ps[:, :],
                                                                            in1=xs[:, j, hf * 512:(hf + 1) * 512],
                                                                            op=ALU.add), [pn, xsn], [xsn])
                        if final_idx is None:
                            P.dma('sp', xsn + 's', lambda: nc.sync.dma_start(
                                out=dst[t0:t0 + TB, :].rearrange("(j p) f -> p j f", p=128), in_=xs[:, 0:nsub, :]),
                                [xsn], [])
                        else:
                            norm_T(lambda j: xs[:, j, :], xsn, nsub, 128, yt, None, do_T=False, gname='grow2', gt=grow2)
                            P.dma('sp', 'yts', lambda: nc.sync.dma_start(
                                out=dst[t0:t0 + TB, :].rearrange("(j p) f -> p j f", p=128), in_=yt[:, 0:nsub, :]),
                                ['xn'], [])
                P.barrier()

        phase_mlp(0, [(X1, X2, T, 256), (XS1, XS2, 128, 128)], 1)

        TC = 256
        with contextlib.ExitStack() as sc_:
            wkv = load_w(sc_, "wkv", w_kv, 8, 2048)
            load_grow(2)
            xt = [sb(sc_, "xtC%d" % i, [128, 2, D], F32) for i in range(2)]
            xn = sb(sc_, "xnC", [128, 2, D], BF16)
            xnT = sb(sc_, "xnTC", [128, 8, TC], BF16)
            KTs = sb(sc_, "KTs", [128, 8, TC], BF16)
            KVo = sb(sc_, "KVo", [128, 2, 2 * D], F32)
            VAs = sb(sc_, "VAs", [128, 2, NH, 65], BF16)
            P.op('dve', lambda: nc.vector.memset(VAs[:, :, :, :], 1.0), [], ['VAs'])
            KTv = KTd.rearrange("(c h r) t -> c h r t", h=NH, r=128)
            for i in range(T // TC):
                t0 = i * TC
                xs = xt[i % 2]
                xsn = "xtC%d" % (i % 2)
                P.dma('sp', xsn, lambda: nc.sync.dma_start(
                    out=xs[:, :, :], in_=X2[t0:t0 + TC, :].rearrange("(j p) f -> p j f", p=128)), [], [xsn])
                norm_T(lambda j: xs[:, j, :], xsn, 2, 128, xn, xnT)
                for oc in range(8):
                    ps, pn = next_psM()
                    for kc in range(8):
                        P.op('pe', lambda: nc.tensor.matmul(ps[:, :TC], lhsT=wkv[:, kc, oc * 128:(oc + 1) * 128],
                                                            rhs=xnT[:, kc, :], start=(kc == 0), stop=(kc == 7)),
                             ['wkv', 'xnT'], [pn])
                    P.op('act', lambda: nc.scalar.copy(out=KTs[:, oc, :], in_=ps[:, :TC]), [pn], ['KTs'])
                ch, off = t0 // 512, t0 % 512
                for hh in range(2):
                    P.dma('sp', 'KTs%d' % hh, lambda: nc.sync.dma_start(
                        out=KTv[ch, hh::2, 0:64, off:off + TC].rearrange("h r t -> r h t"),
                        in_=KTs[hh * 64:(hh + 1) * 64, :, :]), ['KTs'], [])
                for j in range(2):
                    for q4 in range(4):
                        ps, pn = next_psM()
                        for kc in range(8):
                            P.op('pe', lambda: nc.tensor.matmul(ps[:, :], lhsT=xnT[:, kc, j * 128:(j + 1) * 128],
                                                                rhs=wkv[:, kc, q4 * 512:(q4 + 1) * 512],
                                                                start=(kc == 0), stop=(kc == 7)), ['wkv', 'xnT'], [pn])
                        if q4 % 2 == 0:
                            P.op('act', lambda: nc.scalar.copy(out=KVo[:, j, q4 * 512:(q4 + 1) * 512], in_=ps[:, :]),
                                 [pn], ['KVo'])
                        else:
                            P.op('dve', lambda: nc.vector.tensor_copy(out=KVo[:, j, q4 * 512:(q4 + 1) * 512],
                                                                      in_=ps[:, :]), [pn], ['KVo'])
                    P.op('pool', lambda: nc.gpsimd.tensor_copy(
                        out=VAs[:, j, :, 0:64], in_=KVo[:, j, D:2 * D].rearrange("p (h e) -> p h e", e=64)),
                        ['KVo'], ['VAs'])
                P.dma('sp', 'KVo_k', lambda: nc.sync.dma_start(
                    out=k_out[t0:t0 + TC, :].rearrange("(j p) f -> p j f", p=128), in_=KVo[:, :, 0:D]), ['KVo'], [])
                P.dma('sp', 'KVo_v', lambda: nc.sync.dma_start(
                    out=v_out[t0:t0 + TC, :].rearrange("(j p) f -> p j f", p=128), in_=KVo[:, :, D:2 * D]), ['KVo'], [])
                for j in range(2):
                    P.dma('sp', 'VAs_1', lambda: nc.sync.dma_start(
                        out=Vd[:, t0 + j * 128:t0 + (j + 1) * 128, :].rearrange("h p e -> p h e"),
                        in_=VAs[:, j, :, :]), ['VAs'], [])
                P.dma('sp', 'VAs_2', lambda: nc.sync.dma_start(
                    out=Vd2[t0:t0 + TC, :].rearrange("(j p) f -> p j f", p=128),
                    in_=VAs[:, :, :, :].rearrange("p j h e -> p j (h e)")), ['VAs'], [])
            xs = xt[0]
            P.dma('sp', 'xtC0', lambda: nc.sync.dma_start(out=xs[:, 0, :], in_=XS2[:, :]), [], ['xtC0'])
            norm_T(lambda j: xs[:, j, :], 'xtC0', 1, 128, xn, xnT)
            for q4 in range(4):
                ps, pn = next_psM()
                for kc in range(8):
                    P.op('pe', lambda: nc.tensor.matmul(ps[:, :], lhsT=xnT[:, kc, 0:128],
                                                        rhs=wkv[:, kc, q4 * 512:(q4 + 1) * 512],
                                                        start=(kc == 0), stop=(kc == 7)), ['wkv', 'xnT'], [pn])
                P.op('act', lambda: nc.scalar.copy(out=KVo[:, 0, q4 * 512:(q4 + 1) * 512], in_=ps[:, :]),
                     [pn], ['KVo'])
            P.dma('sp', 'KVo_k', lambda: nc.sync.dma_start(out=ks_out[:, :], in_=KVo[:, 0, 0:D]), ['KVo'], [])
            P.dma('sp', 'KVo_v', lambda: nc.sync.dma_start(out=vs_out[:, :], in_=KVo[:, 0, D:2 * D]), ['KVo'], [])
            P.dma('sp', 'KVo_s', lambda: nc.sync.dma_start(out=KVS[:, :], in_=KVo[:, 0, :]), ['KVo'], [])
            P.barrier()


        PB = 4
        with contextlib.ExitStack() as ss:
            wqS = load_w(ss, "wqS", w_q, 8, 1024)
            woS = load_w(ss, "woS", w_o, 8, 1024)
            load_grow(3)
            xts = sb(ss, "xtS", [128, 1, D], F32)
            xns = sb(ss, "xnS", [128, 1, D], BF16)
            xnTs = sb(ss, "xnTS", [128, 8, 128], BF16)
            Qs = sb(ss, "Qs", [128, D], F32)
            id32 = sb(ss, "id32S", [128, 128], F32)
            ones32 = sb(ss, "ones32", [128, 128], F32)
            onesb = sb(ss, "onesb", [128, 128], BF16)
            ownidx = sb(ss, "ownidx", [128, 1], I32)
            ownbc = sb(ss, "ownbc", [128, NS], I32)
            iop = sb(ss, "iop", [128, 1], F32)
            ptb_i = sb(ss, "ptb_i", [128, NS * NPG], I32)
            ptb_f = sb(ss, "ptb_f", [128, NS * NPG], F32)
            pidx = sb(ss, "pidx", [128, NS * NPG], I32)
            Qo = sb(ss, "Qo", [NS, D], F32)
            KVo_ = sb(ss, "KVown", [NS, 2 * D], F32)
            x2o = sb(ss, "x2o", [128, 1, D], F32)
            Kp = [sb(ss, "Kp%d" % i, [128, PB, D], BF16) for i in range(2)]
            Vp = [sb(ss, "Vp%d" % i, [128, NPG, D], BF16) for i in range(2)]
            qb = [sb(ss, "qb%d" % i, [128, D], BF16) for i in range(2)]
            prod = sb(ss, "prod", [128, PB, D], F32)
            Sq = sb(ss, "Sq", [128, NPG, 16], F32)
            Eq = sb(ss, "Eq", [128, NPG, 16], BF16)
            gateS = sb(ss, "gateS", [128, 8, 16], F32)
            top8s = sb(ss, "top8s", [128, 16, 8], F32)
            selS = sb(ss, "selS", [128, 8, 16], F32)
            numT = sb(ss, "numT", [128, 8, NS], F32)
            denr = sb(ss, "denr", [128, 16], F32)
            num = sb(ss, "num", [NS, D], F32)
            dens = sb(ss, "dens", [NS, 16], F32)
            snew = sb(ss, "snew", [NS, 16], F32)
            pr0 = sb(ss, "pr0", [NS, D], F32)
            attnb = sb(ss, "attnb", [128, 1, D], BF16)
            attnT = sb(ss, "attnT", [128, 8, 128], BF16)
            P.op('dve', lambda: nc.vector.tensor_copy(out=id32[:, :], in_=ident[:, :]), ['ident'], ['id32S'])
            P.op('dve', lambda: nc.vector.memset(ones32[:, :], 1.0), [], ['ones32'])
            P.op('dve', lambda: nc.vector.memset(onesb[:, :], 1.0), [], ['onesb'])
            P.op('dve', lambda: nc.vector.memset(x2o[:, :, :], 0.0), [], ['x2o'])
            P.op('pool', lambda: nc.gpsimd.memset(attnb[:, :, :], 0.0), [], ['attnb'])
            P.dma('sp', 'c_own', lambda: nc.sync.dma_start(out=ownidx[:, :], in_=ownidx_d[:, :]), [], ['ownidx'])
            P.dma('sp', 'c_ownbc', lambda: nc.sync.dma_start(out=ownbc[:, :], in_=ownbc_d[:, :]), [], ['ownbc'])
            P.dma('sp', 'c_iop', lambda: nc.sync.dma_start(out=iop[:, :], in_=iop_d[:, :]), [], ['iop'])
            P.dma('sp', 'c_ptab', lambda: nc.sync.dma_start(
                out=ptb_i[:, :], in_=ptab_d.rearrange("b s -> (b s)").partition_broadcast(128)), [], ['ptb_i'])
            P.op('dve', lambda: nc.vector.tensor_copy(out=ptb_f[:, :], in_=ptb_i[:, :]), ['ptb_i'], ['ptb_f'])
            P.op('dve', lambda: nc.vector.tensor_scalar(out=ptb_f[:, :], in0=ptb_f[:, :], scalar1=128.0,
                                                        scalar2=iop[:, 0:1], op0=ALU.mult, op1=ALU.add),
                 ['ptb_f', 'iop'], ['ptb_f'])
            P.op('dve', lambda: nc.vector.tensor_copy(out=pidx[:, :], in_=ptb_f[:, :]), ['ptb_f'], ['pidx'])
            P.dma('sp', 'xtS', lambda: nc.sync.dma_start(out=xts[:, 0, :], in_=XS2[:, :]), [], ['xtS'])
            norm_T(lambda j: xts[:, j, :], 'xtS', 1, 128, xns, xnTs)
            for hf in range(2):
                ps, pn = next_psM()
                for kc in range(8):
                    P.op('pe', lambda: nc.tensor.matmul(ps[:, :], lhsT=xnTs[:, kc, :],
                                                        rhs=wqS[:, kc, hf * 512:(hf + 1) * 512],
                                                        start=(kc == 0), stop=(kc == 7)), ['wqS', 'xnT'], [pn])
                P.op('act', lambda: nc.scalar.copy(out=Qs[:, hf * 512:(hf + 1) * 512], in_=ps[:, :]), [pn], ['Qs'])
            P.dma('sp', 'Qd', lambda: nc.sync.dma_start(out=Qd[:, :], in_=Qs[:, :]), ['Qs'], ['Qd'])
            P.dma('pool', 'Qo', lambda: nc.gpsimd.indirect_dma_start(
                out=Qo[:, :], out_offset=None, in_=Qd[:, :],
                in_offset=bass.IndirectOffsetOnAxis(ap=ownidx[0:NS, 0:1], axis=0)), ['ownidx', 'Qd'], ['Qo'])
            P.dma('pool', 'KVown', lambda: nc.gpsimd.indirect_dma_start(
                out=KVo_[:, :], out_offset=None, in_=KVS[:, :],
                in_offset=bass.IndirectOffsetOnAxis(ap=ownidx[0:NS, 0:1], axis=0)), ['ownidx'], ['KVown'])
            P.dma('pool', 'x2o', lambda: nc.gpsimd.indirect_dma_start(
                out=x2o[0:NS, 0, :], out_offset=None, in_=XS2[:, :],
                in_offset=bass.IndirectOffsetOnAxis(ap=ownidx[0:NS, 0:1], axis=0)), ['ownidx'], ['x2o'])
            for i in range(NS):
                sl = i % 2
                P.dma('pool', 'qb%d' % sl, lambda: nc.gpsimd.indirect_dma_start(
                    out=qb[sl][:, :], out_offset=None, in_=Qd[:, :],
                    in_offset=bass.IndirectOffsetOnAxis(ap=ownbc[:, i:i + 1], axis=0)), ['ownbc', 'Qd'], ['qb%d' % sl])
                for s_ in range(NPG):
                    P.dma('pool', 'Vp%d' % sl, lambda: nc.gpsimd.indirect_dma_start(
                        out=Vp[sl][:, s_, :], out_offset=None, in_=cv_d[:, :],
                        in_offset=bass.IndirectOffsetOnAxis(ap=pidx[:, i * NPG + s_:i * NPG + s_ + 1], axis=0)),
                        ['pidx'], ['Vp%d' % sl])
                for bq in range(NPG // PB):
                    ksl = (i * (NPG // PB) + bq) % 2
                    for g_ in range(PB):
                        s_ = bq * PB + g_
                        P.dma('pool', 'Kp%d' % ksl, lambda: nc.gpsimd.indirect_dma_start(
                            out=Kp[ksl][:, g_, :], out_offset=None, in_=ck_d[:, :],
                            in_offset=bass.IndirectOffsetOnAxis(ap=pidx[:, i * NPG + s_:i * NPG + s_ + 1], axis=0)),
                            ['pidx'], ['Kp%d' % ksl])
                    P.op('dve', lambda: nc.vector.tensor_tensor(
                        out=prod[:, :, :], in0=Kp[ksl][:, :, :],
                        in1=qb[sl][:, :].unsqueeze(1).broadcast_to([128, PB, D]), op=ALU.mult),
                        ['Kp%d' % ksl, 'qb%d' % sl], ['prod'])
                    P.op('dve', lambda: nc.vector.tensor_reduce(
                        out=Sq[:, bq * PB:(bq + 1) * PB, :], in_=prod[:, :, :].rearrange("p g (h e) -> p g h e", e=64),
                        axis=AX.X, op=ALU.add), ['prod'], ['Sq'])
                ps, pn = next_psM()
                P.op('pe', lambda: nc.tensor.matmul(ps[:, 0:256], lhsT=ones32[:, :],
                                                    rhs=Sq[:, :, :].rearrange("p s h -> p (s h)"), start=True, stop=True),
                     ['ones32', 'Sq'], [pn])
                gv = ps[:, 0:256].rearrange("p (n t h) -> p n t h", t=2, h=16)
                P.op('dve', lambda: nc.vector.tensor_copy(out=gateS[:, :, :], in_=gv[:, :, 0, :]), [pn], ['gateS'])
                P.op('dve', lambda: nc.vector.tensor_tensor(out=gateS[:, :, :], in0=gv[:, :, 1, :], in1=gateS[:, :, :],
                                                            op=ALU.add), [pn, 'gateS'], ['gateS'])
                for h in range(NH):
                    P.op('dve', lambda: nc.vector.max(out=top8s[:, h, :], in_=gateS[:, :, h]), ['gateS'], ['top8s'])
                P.op('dve', lambda: nc.vector.tensor_tensor(
                    out=selS[:, :, :], in0=gateS[:, :, :],
                    in1=top8s[:, :, 2:3].rearrange("p h o -> p o h").broadcast_to([128, 8, 16]), op=ALU.is_ge),
                    ['gateS', 'top8s'], ['selS'])
                P.op('act', lambda: nc.scalar.activation(out=Sq[:, :, :], in_=Sq[:, :, :], func=AF.Exp, scale=0.125),
                     ['Sq'], ['Sq'])
                for t_ in range(2):
                    P.op('dve', lambda: nc.vector.tensor_tensor(
                        out=Eq[:, :, :].rearrange("p (n t) h -> p n t h", t=2)[:, :, t_, :],
                        in0=Sq[:, :, :].rearrange("p (n t) h -> p n t h", t=2)[:, :, t_, :], in1=selS[:, :, :],
                        op=ALU.mult), ['Sq', 'selS'], ['Eq'])
                ps, pn = next_psM()
                P.op('pe', lambda: nc.tensor.matmul(ps[:, 0:256], lhsT=onesb[:, :],
                                                    rhs=Eq[:, :, :].rearrange("p s h -> p (s h)"), start=True, stop=True),
                     ['onesb', 'Eq'], [pn])
                P.op('dve', lambda: nc.vector.tensor_reduce(out=denr[:, :],
                                                            in_=ps[:, 0:256].rearrange("p (s h) -> p h s", h=16),
                                                            axis=AX.X, op=ALU.add), [pn], ['denr'])
                P.dma('sp', 'Dn', lambda: nc.sync.dma_start(out=Dn[i:i + 1, :], in_=denr[0:1, :]), ['denr'], ['Dn'])
                ps, pn = next_psM()
                for c in range(8):
                    for s_ in range(NPG):
                        P.op('pe', lambda: nc.tensor.matmul(ps[:, c * 16:(c + 1) * 16],
                                                            lhsT=Vp[sl][:, s_, c * 128:(c + 1) * 128], rhs=Eq[:, s_, :],
                                                            start=(s_ == 0), stop=(s_ == NPG - 1)),
                             ['Vp%d' % sl, 'Eq'], [pn])
                P.op('act', lambda: nc.scalar.copy(out=numT[0:64, :, i], in_=ps[0:64, 0:127:18]), [pn], ['numT'])
                P.op('dve', lambda: nc.vector.tensor_copy(out=numT[64:128, :, i], in_=ps[64:128, 1:128:18]),
                     [pn], ['numT'])
            ps, pn = next_psM()
            ps2, pn2 = next_psM()
            for c in range(8):
                pp_ = ps if c < 4 else ps2
                P.op('pe', lambda: nc.tensor.transpose(out=pp_[0:NS, (c % 4) * 128:(c % 4 + 1) * 128],
                                                       in_=numT[:, c, :], identity=id32[:, :]),
                     ['numT', 'id32S'], [pn if c < 4 else pn2])
            P.op('act', lambda: nc.scalar.copy(out=num[:, 0:512], in_=ps[0:NS, :]), [pn], ['num'])
            P.op('dve', lambda: nc.vector.tensor_copy(out=num[:, 512:1024], in_=ps2[0:NS, :]), [pn2], ['num'])
            P.dma('sp', 'dens', lambda: nc.sync.dma_start(out=dens[:, :], in_=Dn[:, :]), ['Dn'], ['dens'])
            P.op('dve', lambda: nc.vector.tensor_tensor(out=pr0[:, :], in0=Qo[:, :], in1=KVo_[:, 0:D], op=ALU.mult),
                 ['Qo', 'KVown'], ['pr0'])
            P.op('dve', lambda: nc.vector.tensor_reduce(out=snew[:, :], in_=pr0[:, :].rearrange("p (h e) -> p h e", e=64),
                                                        axis=AX.X, op=ALU.add), ['pr0'], ['snew'])
            P.op('act', lambda: nc.scalar.activation(out=snew[:, :], in_=snew[:, :], func=AF.Exp, scale=0.125),
                 ['snew'], ['snew'])
            P.op('dve', lambda: nc.vector.tensor_tensor(
                out=pr0[:, :].rearrange("p (h e) -> p h e", e=64), in0=KVo_[:, D:2 * D].rearrange("p (h e) -> p h e", e=64),
                in1=snew[:, :].unsqueeze(2).broadcast_to([NS, 16, 64]), op=ALU.mult), ['KVown', 'snew'], ['pr0'])
            P.op('dve', lambda: nc.vector.tensor_tensor(out=num[:, :], in0=num[:, :], in1=pr0[:, :], op=ALU.add),
                 ['num', 'pr0'], ['num'])
            P.op('dve', lambda: nc.vector.tensor_tensor(out=dens[:, :], in0=dens[:, :], in1=snew[:, :], op=ALU.add),
                 ['dens', 'snew'], ['dens'])
            P.op('dve', lambda: nc.vector.reciprocal(out=dens[:, :], in_=dens[:, :]), ['dens'], ['dens'])
            P.op('dve', lambda: nc.vector.tensor_tensor(
                out=attnb[0:NS, 0, :].rearrange("p (h e) -> p h e", e=64), in0=num[:, :].rearrange("p (h e) -> p h e", e=64),
                in1=dens[:, :].unsqueeze(2).broadcast_to([NS, 16, 64]), op=ALU.mult), ['num', 'dens', 'attnb'], ['attnb'])
            for c in range(8):
                pt_, ptn = next_psT()
                P.op('pe', lambda: nc.tensor.transpose(out=pt_[:, 0:128], in_=attnb[:, 0, c * 128:(c + 1) * 128],
                                                       identity=ident[:, :]), ['attnb', 'ident'], [ptn])
                P.op('act', lambda: nc.scalar.copy(out=attnT[:, c, :], in_=pt_[:, 0:128]), [ptn], ['attnT'])
            for hf in range(2):
                ps, pn = next_psM()
                for c in range(8):
                    P.op('pe', lambda: nc.tensor.matmul(ps[:, :], lhsT=attnT[:, c, :], rhs=woS[:, c, hf * 512:(hf + 1) * 512],
                                                        start=(c == 0), stop=(c == 7)), ['attnT', 'woS'], [pn])
                P.op('dve', lambda: nc.vector.tensor_tensor(out=x2o[:, 0, hf * 512:(hf + 1) * 512], in0=ps[:, :],
                                                            in1=x2o[:, 0, hf * 512:(hf + 1) * 512], op=ALU.add),
                     [pn, 'x2o'], ['x2o'])
            P.dma('sp', 'x2os', lambda: nc.sync.dma_start(out=XS3[:, :], in_=x2o[:, 0, :]), ['x2o'], [])
            P.barrier()

        SEGL = 4096
        with contextlib.ExitStack() as sd:
            wq = load_w(sd, "wq", w_q, 8, 1024)
            woH = sb(sd, "woH", [64, NH, D], BF16)
            P.dma('pool', 'w_woH', lambda: nc.gpsimd.dma_start(
                out=woH[:, :, :], in_=w_o.rearrange("(h d) f -> d h f", d=64)), [], ['woH'])
            load_grow(3)
            qidx = sb(sd, "qidx", [128, 4 * G], I32)
            ktidx = sb(sd, "ktidx", [128, 16 * G], I32)
            pastb = sb(sd, "pastb", [128, 4 * G, 32], F32)
            past01 = sb(sd, "past01", [128, 4 * G, 32], F32)
            cmask = sb(sd, "cmask", [128, 2, 256], BF16)
            onesf = sb(sd, "onesf", [128, 64], F32)
            P.dma('sp', 'c_qidx', lambda: nc.sync.dma_start(out=qidx[:, :], in_=qidx_d[:, :]), [], ['qidx'])
            P.dma('sp', 'c_ktidx', lambda: nc.sync.dma_start(out=ktidx[:, :], in_=ktidx_d[:, :]), [], ['ktidx'])
            P.dma('sp', 'c_pastb', lambda: nc.sync.dma_start(out=pastb[:, :, :], in_=pastb_d[:, :, :]), [], ['pastb'])
            P.dma('sp', 'c_past01', lambda: nc.sync.dma_start(out=past01[:, :, :], in_=past01_d[:, :, :]), [], ['past01'])
            P.dma('sp', 'c_cmask', lambda: nc.sync.dma_start(out=cmask[:, :, :], in_=cmask_d[:, :, :]), [], ['cmask'])
            P.op('dve', lambda: nc.vector.memset(onesf[:, :], 1.0), [], ['onesf'])
            KA = [sb(sd, "KA%d" % i, [96, SEGL], BF16) for i in range(2)]
            VA = [sb(sd, "VA%d" % i, [128, SEGL // 128, 65], BF16) for i in range(2)]
            KD = [sb(sd, "KD%d" % i, [128, 512], BF16) for i in range(2)]
            VD = sb(sd, "VD", [128, 4, NH * 65], BF16)
            xt = sb(sd, "xtD", [128, 4, D], F32)
            xn = sb(sd, "xnD", [128, 4, D], BF16)
            xnT = sb(sd, "xnTD", [128, 8, 512], BF16)
            QA = sb(sd, "QA", [96, NH, 512], BF16)
            selp = sb(sd, "selp", [128, NH, 96], BF16)
            gm = sb(sd, "gm", [128, NH, 32], F32)
            selw = sb(sd, "selw", [128, NH, 32], F32)
            top8 = sb(sd, "top8", [128, NH, 8], F32)
            kmT = sb(sd, "kmT", [64, NH, 32], BF16)
            kms = sb(sd, "kms", [64, 32], F32)
            PT = [sb(sd, "PT%d" % i, [128, 512], BF16) for i in range(3)]
            PTd = [sb(sd, "PTd%d" % i, [128, 256], BF16) for i in range(2)]
            ON = sb(sd, "ON", [64, 512], F32)
            rden = sb(sd, "rden", [65, 512], F32)
            AT = sb(sd, "AT", [64, NH, 512], BF16)
            P.op('pool', lambda: nc.gpsimd.memset(selp[:, :, :], 0.0), [], ['selp'])
            KTv = KTd.rearrange("(c h r) t -> c h r t", h=NH, r=128)
            psS = psM[0:3]
            psO = psM[3:5]
            psX = psM[5]

            nseg_all = T // SEGL if T >= SEGL else 1
            segl_all = min(T, SEGL)
            for h in range(NH):
                for sgi in range(nseg_all):
                    base = sgi * segl_all
                    s_ = (h * nseg_all + sgi) % 2
                    P.dma('sp', 'KA%d' % s_, lambda: nc.sync.dma_start(
                        out=KA[s_][0:64, 0:segl_all].rearrange("r (c t) -> r c t", t=512),
                        in_=KTv[base // 512:(base + segl_all) // 512, h, 0:64, :].rearrange("c r t -> r c t")),
                        [], ['KA%d' % s_])
                    nb = segl_all // 256
                    P.op('dve', lambda: nc.vector.tensor_reduce(
                        out=kms[:, 0:nb], in_=KA[s_][0:64, 0:segl_all].rearrange("r (n k) -> r n k", k=256),
                        axis=AX.X, op=ALU.add), ['KA%d' % s_], ['kms'])
                    P.op('dve', lambda: nc.vector.tensor_scalar(
                        out=kmT[:, h, base // 256:base // 256 + nb], in0=kms[:, 0:nb], scalar1=1.0 / 256.0,
                        scalar2=None, op0=ALU.mult), ['kms'], ['kmT'])
            if T < 32 * 256:
                P.op('dve', lambda: nc.vector.memset(kmT[:, :, T // 256:32], 0.0), [], ['kmT'])

            cnt = dict(s=0, seg=0, kd=0, o=0, pd=0)
            for g in range(G):
                L = 2048 * (g + 1)
                for j in range(4):
                    P.dma('pool', 'xtD', lambda: nc.gpsimd.indirect_dma_start(
                        out=xt[:, j, :], out_offset=None, in_=X2[:, :],
                        in_offset=bass.IndirectOffsetOnAxis(ap=qidx[:, g * 4 + j:g * 4 + j + 1], axis=0)),
                        ['qidx'], ['xtD'])
                    P.dma('pool', 'VD', lambda: nc.gpsimd.indirect_dma_start(
                        out=VD[:, j, :], out_offset=None, in_=Vd2[:, :],
                        in_offset=bass.IndirectOffsetOnAxis(ap=qidx[:, g * 4 + j:g * 4 + j + 1], axis=0)),
                        ['qidx'], ['VD'])
                norm_T(lambda j: xt[:, j, :], 'xtD', 4, 128, xn, xnT)
                for h in range(NH):
                    for kc in range(8):
                        P.op('pe', lambda: nc.tensor.matmul(psX[0:64, :], lhsT=wq[:, kc, h * 64:(h + 1) * 64],
                                                            rhs=xnT[:, kc, :], start=(kc == 0), stop=(kc == 7)),
                             ['wq', 'xnT'], ['psM5'])
                    if h % 2 == 0:
                        P.op('act', lambda: nc.scalar.copy(out=QA[0:64, h, :], in_=psX[0:64, :]), ['psM5'], ['QA'])
                    else:
                        P.op('dve', lambda: nc.vector.tensor_copy(out=QA[0:64, h, :], in_=psX[0:64, :]), ['psM5'], ['QA'])
                for j in range(4):
                    gj = g * 4 + j
                    for h in range(NH):
                        P.op('pe', lambda: nc.tensor.matmul(psX[:, h * 32:(h + 1) * 32],
                                                            lhsT=QA[0:64, h, j * 128:(j + 1) * 128],
                                                            rhs=kmT[0:64, h, :], start=True, stop=True),
                             ['QA', 'kmT'], ['psM5'])
                    P.op('dve', lambda: nc.vector.tensor_tensor(
                        out=gm[:, :, :], in0=psX[:, :].rearrange("p (h n) -> p h n", n=32),
                        in1=pastb[:, gj:gj + 1, :].broadcast_to([128, NH, 32]), op=ALU.add), ['psM5', 'pastb'], ['gm'])
                    for h in range(NH):
                        P.op('dve', lambda: nc.vector.max(out=top8[:, h, :], in_=gm[:, h, :]), ['gm'], ['top8'])
                    P.op('dve', lambda: nc.vector.tensor_tensor(
                        out=selw[:, :, :], in0=gm[:, :, :], in1=top8[:, :, 2:3].broadcast_to([128, NH, 32]),
                        op=ALU.is_ge), ['gm', 'top8'], ['selw'])
                    P.op('dve', lambda: nc.vector.tensor_tensor(
                        out=selw[:, :, :], in0=selw[:, :, :], in1=past01[:, gj:gj + 1, :].broadcast_to([128, NH, 32]),
                        op=ALU.mult), ['selw', 'past01'], ['selw'])
                    P.op('dve', lambda: nc.vector.tensor_scalar(
                        out=selp[:, :, 64:96], in0=selw[:, :, :], scalar1=-NEGB, scalar2=NEGB, op0=ALU.mult,
                        op1=ALU.add), ['selw'], ['selp'])
                    for h in range(NH):
                        k4 = h % 4
                        if k4 == 0:
                            oi = (h // 4) % 2
                            ps4, pn4 = psO[oi], 'psM%d' % (3 + oi)
                        P.op('pe', lambda: nc.tensor.matmul(ps4[0:96, k4 * 128:(k4 + 1) * 128],
                                                            lhsT=selp[:, h, :], rhs=ident[:, :], start=True, stop=True),
                             ['selp', 'ident'], [pn4])
                        if k4 == 3:
                            h0 = h - 3
                            P.op('act', lambda: nc.scalar.copy(
                                out=QA[64:96, h0:h0 + 4, j * 128:(j + 1) * 128],
                                in_=ps4[64:96, 0:512].rearrange("p (a q) -> p a q", q=128)), [pn4], ['QA'])
                segs = [(b0, min(SEGL, L - b0)) for b0 in range(0, L, SEGL)]
                LOOK = 2
                jobs = []
                for h in range(NH):
                    for sgi, (base, sl) in enumerate(segs):
                        for kt in range(sl // 128):
                            jobs.append(dict(t='s', h=h, base=base, sl=sl, kt=kt, segfirst=(kt == 0),
                                             first=(sgi == 0 and kt == 0)))
                    for tt in range(4):
                        jobs.append(dict(t='d', h=h, tt=tt, dfirst=(tt == 0), last=(tt == 3)))
                hstate = {}

                def emit_qk(jb):
                    h = jb['h']
                    si = cnt['s'] % 3
                    cnt['s'] += 1
                    jb['si'] = si
                    if jb['t'] == 's':
                        if jb['segfirst']:
                            s_ = cnt['seg'] % 2
                            cnt['seg'] += 1
                            base, sl = jb['base'], jb['sl']
                            nkt = sl // 128
                            P.dma('sp', 'KA%d' % s_, lambda: nc.sync.dma_start(
                                out=KA[s_][0:64, 0:sl].rearrange("r (c t) -> r c t", t=512),
                                in_=KTv[base // 512:(base + sl) // 512, h, 0:64, :].rearrange("c r t -> r c t")),
                                [], ['KA%d' % s_])
                            P.dma('sp', 'KAi%d' % s_, lambda: nc.sync.dma_start(
                                out=KA[s_][64:96, 0:sl], in_=ind_d[:, base:base + sl]), [], ['KAi%d' % s_])
                            P.dma('sp', 'VA%d' % s_, lambda: nc.sync.dma_start(
                                out=VA[s_][:, 0:nkt, :],
                                in_=Vd[h, base:base + sl, :].rearrange("(p k) e -> p k e", k=nkt)), [], ['VA%d' % s_])
                            hstate['seg'] = s_
                        s_ = hstate['seg']
                        jb['s_'] = s_
                        sl = jb['sl']
                        nkt = sl // 128
                        kav = KA[s_][0:96, 0:sl].rearrange("r (p k) -> r k p", k=nkt)
                        P.op('pe', lambda: nc.tensor.matmul(psS[si][:, :], lhsT=kav[:, jb['kt'], :], rhs=QA[0:96, h, :],
                                                            start=True, stop=True),
                             ['KA%d' % s_, 'KAi%d' % s_, 'QA'], ['psM%d' % si])
                        P.op('act', lambda: nc.scalar.activation(out=PT[si][:, :], in_=psS[si][:, :], func=AF.Exp,
                                                                  scale=0.125), ['psM%d' % si], ['PT%d' % si])
                    else:
                        if jb['dfirst']:
                            kd_i = cnt['kd'] % 2
                            cnt['kd'] += 1
                            P.dma('pool', 'KD%d' % kd_i, lambda: nc.gpsimd.indirect_dma_start(
                                out=KD[kd_i][:, :], out_offset=None, in_=KTd[:, :],
                                in_offset=bass.IndirectOffsetOnAxis(ap=ktidx[:, g * 16 + h:g * 16 + h + 1], axis=0)),
                                ['ktidx'], ['KD%d' % kd_i])
                            hstate['kd'] = kd_i
                        kd_i = hstate['kd']
                        tt = jb['tt']
                        X, i2 = tt // 2, tt % 2
                        P.op('pe', lambda: nc.tensor.matmul(psS[si][:, 0:256],
                                                            lhsT=KD[kd_i][0:64, tt * 128:(tt + 1) * 128],
                                                            rhs=QA[0:64, h, X * 256:(X + 1) * 256],
                                                            start=True, stop=True),
                             ['KD%d' % kd_i, 'QA'], ['psM%d' % si])
                        P.op('act', lambda: nc.scalar.activation(out=PT[si][:, 0:256], in_=psS[si][:, 0:256],
                                                                  func=AF.Exp, scale=0.125),
                             ['psM%d' % si], ['PT%d' % si])
                        P.op('pool', lambda: nc.gpsimd.tensor_tensor(out=PT[si][:, 0:256], in0=PT[si][:, 0:256],
                                                                     in1=cmask[:, i2, :], op=ALU.mult),
                             ['PT%d' % si, 'cmask'], ['PT%d' % si])

                def emit_pv(jb):
                    h = jb['h']
                    si = jb['si']
                    if jb['t'] == 's':
                        if jb['first']:
                            o_i = cnt['o'] % 2
                            cnt['o'] += 1
                            hstate['o'] = o_i
                        o_i = hstate['o']
                        po, pon = psO[o_i], 'psM%d' % (3 + o_i)
                        s_ = jb['s_']
                        P.op('pe', lambda: nc.tensor.matmul(po[0:65, :], lhsT=VA[s_][:, jb['kt'], :], rhs=PT[si][:, :],
                                                            start=jb['first'], stop=False),
                             ['VA%d' % s_, 'PT%d' % si], [pon])
                    else:
                        o_i = hstate['o']
                        po, pon = psO[o_i], 'psM%d' % (3 + o_i)
                        tt = jb['tt']
                        X = tt // 2
                        P.op('pe', lambda: nc.tensor.matmul(po[0:65, X * 256:(X + 1) * 256],
                                                            lhsT=VD[:, tt, h * 65:(h + 1) * 65], rhs=PT[si][:, 0:256],
                                                            start=False, stop=jb['last']),
                             ['VD', 'PT%d' % si], [pon])
                        if jb['last']:
                            P.op('dve', lambda: nc.vector.reciprocal(out=rden[64:65, :], in_=po[64:65, :]),
                                 [pon], ['rden'])
                            P.op('pe', lambda: nc.tensor.matmul(psX[0:64, :], lhsT=onesf[64:65, 0:64],
                                                                rhs=rden[64:65, :], start=True, stop=True),
                                 ['onesf', 'rden'], ['psM5'])
                            P.op('act', lambda: nc.scalar.copy(out=ON[:, :], in_=po[0:64, :]), [pon], ['ON'])
                            P.op('dve', lambda: nc.vector.tensor_tensor(out=AT[:, h, :], in0=psX[0:64, :], in1=ON[:, :],
                                                                        op=ALU.mult), ['psM5', 'ON'], ['AT'])

                for idx in range(len(jobs) + LOOK):
                    if idx < len(jobs):
                        emit_qk(jobs[idx])
                    if idx >= LOOK:
                        emit_pv(jobs[idx - LOOK])
                for j in range(4):
                    for hf in range(2):
                        for h in range(NH):
                            P.op('pe', lambda: nc.tensor.matmul(psX[:, :], lhsT=AT[0:64, h, j * 128:(j + 1) * 128],
                                                                rhs=woH[0:64, h, hf * 512:(hf + 1) * 512],
                                                                start=(h == 0), stop=(h == NH - 1)),
                                 ['AT', 'woH'], ['psM5'])
                        P.op('dve', lambda: nc.vector.tensor_tensor(out=xt[:, j, hf * 512:(hf + 1) * 512],
                                                                    in0=psX[:, :], in1=xt[:, j, hf * 512:(hf + 1) * 512],
                                                                    op=ALU.add), ['psM5', 'xtD'], ['xtD'])
                P.dma('sp', 'xtDs', lambda: nc.sync.dma_start(
                    out=X3[g * 512:(g + 1) * 512, :].rearrange("(j p) f -> p j f", p=128), in_=xt[:, :, :]),
                    ['xtD'], [])
            P.barrier()

        phase_mlp(1, [(X3, y_out, 512 * G, 256), (XS3, ys_out, 128, 128)], 4, final_idx=5)

        P.barrier()
    return nc


_NC_CACHE = {}


def _bf16(a):
    return np.asarray(a, dtype=np.float32).astype(ml_dtypes.bfloat16)


def _fm(v):
    return np.ascontiguousarray(np.asarray(v, np.float32).reshape(8, 128).T)


def kernel(x_prompt, x_sample, state_conv, state_h, cache_k, cache_v, page_table,
           norm_mix, norm_mlp, w_ff1, w_ff2, w_rg_in, b_rg_in, conv_w, conv_b,
           w_gate_a, b_gate_a, w_gate_i, b_gate_i, lru_lambda, w_rg_out, b_rg_out,
           norm_kv, w_kv, w_q, w_o, norm_out):
    f32 = np.float32
    x_prompt = np.asarray(x_prompt, f32)
    B, T, _ = x_prompt.shape
    G = T // 2048
    nphys = int(np.asarray(cache_k).shape[0])
    if (G, nphys) not in _NC_CACHE:
        _NC_CACHE[(G, nphys)] = build(G, nphys)
    nc = _NC_CACHE[(G, nphys)]

    vecs = np.zeros((128, NV), f32)
    b_in = np.asarray(b_rg_in, f32)[0]
    vecs[:, 0:8] = _fm(b_in[:D])
    vecs[:, 8:16] = _fm(b_in[D:])
    for k in range(4):
        vecs[:, 16 + 8 * k:24 + 8 * k] = _fm(np.asarray(conv_w, f32)[0, k])
    vecs[:, 48:56] = _fm(np.asarray(conv_b, f32)[0])
    vecs[:, 56:64] = _fm(np.asarray(b_gate_a, f32)[0].reshape(-1))
    vecs[:, 64:72] = _fm(np.asarray(b_gate_i, f32)[0].reshape(-1))
    vecs[:, 72:80] = _fm(np.asarray(lru_lambda, f32)[0])
    rows = np.stack([np.asarray(norm_mix, f32)[0], np.asarray(norm_mlp, f32)[0], np.asarray(norm_kv, f32),
                     np.asarray(norm_mix, f32)[1], np.asarray(norm_mlp, f32)[1], np.asarray(norm_out, f32),
                     np.asarray(b_rg_out, f32)[0]])
    rows = np.concatenate([rows, b_in[None, D:]], axis=0)
    rows = np.ascontiguousarray(np.broadcast_to(rows[:, None, :], (8, 128, D)))
    ind = _bf16((np.arange(T)[None, :] // 256) == np.arange(32)[:, None])
    kk = np.arange(128)[:, None, None]
    ii = np.arange(2)[None, :, None]
    qq = np.arange(256)[None, None, :]
    cmask = _bf16(qq >= ii * 128 + kk)
    ident = _bf16(np.eye(128))
    common = dict(
        w_ff1=np.asarray(w_ff1, f32), w_ff2=np.asarray(w_ff2, f32), w_rg_in=np.asarray(w_rg_in, f32)[0],
        w_ga=np.asarray(w_gate_a, f32)[0], w_gi=np.asarray(w_gate_i, f32)[0],
        w_rg_out=np.asarray(w_rg_out, f32)[0], w_kv=np.asarray(w_kv, f32), w_q=np.asarray(w_q, f32)[0],
        w_o=np.asarray(w_o, f32)[0], vecs=vecs, rows=rows, ind=ind, cmask=cmask, ident=ident)
    common.update(
        xs=np.ascontiguousarray(np.asarray(x_sample, f32)[:, 0, :]),
        sconv=np.ascontiguousarray(np.asarray(state_conv, f32)[0]),
        sh=np.ascontiguousarray(np.asarray(state_h, f32)[0]),
        iop=np.arange(128, dtype=f32).reshape(128, 1),
        ck=np.asarray(cache_k, f32).reshape(-1, D), cv=np.asarray(cache_v, f32).reshape(-1, D))
    ptab_all = np.asarray(page_table).astype(np.int32)
    in_maps = []
    chunks = []
    p = np.arange(128)
    for c in range(8):
        s, j = c // 4, c % 4
        cgs = [4 * g + (j if g % 2 == 0 else 3 - j) for g in range(G)]
        chunks.append(cgs)
        qidx = np.zeros((128, 4 * G), np.int32)
        ktidx = np.zeros((128, 16 * G), np.int32)
        for g, cg in enumerate(cgs):
            for jj in range(4):
                qidx[:, g * 4 + jj] = cg * 512 + jj * 128 + p
            for h in range(NH):
                ktidx[:, g * 16 + h] = (cg * NH + h) * 128 + p
        qblk = qidx // 256
        past01 = (np.arange(32)[None, None, :] < qblk[:, :, None]).astype(f32)
        pastb = ((past01 - 1.0) * 1e30).astype(f32)
        m = dict(common)
        ownidx = np.zeros((128, 1), np.int32)
        ownidx[:NS, 0] = NS * c + np.arange(NS)
        ownbc = np.ascontiguousarray(np.broadcast_to((NS * c + np.arange(NS, dtype=np.int32))[None, :], (128, NS)))
        m.update(xp=np.ascontiguousarray(x_prompt[s]), qidx=qidx, ktidx=ktidx, past01=past01, pastb=pastb,
                 ptab=np.ascontiguousarray(ptab_all[NS * c:NS * (c + 1)]), ownidx=ownidx, ownbc=ownbc)
        in_maps.append(m)
    res = run_bass_kernel_spmd(nc, in_maps, core_ids=list(range(8))).results

    y_prompt = np.zeros((B, T, D), f32)
    for c in range(8):
        s = c // 4
        for g, cg in enumerate(chunks[c]):
            y_prompt[s, cg * 512:(cg + 1) * 512] = res[c]["y"][g * 512:(g + 1) * 512]
    k_p = np.stack([res[0]["k"], res[4]["k"]]).reshape(B, T, NH, HD)
    v_p = np.stack([res[0]["v"], res[4]["v"]]).reshape(B, T, NH, HD)
    conv_p = np.stack([res[0]["convp"], res[4]["convp"]])[None]
    h_p = np.stack([res[0]["hp"], res[4]["hp"]])[None]
    y_s = np.concatenate([res[c]["ys"][:NS] for c in range(8)], axis=0)[:, None, :]
    conv_s = res[0]["convs"][None]
    h_s = res[0]["hs"][None]
    k_s = res[0]["ks"].reshape(128, 1, NH, HD)
    v_s = res[0]["vs"].reshape(128, 1, NH, HD)
    return (y_prompt, y_s, conv_p, h_p, k_p, v_p, conv_s, h_s, k_s, v_s)
```

```python
import contextlib
import numpy as np
import ml_dtypes
import concourse.bass as bass
import concourse.mybir as mybir
from concourse.bass_utils import run_bass_kernel_spmd

F32 = mybir.dt.float32
BF16 = mybir.dt.bfloat16
I32 = mybir.dt.int32
AF = mybir.ActivationFunctionType
ALU = mybir.AluOpType
AX = mybir.AxisListType

D = 1024
DFF = 4096
NH = 16
HD = 64
EPS = 1e-6
NS = 16
NPG = 16
NEGB = -30000.0
NV = 80


class Prog:
    def __init__(self, nc):
        self.nc = nc
        self.E = dict(pe=nc.tensor, act=nc.scalar, dve=nc.vector, pool=nc.gpsimd, sp=nc.sync)
        self.sem = {}
        self.cnt = {}
        for e in self.E:
            self.sem[e] = nc.alloc_semaphore("s_" + e)
            self.cnt[e] = 0
        self.known = {e: {} for e in self.E}
        self.lastw = {}
        self.readers = {}
        self.dk = {}
        self.nps = 0

    def _wait(self, eng, deps):
        best = {}
        for t in deps:
            if t is None:
                continue
            k, v = t
            if eng == 'pe' and k == 'pe':
                continue
            if v > best.get(k, 0):
                best[k] = v
        for k, v in best.items():
            if self.known[eng].get(k, 0) >= v:
                continue
            sem = self.sem[k] if k in self.sem else self.dk[k][0]
            self.E[eng].wait_ge(sem, v)
            self.known[eng][k] = v

    def _deps(self, reads, writes):
        deps = []
        for b in reads:
            deps.append(self.lastw.get(b))
        for b in writes:
            deps.append(self.lastw.get(b))
            deps.extend(self.readers.get(b, {}).items())
        return deps

    def _record(self, tok, reads, writes):
        k, v = tok
        for b in reads:
            self.readers.setdefault(b, {})[k] = v
        for b in writes:
            self.lastw[b] = tok
            self.readers[b] = {}

    def op(self, eng, fn, reads=(), writes=()):
        self._wait(eng, self._deps(reads, writes))
        ins = fn()
        self.cnt[eng] += 1
        ins.then_inc(self.sem[eng], 1)
        self._record((eng, self.cnt[eng]), reads, writes)

    def dma(self, q, key, fn, reads=(), writes=()):
        self._wait(q, self._deps(reads, writes))
        ins = fn()
        if key not in self.dk:
            self.dk[key] = [self.nc.alloc_semaphore("d_" + key), 0]
        self.dk[key][1] += 16
        ins.then_inc(self.dk[key][0], 16)
        self._record((key, self.dk[key][1]), reads, writes)

    def barrier(self):
        for e in self.E:
            deps = [(k, self.cnt[k]) for k in self.E if self.cnt[k] > 0]
            deps += [(k, v[1]) for k, v in self.dk.items()]
            self._wait(e, deps)
        self.lastw = {}
        self.readers = {}


def build(G, NPHYS=2560):
    T = 2048 * G
    NCH = T // 512
    nc = bass.Bass("TRN2", target_bir_lowering=False)
    P = Prog(nc)

    def din(name, shape, dt=F32):
        return nc.dram_tensor(name, list(shape), dt, kind="ExternalInput").ap()

    def dout(name, shape, dt=F32):
        return nc.dram_tensor(name, list(shape), dt, kind="ExternalOutput").ap()

    def dscr(name, shape, dt=F32):
        return nc.dram_tensor(name, list(shape), dt, kind="Internal").ap()

    xp = din("xp", [T, D])
    w_ff1 = din("w_ff1", [2, D, DFF])
    w_ff2 = din("w_ff2", [2, DFF, D])
    w_rg_in = din("w_rg_in", [D, 2 * D])
    w_ga = din("w_ga", [4, 256, 256])
    w_gi = din("w_gi", [4, 256, 256])
    w_rg_out = din("w_rg_out", [D, D])
    w_kv = din("w_kv", [D, 2 * D])
    w_q = din("w_q", [D, D])
    w_o = din("w_o", [D, D])
    vecs_d = din("vecs", [128, NV])
    rows_d = din("rows", [8, 128, D])
    qidx_d = din("qidx", [128, 4 * G], I32)
    ktidx_d = din("ktidx", [128, 16 * G], I32)
    pastb_d = din("pastb", [128, 4 * G, 32])
    past01_d = din("past01", [128, 4 * G, 32])
    ind_d = din("ind", [32, T], BF16)
    cmask_d = din("cmask", [128, 2, 256], BF16)
    ident_d = din("ident", [128, 128], BF16)
    xs_d = din("xs", [128, D])
    sconv_d = din("sconv", [128, 3, D])
    sh_d = din("sh", [128, D])
    ck_d = din("ck", [NPHYS * 128, D])
    cv_d = din("cv", [NPHYS * 128, D])
    ptab_d = din("ptab", [NS, NPG], I32)
    ownidx_d = din("ownidx", [128, 1], I32)
    ownbc_d = din("ownbc", [128, NS], I32)
    iop_d = din("iop", [128, 1])
    y_out = dout("y", [512 * G, D])
    k_out = dout("k", [T, D])
    v_out = dout("v", [T, D])
    conv_out = dout("convp", [3, D])
    h_out = dout("hp", [D])
    ys_out = dout("ys", [128, D])
    convs_out = dout("convs", [128, 3, D])
    hs_out = dout("hs", [128, D])
    ks_out = dout("ks", [128, D])
    vs_out = dout("vs", [128, D])
    XS1 = dscr("XS1", [128, D])
    XS2 = dscr("XS2", [128, D])
    XS3 = dscr("XS3", [128, D])
    KVS = dscr("KVS", [128, 2 * D])
    Qd = dscr("Qd", [128, D])
    Dn = dscr("Dn", [NS, 16])
    X1 = dscr("X1", [T, D])
    X2 = dscr("X2", [T, D])
    X3 = dscr("X3", [512 * G, D])
    KTd = dscr("KTd", [NCH * NH * 128, 512], BF16)
    Vd = dscr("Vd", [NH, T, 65], BF16)
    Vd2 = dscr("Vd2", [T, NH * 65], BF16)

    es = contextlib.ExitStack()

    uniq = [0]

    def sb(stack, name, shape, dt):
        uniq[0] += 1
        return stack.enter_context(nc.sbuf_tensor("sb%d_%s" % (uniq[0], name), list(shape), dt))

    def pst(stack, name, shape, dt):
        return stack.enter_context(nc.psum_tensor(name, list(shape), dt))

    with es:
        ident = sb(es, "ident", [128, 128], BF16)
        vecs = sb(es, "vecs", [128, NV], F32)
        grow = sb(es, "grow", [128, D], F32)
        ssq = sb(es, "ssq", [128, 4], F32)
        msq = sb(es, "msq", [128, 4], F32)
        rstd = sb(es, "rstd", [128, 4], F32)
        junk = sb(es, "junk", [128, D], BF16)
        P.dma('sp', 'c_ident', lambda: nc.sync.dma_start(out=ident[:, :], in_=ident_d[:, :]), [], ['ident'])
        P.dma('sp', 'c_vecs', lambda: nc.sync.dma_start(out=vecs[:, :], in_=vecs_d[:, :]), [], ['vecs'])

        psT = [pst(es, "psT%d" % i, [128, 1024], BF16) for i in range(2)]
        psM = [pst(es, "psM%d" % i, [128, 512], F32) for i in range(6)]
        st = dict(t=0, m=0)

        def next_psT():
            st['t'] += 1
            i = st['t'] % 2
            return psT[i], "psT%d" % i

        def next_psM(nm=4, base=0):
            st['m'] += 1
            i = base + st['m'] % nm
            return psM[i], "psM%d" % i

        def load_grow(idx):
            P.dma('sp', 'c_grow', lambda: nc.sync.dma_start(out=grow[:, :], in_=rows_d[idx, :, :]), [], ['grow'])

        def norm_T(xsub, xname, nsub, nt, xn, xnT, do_T=True, gname='grow', gt=None, xnn='xn', xnTn='xnT'):
            g_t = grow if gt is None else gt
            for j in range(nsub):
                P.op('act', lambda: nc.scalar.activation(out=junk[:nt, :], in_=xsub(j), func=AF.Square,
                                                          accum_out=ssq[:nt, j:j + 1]), [xname], ['junk', 'ssq'])
            P.op('dve', lambda: nc.vector.tensor_scalar(out=msq[:nt, :nsub], in0=ssq[:nt, :nsub], scalar1=1.0 / D,
                                                        scalar2=EPS, op0=ALU.mult, op1=ALU.add), ['ssq'], ['msq'])
            P.op('act', lambda: nc.scalar.activation(out=msq[:nt, :nsub], in_=msq[:nt, :nsub], func=AF.Sqrt),
                 ['msq'], ['msq'])
            P.op('dve', lambda: nc.vector.reciprocal(out=rstd[:nt, :nsub], in_=msq[:nt, :nsub]), ['msq'], ['rstd'])
            for j in range(nsub):
                P.op('dve', lambda: nc.vector.scalar_tensor_tensor(out=xn[:nt, j, :], in0=xsub(j),
                                                                    scalar=rstd[:nt, j:j + 1], in1=g_t[:nt, :],
                                                                    op0=ALU.mult, op1=ALU.mult),
                     [xname, 'rstd', gname], [xnn])
            if not do_T:
                return
            for c in range(8):
                ps, pn = next_psT()
                for j in range(nsub):
                    P.op('pe', lambda: nc.tensor.transpose(out=ps[:, j * nt:(j + 1) * nt],
                                                           in_=xn[:nt, j, c * 128:(c + 1) * 128],
                                                           identity=ident[:nt, :nt]), [xnn, 'ident'], [pn])
                if c % 2 == 0:
                    P.op('act', lambda: nc.scalar.copy(out=xnT[:, c, :nsub * nt], in_=ps[:, :nsub * nt]), [pn], [xnTn])
                else:
                    P.op('dve', lambda: nc.vector.tensor_copy(out=xnT[:, c, :nsub * nt], in_=ps[:, :nsub * nt]),
                         [pn], [xnTn])

        def load_w(stack_w, name, src, shape_kc, ncols, splits=1):
            t = sb(stack_w, name, [128, shape_kc, ncols], BF16)
            w = ncols // splits
            for s_ in range(splits):
                P.dma('pool', 'w_' + name, lambda: nc.gpsimd.dma_start(
                    out=t[:, :, s_ * w:(s_ + 1) * w],
                    in_=src[:, s_ * w:(s_ + 1) * w].rearrange("(kc p) f -> p kc f", p=128)), [], [name])
            return t

        TA = 256
        with contextlib.ExitStack() as sa:
            w_in_sb = load_w(sa, "w_in", w_rg_in, 8, 2048)
            wga = sb(sa, "wga", [128, 4, 2, 256], BF16)
            wgi = sb(sa, "wgi", [128, 4, 2, 256], BF16)
            P.dma('pool', 'w_wga', lambda: nc.gpsimd.dma_start(
                out=wga[:, :, :, :], in_=w_ga.rearrange("n (kc p) f -> p n kc f", p=128)), [], ['wga'])
            P.dma('pool', 'w_wgi', lambda: nc.gpsimd.dma_start(
                out=wgi[:, :, :, :], in_=w_gi.rearrange("n (kc p) f -> p n kc f", p=128)), [], ['wgi'])
            wout_sb = load_w(sa, "w_out", w_rg_out, 8, 1024)
            brow = sb(sa, "brow", [128, D], F32)
            P.dma('sp', 'c_brow', lambda: nc.sync.dma_start(out=brow[:, :], in_=rows_d[6, :, :]), [], ['brow'])
            load_grow(0)
            sc = sb(sa, "sc", [128, 8], F32)
            sc2 = sb(sa, "sc2", [128, 8], F32)
            P.op('act', lambda: nc.scalar.activation(out=sc[:, :], in_=vecs[:, 72:80], func=AF.Exp, scale=-1.0),
                 ['vecs'], ['sc'])
            P.op('act', lambda: nc.scalar.activation(out=sc[:, :], in_=sc[:, :], func=AF.Ln, bias=1.0), ['sc'], ['sc'])
            P.op('dve', lambda: nc.vector.tensor_scalar(out=sc2[:, :], in0=sc[:, :], scalar1=-16.0, scalar2=None,
                                                        op0=ALU.mult), ['sc'], ['sc2'])
            P.op('dve', lambda: nc.vector.tensor_scalar(out=sc[:, :], in0=sc[:, :], scalar1=-8.0, scalar2=None,
                                                        op0=ALU.mult), ['sc'], ['sc'])
            xt3 = [sb(sa, "xtA%d" % i, [128, 2, D], F32) for i in range(3)]
            xn2 = [sb(sa, "xnA%d" % i, [128, 2, D], BF16) for i in range(2)]
            xnT2 = [sb(sa, "xnTA%d" % i, [128, 8, TA], BF16) for i in range(2)]
            uT2 = [sb(sa, "uT%d" % i, [128, 8, TA + 3], F32) for i in range(2)]
            GX2 = [sb(sa, "GX%d" % i, [128, 8, TA], F32) for i in range(2)]
            Z = sb(sa, "Z", [128, 8, TA], F32)
            C = sb(sa, "C", [128, 8, TA], F32)
            Cbf = sb(sa, "Cbf", [128, 8, TA], BF16)
            R = sb(sa, "R", [128, 8, TA], F32)
            I_ = sb(sa, "I", [128, 8, TA], F32)
            E_ = sb(sa, "E", [128, 8, TA], F32)
            H = sb(sa, "H", [128, 8, TA], F32)
            HG = sb(sa, "HG", [128, 8, TA], BF16)
            hst = sb(sa, "hst", [128, 8], F32)
            UTM = sb(sa, "UTM", [128, D], F32)
            urow = sb(sa, "urow", [128, D], F32)
            id32 = sb(sa, "id32", [128, 128], F32)
            P.dma('sp', 'c_urow', lambda: nc.sync.dma_start(out=urow[:, :], in_=rows_d[7, :, :]), [], ['urow'])
            P.op('dve', lambda: nc.vector.tensor_copy(out=id32[:, :], in_=ident[:, :]), ['ident'], ['id32'])
            P.op('dve', lambda: nc.vector.memset(uT2[0][:, :, 0:3], 0.0), [], ['uTh0'])
            P.op('dve', lambda: nc.vector.memset(hst[:, :], 0.0), [], ['hst'])

            jobs = [('p', i) for i in range(T // TA)] + [('s', 0)]

            def run_merged(gens):
                pend = []
                for gi, g in enumerate(gens):
                    try:
                        pend.append([next(g), gi, g])
                    except StopIteration:
                        pass
                while pend:
                    pend.sort(key=lambda x_: (x_[0], x_[1]))
                    it = pend[0]
                    try:
                        it[0] = next(it[2])
                    except StopIteration:
                        pend.pop(0)

            def jvars(k):
                kind, i = jobs[k]
                smp = (kind == 's')
                NCl = 128 if smp else TA
                nsub = 1 if smp else 2
                t0 = i * TA
                sl2 = k % 2
                xs = xt3[k % 3]
                xsn = "xtA%d" % (k % 3)
                return kind, i, smp, NCl, nsub, t0, sl2, xs, xsn

            def loadA(k):
                kind, i, smp, NCl, nsub, t0, sl2, xs, xsn = jvars(k)
                if smp:
                    P.dma('sp', xsn, lambda: nc.sync.dma_start(out=xs[:, 0, :], in_=xs_d[:, :]), [], [xsn])
                    for k3 in range(2):
                        P.dma('sp', 'o_cs', lambda: nc.sync.dma_start(out=convs_out[:, k3, :],
                                                                       in_=sconv_d[:, k3 + 1, :]), [], [])
                else:
                    P.dma('sp', xsn, lambda: nc.sync.dma_start(
                        out=xs[:, :, :], in_=xp[t0:t0 + TA, :].rearrange("(j p) f -> p j f", p=128)), [], [xsn])

            def front(k):
                kind, i, smp, NCl, nsub, t0, sl2, xs, xsn = jvars(k)
                xn, xnT, uT, GX = xn2[sl2], xnT2[sl2], uT2[sl2], GX2[sl2]
                XN, XNT, UT, GXn = 'xn%d' % sl2, 'xnT%d' % sl2, 'uT%d' % sl2, 'GX%d' % sl2
                yield 0.0
                norm_T(lambda j: xs[:, j, :], xsn, nsub, 128, xn, None, do_T=False, xnn=XN)
                for c in range(8):
                    yield 4.0 + 0.5 * c
                    ps, pn = next_psT()
                    for j in range(nsub):
                        P.op('pe', lambda: nc.tensor.transpose(out=ps[:, j * 128:(j + 1) * 128],
                                                               in_=xn[:128, j, c * 128:(c + 1) * 128],
                                                               identity=ident[:128, :128]), [XN, 'ident'], [pn])
                    if c % 2 == 0:
                        P.op('act', lambda: nc.scalar.copy(out=xnT[:, c, :nsub * 128], in_=ps[:, :nsub * 128]),
                             [pn], [XNT])
                    else:
                        P.op('dve', lambda: nc.vector.tensor_copy(out=xnT[:, c, :nsub * 128], in_=ps[:, :nsub * 128]),
                             [pn], [XNT])
                for oc in range(16):
                    yield 8.0 + 2.1 * oc
                    ps, pn = next_psM()
                    for kc in range(8):
                        P.op('pe', lambda: nc.tensor.matmul(ps[:, :NCl], lhsT=w_in_sb[:, kc, oc * 128:(oc + 1) * 128],
                                                            rhs=xnT[:, kc, :NCl], start=(kc == 0), stop=(kc == 7)),
                             ['w_in', XNT], [pn])
                    if oc < 8:
                        P.op('act', lambda: nc.scalar.activation(out=GX[:, oc, :NCl], in_=ps[:, :NCl], func=AF.Identity,
                                                                  bias=vecs[:, oc:oc + 1]), [pn, 'vecs'], [GXn])
                    else:
                        c = oc - 8
                        P.op('dve', lambda: nc.vector.tensor_scalar(out=uT[:, c, 3:3 + NCl], in0=ps[:, :NCl],
                                                                    scalar1=vecs[:, 8 + c:9 + c], scalar2=None,
                                                                    op0=ALU.add), [pn, 'vecs'], [UT])
                if smp:
                    for hf in range(2):
                        yield 43.0 + hf
                        ps, pn = next_psM()
                        for kc in range(8):
                            P.op('pe', lambda: nc.tensor.matmul(ps[:, :], lhsT=xnT[:, kc, 0:128],
                                                                rhs=w_in_sb[:, kc, D + hf * 512:D + (hf + 1) * 512],
                                                                start=(kc == 0), stop=(kc == 7)), ['w_in', XNT], [pn])
                        P.op('dve', lambda: nc.vector.tensor_tensor(out=UTM[:, hf * 512:(hf + 1) * 512], in0=ps[:, :],
                                                                    in1=urow[:, hf * 512:(hf + 1) * 512], op=ALU.add),
                             [pn, 'urow'], ['UTM'])
                    P.dma('sp', 'o_cs2', lambda: nc.sync.dma_start(out=convs_out[:, 2, :], in_=UTM[:, :]), ['UTM'], [])

            def gelu(k):
                kind, i, smp, NCl, nsub, t0, sl2, xs, xsn = jvars(k)
                GX = GX2[sl2]
                GXn = 'GX%d' % sl2
                yield 25.0
                P.op('pool', lambda: nc.gpsimd.tensor_tensor(out=Z[:, :, :NCl], in0=GX[:, :, :NCl],
                                                             in1=GX[:, :, :NCl], op=ALU.mult), [GXn], ['Z'])
                yield 29.0
                P.op('dve', lambda: nc.vector.tensor_scalar(out=Z[:, :, :NCl], in0=Z[:, :, :NCl],
                                                            scalar1=0.044715, scalar2=1.0, op0=ALU.mult,
                                                            op1=ALU.add), ['Z'], ['Z'])
                yield 31.0
                P.op('pool', lambda: nc.gpsimd.tensor_tensor(out=Z[:, :, :NCl], in0=Z[:, :, :NCl],
                                                             in1=GX[:, :, :NCl], op=ALU.mult), ['Z', GXn], ['Z'])
                yield 35.0
                P.op('act', lambda: nc.scalar.activation(out=Z[:, :, :NCl], in_=Z[:, :, :NCl], func=AF.Sigmoid,
                                                          scale=1.5957691216057308), ['Z'], ['Z'])
                yield 36.5
                P.op('pool', lambda: nc.gpsimd.tensor_tensor(out=GX[:, :, :NCl], in0=GX[:, :, :NCl],
                                                             in1=Z[:, :, :NCl], op=ALU.mult), ['Z', GXn], [GXn])

            def back(k):
                kind, i, smp, NCl, nsub, t0, sl2, xs, xsn = jvars(k)
                xn, xnT, uT, GX = xn2[sl2], xnT2[sl2], uT2[sl2], GX2[sl2]
                XN, XNT, UT, GXn = 'xn%d' % sl2, 'xnT%d' % sl2, 'uT%d' % sl2, 'GX%d' % sl2
                UTH = 'uTh%d' % sl2
                uTn = uT2[1 - sl2]
                UTHn = 'uTh%d' % (1 - sl2)
                yield 0.5
                if smp:
                    for k3 in range(4):
                        if k3 < 3:
                            P.dma('sp', xsn, lambda: nc.sync.dma_start(out=xs[:, 1, :], in_=sconv_d[:, k3, :]),
                                  [], [xsn])
                        else:
                            P.dma('sp', xsn, lambda: nc.sync.dma_start(out=xs[:, 1, :], in_=sh_d[:, :]), [], [xsn])
                        for c in range(8):
                            ps, pn = next_psM()
                            P.op('pe', lambda: nc.tensor.transpose(out=ps[:, 0:128], in_=xs[:, 1, c * 128:(c + 1) * 128],
                                                                   identity=id32[:, :]), [xsn, 'id32'], [pn])
                            dstt, dn, o_ = ((Z, 'Z', 0), (Z, 'Z', 128), (E_, 'E', 0), (E_, 'E', 128))[k3]
                            P.op('act', lambda: nc.scalar.copy(out=dstt[:, c, o_:o_ + 128], in_=ps[:, 0:128]),
                                 [pn], [dn])
                sdeps = ['Z', 'E'] if smp else []
                for c in range(8):
                    if c > 0:
                        yield 0.5 + 1.4 * c
                    if smp:
                        taps = [Z[:, c, 0:128], Z[:, c, 128:256], E_[:, c, 0:128], uT[:, c, 3:3 + 128]]
                    else:
                        taps = [uT[:, c, kk:kk + TA] for kk in range(4)]
                    P.op('dve', lambda: nc.vector.tensor_scalar(out=C[:, c, :NCl], in0=taps[0],
                                                                scalar1=vecs[:, 16 + c:17 + c],
                                                                scalar2=vecs[:, 48 + c:49 + c],
                                                                op0=ALU.mult, op1=ALU.add), [UT, UTH, 'vecs'] + sdeps, ['C'])
                    for kk in range(1, 4):
                        P.op('dve', lambda: nc.vector.scalar_tensor_tensor(
                            out=C[:, c, :NCl], in0=taps[kk], scalar=vecs[:, 16 + 8 * kk + c:17 + 8 * kk + c],
                            in1=C[:, c, :NCl], op0=ALU.mult, op1=ALU.add), [UT, UTH, 'vecs', 'C'] + sdeps, ['C'])
                yield 11.5
                P.op('act', lambda: nc.scalar.copy(out=Cbf[:, :, :NCl], in_=C[:, :, :NCl]), ['C'], ['Cbf'])
                if kind == 'p' and i == T // TA - 1:
                    with nc.allow_non_contiguous_dma(reason="tiny state output"):
                        for k3 in range(3):
                            P.dma('sp', 'o_conv', lambda: nc.sync.dma_start(
                                out=conv_out[k3].rearrange("(c p) -> p c", p=128), in_=uT[:, :, TA + k3]),
                                [UT], [])
                if not smp:
                    P.op('pool', lambda: nc.gpsimd.tensor_copy(out=uTn[:, :, 0:3], in_=uT[:, :, TA:TA + 3]),
                         [UT], [UTHn])
                for c in range(8):
                    yield 13.0 + 1.0 * c
                    n, dc = c // 2, c % 2
                    for (wg, dst, dn, bo) in ((wga, R, 'R', 56), (wgi, I_, 'I', 64)):
                        ps, pn = next_psM()
                        for kc in range(2):
                            P.op('pe', lambda: nc.tensor.matmul(ps[:, :NCl], lhsT=wg[:, n, kc, dc * 128:(dc + 1) * 128],
                                                                rhs=Cbf[:, 2 * n + kc, :NCl], start=(kc == 0),
                                                                stop=(kc == 1)), ['wga', 'wgi', 'Cbf'], [pn])
                        P.op('act', lambda: nc.scalar.activation(out=dst[:, c, :NCl], in_=ps[:, :NCl], func=AF.Sigmoid,
                                                                  bias=vecs[:, bo + c:bo + c + 1]), [pn, 'vecs'], [dn])
                E2 = H if smp else E_
                E2n = 'H' if smp else 'E'
                for c in range(8):
                    yield 21.0 + 0.4 * c
                    P.op('act', lambda: nc.scalar.activation(out=E2[:, c, :NCl], in_=R[:, c, :NCl], func=AF.Exp,
                                                              scale=sc2[:, c:c + 1]), ['R', 'sc2'], [E2n])
                for c in range(8):
                    yield 24.2 + 0.4 * c
                    P.op('act', lambda: nc.scalar.activation(out=R[:, c, :NCl], in_=R[:, c, :NCl], func=AF.Exp,
                                                              scale=sc[:, c:c + 1]), ['R', 'sc'], ['R'])
                yield 27.5
                P.op('act', lambda: nc.scalar.activation(out=E2[:, :, :NCl], in_=E2[:, :, :NCl], func=AF.Sqrt,
                                                          scale=-1.0, bias=1.0), [E2n], [E2n])
                yield 28.0
                P.op('pool', lambda: nc.gpsimd.tensor_tensor(out=I_[:, :, :NCl], in0=I_[:, :, :NCl], in1=E2[:, :, :NCl],
                                                             op=ALU.mult), ['I', E2n], ['I'])
                yield 32.0
                P.op('pool', lambda: nc.gpsimd.tensor_tensor(out=I_[:, :, :NCl], in0=I_[:, :, :NCl], in1=C[:, :, :NCl],
                                                             op=ALU.mult), ['I', 'C'], ['I'])
                if smp:
                    yield 36.0
                    P.op('dve', lambda: nc.vector.tensor_tensor(out=H[:, :, :128], in0=R[:, :, :128],
                                                                in1=E_[:, :, 128:256], op=ALU.mult),
                         ['R', 'E', 'I'], ['H'])
                    P.op('dve', lambda: nc.vector.tensor_tensor(out=H[:, :, :128], in0=H[:, :, :128],
                                                                in1=I_[:, :, :128], op=ALU.add), ['H', 'I'], ['H'])
                    for c in range(8):
                        ps, pn = next_psM()
                        P.op('pe', lambda: nc.tensor.transpose(out=ps[:, 0:128], in_=H[:, c, 0:128],
                                                               identity=id32[:, :]), ['H', 'id32'], [pn])
                        P.op('act', lambda: nc.scalar.copy(out=UTM[:, c * 128:(c + 1) * 128], in_=ps[:, 0:128]),
                             [pn], ['UTM'])
                    P.dma('sp', 'o_hs', lambda: nc.sync.dma_start(out=hs_out[:, :], in_=UTM[:, :]), ['UTM'], [])
                else:
                    for c in range(8):
                        yield 36.0 + 0.6 * c
                        P.op('dve', lambda: nc.vector.tensor_tensor_scan(out=H[:, c, :], data0=R[:, c, :],
                                                                         data1=I_[:, c, :], initial=hst[:, c:c + 1],
                                                                         op0=ALU.mult, op1=ALU.add),
                             ['R', 'I', 'hst'], ['H'])
                    yield 40.8
                    P.op('dve', lambda: nc.vector.tensor_copy(out=hst[:, :], in_=H[:, :, TA - 1]), ['H'], ['hst'])
                yield 41.0
                P.op('pool', lambda: nc.gpsimd.tensor_tensor(out=HG[:, :, :NCl], in0=H[:, :, :NCl], in1=GX[:, :, :NCl],
                                                             op=ALU.mult), ['H', GXn], ['HG'])
                for j in range(nsub):
                    for hf in range(2):
                        yield 44.0 + 1.5 * (2 * j + hf)
                        ps, pn = next_psM()
                        for kc in range(8):
                            P.op('pe', lambda: nc.tensor.matmul(ps[:, :], lhsT=HG[:, kc, j * 128:(j + 1) * 128],
                                                                rhs=wout_sb[:, kc, hf * 512:(hf + 1) * 512],
                                                                start=(kc == 0), stop=(kc == 7)), ['HG', 'w_out'], [pn])
                        P.op('dve', lambda: nc.vector.tensor_tensor(out=xs[:, j, hf * 512:(hf + 1) * 512],
                                                                    in0=ps[:, :], in1=xs[:, j, hf * 512:(hf + 1) * 512],
                                                                    op=ALU.add), [pn, xsn], [xsn])
                    P.op('pool', lambda: nc.gpsimd.tensor_tensor(out=xs[:, j, :], in0=xs[:, j, :], in1=brow[:, :],
                                                                 op=ALU.add), [xsn, 'brow'], [xsn])
                yield 50.0
                if smp:
                    P.dma('sp', xsn + 's', lambda: nc.sync.dma_start(out=XS1[:, :], in_=xs[:, 0, :]), [xsn], [])
                else:
                    P.dma('sp', xsn + 's', lambda: nc.sync.dma_start(
                        out=X1[t0:t0 + TA, :].rearrange("(j p) f -> p j f", p=128), in_=xs[:, :, :]), [xsn], [])
                    if i == T // TA - 1:
                        with nc.allow_non_contiguous_dma(reason="tiny state output"):
                            P.dma('sp', 'o_h', lambda: nc.sync.dma_start(out=h_out.rearrange("(c p) -> p c", p=128),
                                                                         in_=hst[:, :]), ['hst'], [])

            loadA(0)
            loadA(1)
            run_merged([front(0), gelu(0)])
            for k in range(len(jobs)):
                if k + 2 < len(jobs):
                    loadA(k + 2)
                gens = [back(k)]
                if k + 1 < len(jobs):
                    gens.append(front(k + 1))
                    gens.append(gelu(k + 1))
                run_merged(gens)
            P.barrier()

        def phase_mlp(layer, jobs, grow_idx, final_idx=None):
            TBm = 256
            with contextlib.ExitStack() as sm:
                ff1 = load_w(sm, "ff1", w_ff1[layer], 8, DFF, splits=2)
                ff2 = load_w(sm, "ff2", w_ff2[layer], 32, D)
                load_grow(grow_idx)
                if final_idx is not None:
                    grow2 = sb(sm, "grow2", [128, D], F32)
                    P.dma('sp', 'c_grow2', lambda: nc.sync.dma_start(out=grow2[:, :], in_=rows_d[final_idx, :, :]),
                          [], ['grow2'])
                    yt = sb(sm, "yt", [128, 2, D], F32)
                xt = [sb(sm, "xtB%d" % i, [128, 2, D], F32) for i in range(2)]
                xn = sb(sm, "xnB", [128, 2, D], BF16)
                xnT = sb(sm, "xnTB", [128, 8, TBm], BF16)
                RL = [sb(sm, "RL%d" % i, [128, TBm], F32) for i in range(2)]
                HT = sb(sm, "HT", [128, 32, TBm], BF16)
                ti = 0
                for (src, dst, ntok, TB) in jobs:
                    nsub = TB // 128
                    for i in range(ntok // TB):
                        t0 = i * TB
                        xs = xt[ti % 2]
                        xsn = "xtB%d" % (ti % 2)
                        ti += 1
                        P.dma('sp', xsn, lambda: nc.sync.dma_start(
                            out=xs[:, 0:nsub, :], in_=src[t0:t0 + TB, :].rearrange("(j p) f -> p j f", p=128)),
                            [], [xsn])
                        norm_T(lambda j: xs[:, j, :], xsn, nsub, 128, xn, xnT)
                        for oc in range(32):
                            ps, pn = next_psM()
                            for kc in range(8):
                                P.op('pe', lambda: nc.tensor.matmul(ps[:, :TB], lhsT=ff1[:, kc, oc * 128:(oc + 1) * 128],
                                                                    rhs=xnT[:, kc, :TB], start=(kc == 0),
                                                                    stop=(kc == 7)), ['ff1', 'xnT'], [pn])
                            rl = RL[oc % 2]
                            rln = "RL%d" % (oc % 2)
                            P.op('act', lambda: nc.scalar.activation(out=rl[:, :TB], in_=ps[:, :TB], func=AF.Relu),
                                 [pn], [rln])
                            P.op('pool', lambda: nc.gpsimd.tensor_tensor(out=HT[:, oc, :TB], in0=rl[:, :TB],
                                                                         in1=rl[:, :TB], op=ALU.mult), [rln], ['HT'])
                        for j in range(nsub):
                            for hf in range(2):
                                ps, pn = next_psM()
                                for kc in range(32):
                                    P.op('pe', lambda: nc.tensor.matmul(ps[:, :], lhsT=HT[:, kc, j * 128:(j + 1) * 128],
                                                                        rhs=ff2[:, kc, hf * 512:(hf + 1) * 512],
                                                                        start=(kc == 0), stop=(kc == 31)),
                                         ['HT', 'ff2'], [pn])
                                P.op('dve', lambda: nc.vector.tensor_tensor(out=xs[:, j, hf * 512:(hf + 1) * 512],
                                                                            in0=ps[:, :],
                                                                            in1=xs[:, j, hf * 512:(hf + 1) * 512],
                                                                            op=ALU.add), [pn, xsn], [xsn])
                        if final_idx is None:
                            P.dma('sp', xsn + 's', lambda: nc.sync.dma_start(
                                out=dst[t0:t0 + TB, :].rearrange("(j p) f -> p j f", p=128), in_=xs[:, 0:nsub, :]),
                                [xsn], [])
                        else:
                            norm_T(lambda j: xs[:, j, :], xsn, nsub, 128, yt, None, do_T=False, gname='grow2', gt=grow2)
                            P.dma('sp', 'yts', lambda: nc.sync.dma_start(
                                out=dst[t0:t0 + TB, :].rearrange("(j p) f -> p j f", p=128), in_=yt[:, 0:nsub, :]),
                                ['xn'], [])
                P.barrier()

        phase_mlp(0, [(X1, X2, T, 256), (XS1, XS2, 128, 128)], 1)

        TC = 256
        with contextlib.ExitStack() as sc_:
            wkv = load_w(sc_, "wkv", w_kv, 8, 2048)
            load_grow(2)
            xt = [sb(sc_, "xtC%d" % i, [128, 2, D], F32) for i in range(2)]
            xnC2 = [sb(sc_, "xnC%d" % i, [128, 2, D], BF16) for i in range(2)]
            xnTC2 = [sb(sc_, "xnTC%d" % i, [128, 8, TC], BF16) for i in range(2)]
            KTs2 = [sb(sc_, "KTs%d" % i, [128, 8, TC], BF16) for i in range(2)]
            KVo2 = [sb(sc_, "KVo%d" % i, [128, 2, 2 * D], F32) for i in range(2)]
            VAs2 = [sb(sc_, "VAs%d" % i, [128, 2, NH, 65], BF16) for i in range(2)]
            for i in range(2):
                P.op('dve', lambda: nc.vector.memset(VAs2[i][:, :, :, :], 1.0), [], ['VAs%d' % i])
            KTv = KTd.rearrange("(c h r) t -> c h r t", h=NH, r=128)
            nC = T // TC

            def loadC(i):
                t0 = i * TC
                xs = xt[i % 2]
                xsn = "xtC%d" % (i % 2)
                P.dma('sp', xsn, lambda: nc.sync.dma_start(
                    out=xs[:, :, :], in_=X2[t0:t0 + TC, :].rearrange("(j p) f -> p j f", p=128)), [], [xsn])

            def frontC(i):
                sl = i % 2
                xs = xt[sl]
                xsn = "xtC%d" % sl
                xn, xnT = xnC2[sl], xnTC2[sl]
                XN, XNT = 'xnC%d' % sl, 'xnTC%d' % sl
                yield 0.0
                norm_T(lambda j: xs[:, j, :], xsn, 2, 128, xn, None, do_T=False, xnn=XN)
                for c in range(8):
                    yield 5.0 + 0.5 * c
                    ps, pn = next_psT()
                    for j in range(2):
                        P.op('pe', lambda: nc.tensor.transpose(out=ps[:, j * 128:(j + 1) * 128],
                                                               in_=xn[:128, j, c * 128:(c + 1) * 128],
                                                               identity=ident[:128, :128]), [XN, 'ident'], [pn])
                    if c % 2 == 0:
                        P.op('act', lambda: nc.scalar.copy(out=xnT[:, c, :], in_=ps[:, :TC]), [pn], [XNT])
                    else:
                        P.op('dve', lambda: nc.vector.tensor_copy(out=xnT[:, c, :], in_=ps[:, :TC]), [pn], [XNT])

            def backC(i):
                sl = i % 2
                t0 = i * TC
                xnT, XNT = xnTC2[sl], 'xnTC%d' % sl
                KTs, KTn = KTs2[sl], 'KTs%d' % sl
                KVo, KVn = KVo2[sl], 'KVo%d' % sl
                VAs, VAn = VAs2[sl], 'VAs%d' % sl
                for oc in range(8):
                    yield 0.2 + 1.3 * oc
                    ps, pn = next_psM()
                    for kc in range(8):
                        P.op('pe', lambda: nc.tensor.matmul(ps[:, :TC], lhsT=wkv[:, kc, oc * 128:(oc + 1) * 128],
                                                            rhs=xnT[:, kc, :], start=(kc == 0), stop=(kc == 7)),
                             ['wkv', XNT], [pn])
                    P.op('act', lambda: nc.scalar.copy(out=KTs[:, oc, :], in_=ps[:, :TC]), [pn], [KTn])
                yield 10.7
                ch, off = t0 // 512, t0 % 512
                for hh in range(2):
                    P.dma('sp', 'KTs%d' % hh, lambda: nc.sync.dma_start(
                        out=KTv[ch, hh::2, 0:64, off:off + TC].rearrange("h r t -> r h t"),
                        in_=KTs[hh * 64:(hh + 1) * 64, :, :]), [KTn], [])
                for j in range(2):
                    for q4 in range(4):
                        yield 11.0 + 2.2 * (4 * j + q4)
                        ps, pn = next_psM()
                        for kc in range(8):
                            P.op('pe', lambda: nc.tensor.matmul(ps[:, :], lhsT=xnT[:, kc, j * 128:(j + 1) * 128],
                                                                rhs=wkv[:, kc, q4 * 512:(q4 + 1) * 512],
                                                                start=(kc == 0), stop=(kc == 7)), ['wkv', XNT], [pn])
                        if q4 % 2 == 0:
                            P.op('act', lambda: nc.scalar.copy(out=KVo[:, j, q4 * 512:(q4 + 1) * 512], in_=ps[:, :]),
                                 [pn], [KVn])
                        else:
                            P.op('dve', lambda: nc.vector.tensor_copy(out=KVo[:, j, q4 * 512:(q4 + 1) * 512],
                                                                      in_=ps[:, :]), [pn], [KVn])
                    P.op('pool', lambda: nc.gpsimd.tensor_copy(
                        out=VAs[:, j, :, 0:64], in_=KVo[:, j, D:2 * D].rearrange("p (h e) -> p h e", e=64)),
                        [KVn], [VAn])
                yield 30.0
                P.dma('sp', 'KVo_k', lambda: nc.sync.dma_start(
                    out=k_out[t0:t0 + TC, :].rearrange("(j p) f -> p j f", p=128), in_=KVo[:, :, 0:D]), [KVn], [])
                P.dma('sp', 'KVo_v', lambda: nc.sync.dma_start(
                    out=v_out[t0:t0 + TC, :].rearrange("(j p) f -> p j f", p=128), in_=KVo[:, :, D:2 * D]), [KVn], [])
                for j in range(2):
                    P.dma('sp', 'VAs_1', lambda: nc.sync.dma_start(
                        out=Vd[:, t0 + j * 128:t0 + (j + 1) * 128, :].rearrange("h p e -> p h e"),
                        in_=VAs[:, j, :, :]), [VAn], [])
                P.dma('sp', 'VAs_2', lambda: nc.sync.dma_start(
                    out=Vd2[t0:t0 + TC, :].rearrange("(j p) f -> p j f", p=128),
                    in_=VAs[:, :, :, :].rearrange("p j h e -> p j (h e)")), [VAn], [])

            loadC(0)
            if nC > 1:
                loadC(1)
            run_merged([frontC(0)])
            for i in range(nC):
                if i + 2 < nC:
                    loadC(i + 2)
                gens = [backC(i)]
                if i + 1 < nC:
                    gens.append(frontC(i + 1))
                run_merged(gens)
            xs = xt[0]
            xn, xnT, KVo = xnC2[0], xnTC2[0], KVo2[0]
            P.dma('sp', 'xtC0', lambda: nc.sync.dma_start(out=xs[:, 0, :], in_=XS2[:, :]), [], ['xtC0'])
            norm_T(lambda j: xs[:, j, :], 'xtC0', 1, 128, xn, xnT, xnn='xnC0', xnTn='xnTC0')
            for q4 in range(4):
                ps, pn = next_psM()
                for kc in range(8):
                    P.op('pe', lambda: nc.tensor.matmul(ps[:, :], lhsT=xnT[:, kc, 0:128],
                                                        rhs=wkv[:, kc, q4 * 512:(q4 + 1) * 512],
                                                        start=(kc == 0), stop=(kc == 7)), ['wkv', 'xnTC0'], [pn])
                P.op('act', lambda: nc.scalar.copy(out=KVo[:, 0, q4 * 512:(q4 + 1) * 512], in_=ps[:, :]),
                     [pn], ['KVo0'])
            P.dma('sp', 'KVo_k', lambda: nc.sync.dma_start(out=ks_out[:, :], in_=KVo[:, 0, 0:D]), ['KVo0'], [])
            P.dma('sp', 'KVo_v', lambda: nc.sync.dma_start(out=vs_out[:, :], in_=KVo[:, 0, D:2 * D]), ['KVo0'], [])
            P.dma('sp', 'KVo_s', lambda: nc.sync.dma_start(out=KVS[:, :], in_=KVo[:, 0, :]), ['KVo0'], [])
            P.barrier()


        PB = 4
        with contextlib.ExitStack() as ss:
            wqS = load_w(ss, "wqS", w_q, 8, 1024)
            woS = load_w(ss, "woS", w_o, 8, 1024)
            load_grow(3)
            xts = sb(ss, "xtS", [128, 1, D], F32)
            xns = sb(ss, "xnS", [128, 1, D], BF16)
            xnTs = sb(ss, "xnTS", [128, 8, 128], BF16)
            Qs = sb(ss, "Qs", [128, D], F32)
            id32 = sb(ss, "id32S", [128, 128], F32)
            ones32 = sb(ss, "ones32", [128, 128], F32)
            onesb = sb(ss, "onesb", [128, 128], BF16)
            ownidx = sb(ss, "ownidx", [128, 1], I32)
            ownbc = sb(ss, "ownbc", [128, NS], I32)
            iop = sb(ss, "iop", [128, 1], F32)
            ptb_i = sb(ss, "ptb_i", [128, NS * NPG], I32)
            ptb_f = sb(ss, "ptb_f", [128, NS * NPG], F32)
            pidx = sb(ss, "pidx", [128, NS * NPG], I32)
            Qo = sb(ss, "Qo", [NS, D], F32)
            KVo_ = sb(ss, "KVown", [NS, 2 * D], F32)
            x2o = sb(ss, "x2o", [128, 1, D], F32)
            Kp = [sb(ss, "Kp%d" % i, [128, PB, D], BF16) for i in range(2)]
            Vp = [sb(ss, "Vp%d" % i, [128, NPG, D], BF16) for i in range(2)]
            qb = [sb(ss, "qb%d" % i, [128, D], BF16) for i in range(2)]
            prod = sb(ss, "prod", [128, PB, D], F32)
            Sq = sb(ss, "Sq", [128, NPG, 16], F32)
            Eq = sb(ss, "Eq", [128, NPG, 16], BF16)
            gateS = sb(ss, "gateS", [128, 8, 16], F32)
            top8s = sb(ss, "top8s", [128, 16, 8], F32)
            selS = sb(ss, "selS", [128, 8, 16], F32)
            numT = sb(ss, "numT", [128, 8, NS], F32)
            denr = sb(ss, "denr", [128, 16], F32)
            num = sb(ss, "num", [NS, D], F32)
            dens = sb(ss, "dens", [NS, 16], F32)
            snew = sb(ss, "snew", [NS, 16], F32)
            pr0 = sb(ss, "pr0", [NS, D], F32)
            attnb = sb(ss, "attnb", [128, 1, D], BF16)
            attnT = sb(ss, "attnT", [128, 8, 128], BF16)
            P.op('dve', lambda: nc.vector.tensor_copy(out=id32[:, :], in_=ident[:, :]), ['ident'], ['id32S'])
            P.op('dve', lambda: nc.vector.memset(ones32[:, :], 1.0), [], ['ones32'])
            P.op('dve', lambda: nc.vector.memset(onesb[:, :], 1.0), [], ['onesb'])
            P.op('dve', lambda: nc.vector.memset(x2o[:, :, :], 0.0), [], ['x2o'])
            P.op('pool', lambda: nc.gpsimd.memset(attnb[:, :, :], 0.0), [], ['attnb'])
            P.dma('sp', 'c_own', lambda: nc.sync.dma_start(out=ownidx[:, :], in_=ownidx_d[:, :]), [], ['ownidx'])
            P.dma('sp', 'c_ownbc', lambda: nc.sync.dma_start(out=ownbc[:, :], in_=ownbc_d[:, :]), [], ['ownbc'])
            P.dma('sp', 'c_iop', lambda: nc.sync.dma_start(out=iop[:, :], in_=iop_d[:, :]), [], ['iop'])
            P.dma('sp', 'c_ptab', lambda: nc.sync.dma_start(
                out=ptb_i[:, :], in_=ptab_d.rearrange("b s -> (b s)").partition_broadcast(128)), [], ['ptb_i'])
            P.op('dve', lambda: nc.vector.tensor_copy(out=ptb_f[:, :], in_=ptb_i[:, :]), ['ptb_i'], ['ptb_f'])
            P.op('dve', lambda: nc.vector.tensor_scalar(out=ptb_f[:, :], in0=ptb_f[:, :], scalar1=128.0,
                                                        scalar2=iop[:, 0:1], op0=ALU.mult, op1=ALU.add),
                 ['ptb_f', 'iop'], ['ptb_f'])
            P.op('dve', lambda: nc.vector.tensor_copy(out=pidx[:, :], in_=ptb_f[:, :]), ['ptb_f'], ['pidx'])
            P.dma('sp', 'xtS', lambda: nc.sync.dma_start(out=xts[:, 0, :], in_=XS2[:, :]), [], ['xtS'])
            norm_T(lambda j: xts[:, j, :], 'xtS', 1, 128, xns, xnTs)
            for hf in range(2):
                ps, pn = next_psM()
                for kc in range(8):
                    P.op('pe', lambda: nc.tensor.matmul(ps[:, :], lhsT=xnTs[:, kc, :],
                                                        rhs=wqS[:, kc, hf * 512:(hf + 1) * 512],
                                                        start=(kc == 0), stop=(kc == 7)), ['wqS', 'xnT'], [pn])
                P.op('act', lambda: nc.scalar.copy(out=Qs[:, hf * 512:(hf + 1) * 512], in_=ps[:, :]), [pn], ['Qs'])
            P.dma('sp', 'Qd', lambda: nc.sync.dma_start(out=Qd[:, :], in_=Qs[:, :]), ['Qs'], ['Qd'])
            P.dma('pool', 'Qo', lambda: nc.gpsimd.indirect_dma_start(
                out=Qo[:, :], out_offset=None, in_=Qd[:, :],
                in_offset=bass.IndirectOffsetOnAxis(ap=ownidx[0:NS, 0:1], axis=0)), ['ownidx', 'Qd'], ['Qo'])
            P.dma('pool', 'KVown', lambda: nc.gpsimd.indirect_dma_start(
                out=KVo_[:, :], out_offset=None, in_=KVS[:, :],
                in_offset=bass.IndirectOffsetOnAxis(ap=ownidx[0:NS, 0:1], axis=0)), ['ownidx'], ['KVown'])
            P.dma('pool', 'x2o', lambda: nc.gpsimd.indirect_dma_start(
                out=x2o[0:NS, 0, :], out_offset=None, in_=XS2[:, :],
                in_offset=bass.IndirectOffsetOnAxis(ap=ownidx[0:NS, 0:1], axis=0)), ['ownidx'], ['x2o'])
            for i in range(NS):
                sl = i % 2
                P.dma('pool', 'qb%d' % sl, lambda: nc.gpsimd.indirect_dma_start(
                    out=qb[sl][:, :], out_offset=None, in_=Qd[:, :],
                    in_offset=bass.IndirectOffsetOnAxis(ap=ownbc[:, i:i + 1], axis=0)), ['ownbc', 'Qd'], ['qb%d' % sl])
                for s_ in range(NPG):
                    P.dma('pool', 'Vp%d' % sl, lambda: nc.gpsimd.indirect_dma_start(
                        out=Vp[sl][:, s_, :], out_offset=None, in_=cv_d[:, :],
                        in_offset=bass.IndirectOffsetOnAxis(ap=pidx[:, i * NPG + s_:i * NPG + s_ + 1], axis=0)),
                        ['pidx'], ['Vp%d' % sl])
                for bq in range(NPG // PB):
                    ksl = (i * (NPG // PB) + bq) % 2
                    for g_ in range(PB):
                        s_ = bq * PB + g_
                        P.dma('pool', 'Kp%d' % ksl, lambda: nc.gpsimd.indirect_dma_start(
                            out=Kp[ksl][:, g_, :], out_offset=None, in_=ck_d[:, :],
                            in_offset=bass.IndirectOffsetOnAxis(ap=pidx[:, i * NPG + s_:i * NPG + s_ + 1], axis=0)),
                            ['pidx'], ['Kp%d' % ksl])
                    P.op('dve', lambda: nc.vector.tensor_tensor(
                        out=prod[:, :, :], in0=Kp[ksl][:, :, :],
                        in1=qb[sl][:, :].unsqueeze(1).broadcast_to([128, PB, D]), op=ALU.mult),
                        ['Kp%d' % ksl, 'qb%d' % sl], ['prod'])
                    P.op('dve', lambda: nc.vector.tensor_reduce(
                        out=Sq[:, bq * PB:(bq + 1) * PB, :], in_=prod[:, :, :].rearrange("p g (h e) -> p g h e", e=64),
                        axis=AX.X, op=ALU.add), ['prod'], ['Sq'])
                ps, pn = next_psM()
                P.op('pe', lambda: nc.tensor.matmul(ps[:, 0:256], lhsT=ones32[:, :],
                                                    rhs=Sq[:, :, :].rearrange("p s h -> p (s h)"), start=True, stop=True),
                     ['ones32', 'Sq'], [pn])
                gv = ps[:, 0:256].rearrange("p (n t h) -> p n t h", t=2, h=16)
                P.op('dve', lambda: nc.vector.tensor_copy(out=gateS[:, :, :], in_=gv[:, :, 0, :]), [pn], ['gateS'])
                P.op('dve', lambda: nc.vector.tensor_tensor(out=gateS[:, :, :], in0=gv[:, :, 1, :], in1=gateS[:, :, :],
                                                            op=ALU.add), [pn, 'gateS'], ['gateS'])
                for h in range(NH):
                    P.op('dve', lambda: nc.vector.max(out=top8s[:, h, :], in_=gateS[:, :, h]), ['gateS'], ['top8s'])
                P.op('dve', lambda: nc.vector.tensor_tensor(
                    out=selS[:, :, :], in0=gateS[:, :, :],
                    in1=top8s[:, :, 2:3].rearrange("p h o -> p o h").broadcast_to([128, 8, 16]), op=ALU.is_ge),
                    ['gateS', 'top8s'], ['selS'])
                P.op('act', lambda: nc.scalar.activation(out=Sq[:, :, :], in_=Sq[:, :, :], func=AF.Exp, scale=0.125),
                     ['Sq'], ['Sq'])
                for t_ in range(2):
                    P.op('dve', lambda: nc.vector.tensor_tensor(
                        out=Eq[:, :, :].rearrange("p (n t) h -> p n t h", t=2)[:, :, t_, :],
                        in0=Sq[:, :, :].rearrange("p (n t) h -> p n t h", t=2)[:, :, t_, :], in1=selS[:, :, :],
                        op=ALU.mult), ['Sq', 'selS'], ['Eq'])
                ps, pn = next_psM()
                P.op('pe', lambda: nc.tensor.matmul(ps[:, 0:256], lhsT=onesb[:, :],
                                                    rhs=Eq[:, :, :].rearrange("p s h -> p (s h)"), start=True, stop=True),
                     ['onesb', 'Eq'], [pn])
                P.op('dve', lambda: nc.vector.tensor_reduce(out=denr[:, :],
                                                            in_=ps[:, 0:256].rearrange("p (s h) -> p h s", h=16),
                                                            axis=AX.X, op=ALU.add), [pn], ['denr'])
                P.dma('sp', 'Dn', lambda: nc.sync.dma_start(out=Dn[i:i + 1, :], in_=denr[0:1, :]), ['denr'], ['Dn'])
                ps, pn = next_psM()
                for c in range(8):
                    for s_ in range(NPG):
                        P.op('pe', lambda: nc.tensor.matmul(ps[:, c * 16:(c + 1) * 16],
                                                            lhsT=Vp[sl][:, s_, c * 128:(c + 1) * 128], rhs=Eq[:, s_, :],
                                                            start=(s_ == 0), stop=(s_ == NPG - 1)),
                             ['Vp%d' % sl, 'Eq'], [pn])
                P.op('act', lambda: nc.scalar.copy(out=numT[0:64, :, i], in_=ps[0:64, 0:127:18]), [pn], ['numT'])
                P.op('dve', lambda: nc.vector.tensor_copy(out=numT[64:128, :, i], in_=ps[64:128, 1:128:18]),
                     [pn], ['numT'])
            ps, pn = next_psM()
            ps2, pn2 = next_psM()
            for c in range(8):
                pp_ = ps if c < 4 else ps2
                P.op('pe', lambda: nc.tensor.transpose(out=pp_[0:NS, (c % 4) * 128:(c % 4 + 1) * 128],
                                                       in_=numT[:, c, :], identity=id32[:, :]),
                     ['numT', 'id32S'], [pn if c < 4 else pn2])
            P.op('act', lambda: nc.scalar.copy(out=num[:, 0:512], in_=ps[0:NS, :]), [pn], ['num'])
            P.op('dve', lambda: nc.vector.tensor_copy(out=num[:, 512:1024], in_=ps2[0:NS, :]), [pn2], ['num'])
            P.dma('sp', 'dens', lambda: nc.sync.dma_start(out=dens[:, :], in_=Dn[:, :]), ['Dn'], ['dens'])
            P.op('dve', lambda: nc.vector.tensor_tensor(out=pr0[:, :], in0=Qo[:, :], in1=KVo_[:, 0:D], op=ALU.mult),
                 ['Qo', 'KVown'], ['pr0'])
            P.op('dve', lambda: nc.vector.tensor_reduce(out=snew[:, :], in_=pr0[:, :].rearrange("p (h e) -> p h e", e=64),
                                                        axis=AX.X, op=ALU.add), ['pr0'], ['snew'])
            P.op('act', lambda: nc.scalar.activation(out=snew[:, :], in_=snew[:, :], func=AF.Exp, scale=0.125),
                 ['snew'], ['snew'])
            P.op('dve', lambda: nc.vector.tensor_tensor(
                out=pr0[:, :].rearrange("p (h e) -> p h e", e=64), in0=KVo_[:, D:2 * D].rearrange("p (h e) -> p h e", e=64),
                in1=snew[:, :].unsqueeze(2).broadcast_to([NS, 16, 64]), op=ALU.mult), ['KVown', 'snew'], ['pr0'])
            P.op('dve', lambda: nc.vector.tensor_tensor(out=num[:, :], in0=num[:, :], in1=pr0[:, :], op=ALU.add),
                 ['num', 'pr0'], ['num'])
            P.op('dve', lambda: nc.vector.tensor_tensor(out=dens[:, :], in0=dens[:, :], in1=snew[:, :], op=ALU.add),
                 ['dens', 'snew'], ['dens'])
            P.op('dve', lambda: nc.vector.reciprocal(out=dens[:, :], in_=dens[:, :]), ['dens'], ['dens'])
            P.op('dve', lambda: nc.vector.tensor_tensor(
                out=attnb[0:NS, 0, :].rearrange("p (h e) -> p h e", e=64), in0=num[:, :].rearrange("p (h e) -> p h e", e=64),
                in1=dens[:, :].unsqueeze(2).broadcast_to([NS, 16, 64]), op=ALU.mult), ['num', 'dens', 'attnb'], ['attnb'])
            for c in range(8):
                pt_, ptn = next_psT()
                P.op('pe', lambda: nc.tensor.transpose(out=pt_[:, 0:128], in_=attnb[:, 0, c * 128:(c + 1) * 128],
                                                       identity=ident[:, :]), ['attnb', 'ident'], [ptn])
                P.op('act', lambda: nc.scalar.copy(out=attnT[:, c, :], in_=pt_[:, 0:128]), [ptn], ['attnT'])
            for hf in range(2):
                ps, pn = next_psM()
                for c in range(8):
                    P.op('pe', lambda: nc.tensor.matmul(ps[:, :], lhsT=attnT[:, c, :], rhs=woS[:, c, hf * 512:(hf + 1) * 512],
                                                        start=(c == 0), stop=(c == 7)), ['attnT', 'woS'], [pn])
                P.op('dve', lambda: nc.vector.tensor_tensor(out=x2o[:, 0, hf * 512:(hf + 1) * 512], in0=ps[:, :],
                                                            in1=x2o[:, 0, hf * 512:(hf + 1) * 512], op=ALU.add),
                     [pn, 'x2o'], ['x2o'])
            P.dma('sp', 'x2os', lambda: nc.sync.dma_start(out=XS3[:, :], in_=x2o[:, 0, :]), ['x2o'], [])
            P.barrier()

        SEGL = 4096
        with contextlib.ExitStack() as sd:
            wq = load_w(sd, "wq", w_q, 8, 1024)
            woH = sb(sd, "woH", [64, NH, D], BF16)
            P.dma('pool', 'w_woH', lambda: nc.gpsimd.dma_start(
                out=woH[:, :, :], in_=w_o.rearrange("(h d) f -> d h f", d=64)), [], ['woH'])
            load_grow(3)
            qidx = sb(sd, "qidx", [128, 4 * G], I32)
            ktidx = sb(sd, "ktidx", [128, 16 * G], I32)
            pastb = sb(sd, "pastb", [128, 4 * G, 32], F32)
            past01 = sb(sd, "past01", [128, 4 * G, 32], F32)
            cmask = sb(sd, "cmask", [128, 2, 256], BF16)
            onesf = sb(sd, "onesf", [128, 64], F32)
            P.dma('sp', 'c_qidx', lambda: nc.sync.dma_start(out=qidx[:, :], in_=qidx_d[:, :]), [], ['qidx'])
            P.dma('sp', 'c_ktidx', lambda: nc.sync.dma_start(out=ktidx[:, :], in_=ktidx_d[:, :]), [], ['ktidx'])
            P.dma('sp', 'c_pastb', lambda: nc.sync.dma_start(out=pastb[:, :, :], in_=pastb_d[:, :, :]), [], ['pastb'])
            P.dma('sp', 'c_past01', lambda: nc.sync.dma_start(out=past01[:, :, :], in_=past01_d[:, :, :]), [], ['past01'])
            P.dma('sp', 'c_cmask', lambda: nc.sync.dma_start(out=cmask[:, :, :], in_=cmask_d[:, :, :]), [], ['cmask'])
            P.op('dve', lambda: nc.vector.memset(onesf[:, :], 1.0), [], ['onesf'])
            KA = [sb(sd, "KA%d" % i, [96, SEGL], BF16) for i in range(2)]
            VA = [sb(sd, "VA%d" % i, [128, SEGL // 128, 65], BF16) for i in range(2)]
            KD = [sb(sd, "KD%d" % i, [128, 512], BF16) for i in range(2)]
            VD = sb(sd, "VD", [128, 4, NH * 65], BF16)
            xt = sb(sd, "xtD", [128, 4, D], F32)
            xn = sb(sd, "xnD", [128, 4, D], BF16)
            xnT = sb(sd, "xnTD", [128, 8, 512], BF16)
            QA = sb(sd, "QA", [96, NH, 512], BF16)
            selp = sb(sd, "selp", [128, NH, 96], BF16)
            gm = sb(sd, "gm", [128, NH, 32], F32)
            selw = sb(sd, "selw", [128, NH, 32], F32)
            top8 = sb(sd, "top8", [128, NH, 8], F32)
            kmT = sb(sd, "kmT", [64, NH, 32], BF16)
            kms = sb(sd, "kms", [64, 32], F32)
            PT = [sb(sd, "PT%d" % i, [128, 512], BF16) for i in range(3)]
            PTd = [sb(sd, "PTd%d" % i, [128, 256], BF16) for i in range(2)]
            ON = sb(sd, "ON", [64, 512], F32)
            rden = sb(sd, "rden", [65, 512], F32)
            AT = sb(sd, "AT", [64, NH, 512], BF16)
            P.op('pool', lambda: nc.gpsimd.memset(selp[:, :, :], 0.0), [], ['selp'])
            KTv = KTd.rearrange("(c h r) t -> c h r t", h=NH, r=128)
            psS = psM[0:3]
            psO = psM[3:5]
            psX = psM[5]

            nseg_all = T // SEGL if T >= SEGL else 1
            segl_all = min(T, SEGL)
            for h in range(NH):
                for sgi in range(nseg_all):
                    base = sgi * segl_all
                    s_ = (h * nseg_all + sgi) % 2
                    P.dma('sp', 'KA%d' % s_, lambda: nc.sync.dma_start(
                        out=KA[s_][0:64, 0:segl_all].rearrange("r (c t) -> r c t", t=512),
                        in_=KTv[base // 512:(base + segl_all) // 512, h, 0:64, :].rearrange("c r t -> r c t")),
                        [], ['KA%d' % s_])
                    nb = segl_all // 256
                    P.op('dve', lambda: nc.vector.tensor_reduce(
                        out=kms[:, 0:nb], in_=KA[s_][0:64, 0:segl_all].rearrange("r (n k) -> r n k", k=256),
                        axis=AX.X, op=ALU.add), ['KA%d' % s_], ['kms'])
                    P.op('dve', lambda: nc.vector.tensor_scalar(
                        out=kmT[:, h, base // 256:base // 256 + nb], in0=kms[:, 0:nb], scalar1=1.0 / 256.0,
                        scalar2=None, op0=ALU.mult), ['kms'], ['kmT'])
            if T < 32 * 256:
                P.op('dve', lambda: nc.vector.memset(kmT[:, :, T // 256:32], 0.0), [], ['kmT'])

            cnt = dict(s=0, seg=0, kd=0, o=0, pd=0)
            for g in range(G):
                L = 2048 * (g + 1)
                for j in range(4):
                    P.dma('pool', 'xtD', lambda: nc.gpsimd.indirect_dma_start(
                        out=xt[:, j, :], out_offset=None, in_=X2[:, :],
                        in_offset=bass.IndirectOffsetOnAxis(ap=qidx[:, g * 4 + j:g * 4 + j + 1], axis=0)),
                        ['qidx'], ['xtD'])
                    P.dma('pool', 'VD', lambda: nc.gpsimd.indirect_dma_start(
                        out=VD[:, j, :], out_offset=None, in_=Vd2[:, :],
                        in_offset=bass.IndirectOffsetOnAxis(ap=qidx[:, g * 4 + j:g * 4 + j + 1], axis=0)),
                        ['qidx'], ['VD'])
                norm_T(lambda j: xt[:, j, :], 'xtD', 4, 128, xn, xnT)
                for h in range(NH):
                    for kc in range(8):
                        P.op('pe', lambda: nc.tensor.matmul(psX[0:64, :], lhsT=wq[:, kc, h * 64:(h + 1) * 64],
                                                            rhs=xnT[:, kc, :], start=(kc == 0), stop=(kc == 7)),
                             ['wq', 'xnT'], ['psM5'])
                    if h % 2 == 0:
                        P.op('act', lambda: nc.scalar.copy(out=QA[0:64, h, :], in_=psX[0:64, :]), ['psM5'], ['QA'])
                    else:
                        P.op('dve', lambda: nc.vector.tensor_copy(out=QA[0:64, h, :], in_=psX[0:64, :]), ['psM5'], ['QA'])
                for j in range(4):
                    gj = g * 4 + j
                    for h in range(NH):
                        P.op('pe', lambda: nc.tensor.matmul(psX[:, h * 32:(h + 1) * 32],
                                                            lhsT=QA[0:64, h, j * 128:(j + 1) * 128],
                                                            rhs=kmT[0:64, h, :], start=True, stop=True),
                             ['QA', 'kmT'], ['psM5'])
                    P.op('dve', lambda: nc.vector.tensor_tensor(
                        out=gm[:, :, :], in0=psX[:, :].rearrange("p (h n) -> p h n", n=32),
                        in1=pastb[:, gj:gj + 1, :].broadcast_to([128, NH, 32]), op=ALU.add), ['psM5', 'pastb'], ['gm'])
                    for h in range(NH):
                        P.op('dve', lambda: nc.vector.max(out=top8[:, h, :], in_=gm[:, h, :]), ['gm'], ['top8'])
                    P.op('dve', lambda: nc.vector.tensor_tensor(
                        out=selw[:, :, :], in0=gm[:, :, :], in1=top8[:, :, 2:3].broadcast_to([128, NH, 32]),
                        op=ALU.is_ge), ['gm', 'top8'], ['selw'])
                    P.op('dve', lambda: nc.vector.tensor_tensor(
                        out=selw[:, :, :], in0=selw[:, :, :], in1=past01[:, gj:gj + 1, :].broadcast_to([128, NH, 32]),
                        op=ALU.mult), ['selw', 'past01'], ['selw'])
                    P.op('dve', lambda: nc.vector.tensor_scalar(
                        out=selp[:, :, 64:96], in0=selw[:, :, :], scalar1=-NEGB, scalar2=NEGB, op0=ALU.mult,
                        op1=ALU.add), ['selw'], ['selp'])
                    for h in range(NH):
                        k4 = h % 4
                        if k4 == 0:
                            oi = (h // 4) % 2
                            ps4, pn4 = psO[oi], 'psM%d' % (3 + oi)
                        P.op('pe', lambda: nc.tensor.matmul(ps4[0:96, k4 * 128:(k4 + 1) * 128],
                                                            lhsT=selp[:, h, :], rhs=ident[:, :], start=True, stop=True),
                             ['selp', 'ident'], [pn4])
                        if k4 == 3:
                            h0 = h - 3
                            P.op('act', lambda: nc.scalar.copy(
                                out=QA[64:96, h0:h0 + 4, j * 128:(j + 1) * 128],
                                in_=ps4[64:96, 0:512].rearrange("p (a q) -> p a q", q=128)), [pn4], ['QA'])
                segs = [(b0, min(SEGL, L - b0)) for b0 in range(0, L, SEGL)]
                LOOK = 2
                jobs = []
                for h in range(NH):
                    for sgi, (base, sl) in enumerate(segs):
                        for kt in range(sl // 128):
                            jobs.append(dict(t='s', h=h, base=base, sl=sl, kt=kt, segfirst=(kt == 0),
                                             first=(sgi == 0 and kt == 0)))
                    for tt in range(4):
                        jobs.append(dict(t='d', h=h, tt=tt, dfirst=(tt == 0), last=(tt == 3)))
                hstate = {}

                def emit_qk(jb):
                    h = jb['h']
                    si = cnt['s'] % 3
                    cnt['s'] += 1
                    jb['si'] = si
                    if jb['t'] == 's':
                        if jb['segfirst']:
                            s_ = cnt['seg'] % 2
                            cnt['seg'] += 1
                            base, sl = jb['base'], jb['sl']
                            nkt = sl // 128
                            P.dma('sp', 'KA%d' % s_, lambda: nc.sync.dma_start(
                                out=KA[s_][0:64, 0:sl].rearrange("r (c t) -> r c t", t=512),
                                in_=KTv[base // 512:(base + sl) // 512, h, 0:64, :].rearrange("c r t -> r c t")),
                                [], ['KA%d' % s_])
                            P.dma('sp', 'KAi%d' % s_, lambda: nc.sync.dma_start(
                                out=KA[s_][64:96, 0:sl], in_=ind_d[:, base:base + sl]), [], ['KAi%d' % s_])
                            P.dma('sp', 'VA%d' % s_, lambda: nc.sync.dma_start(
                                out=VA[s_][:, 0:nkt, :],
                                in_=Vd[h, base:base + sl, :].rearrange("(p k) e -> p k e", k=nkt)), [], ['VA%d' % s_])
                            hstate['seg'] = s_
                        s_ = hstate['seg']
                        jb['s_'] = s_
                        sl = jb['sl']
                        nkt = sl // 128
                        kav = KA[s_][0:96, 0:sl].rearrange("r (p k) -> r k p", k=nkt)
                        P.op('pe', lambda: nc.tensor.matmul(psS[si][:, :], lhsT=kav[:, jb['kt'], :], rhs=QA[0:96, h, :],
                                                            start=True, stop=True),
                             ['KA%d' % s_, 'KAi%d' % s_, 'QA'], ['psM%d' % si])
                        P.op('act', lambda: nc.scalar.activation(out=PT[si][:, :], in_=psS[si][:, :], func=AF.Exp,
                                                                  scale=0.125), ['psM%d' % si], ['PT%d' % si])
                    else:
                        if jb['dfirst']:
                            kd_i = cnt['kd'] % 2
                            cnt['kd'] += 1
                            P.dma('pool', 'KD%d' % kd_i, lambda: nc.gpsimd.indirect_dma_start(
                                out=KD[kd_i][:, :], out_offset=None, in_=KTd[:, :],
                                in_offset=bass.IndirectOffsetOnAxis(ap=ktidx[:, g * 16 + h:g * 16 + h + 1], axis=0)),
                                ['ktidx'], ['KD%d' % kd_i])
                            hstate['kd'] = kd_i
                        kd_i = hstate['kd']
                        tt = jb['tt']
                        X, i2 = tt // 2, tt % 2
                        P.op('pe', lambda: nc.tensor.matmul(psS[si][:, 0:256],
                                                            lhsT=KD[kd_i][0:64, tt * 128:(tt + 1) * 128],
                                                            rhs=QA[0:64, h, X * 256:(X + 1) * 256],
                                                            start=True, stop=True),
                             ['KD%d' % kd_i, 'QA'], ['psM%d' % si])
                        P.op('act', lambda: nc.scalar.activation(out=PT[si][:, 0:256], in_=psS[si][:, 0:256],
                                                                  func=AF.Exp, scale=0.125),
                             ['psM%d' % si], ['PT%d' % si])
                        P.op('pool', lambda: nc.gpsimd.tensor_tensor(out=PT[si][:, 0:256], in0=PT[si][:, 0:256],
                                                                     in1=cmask[:, i2, :], op=ALU.mult),
                             ['PT%d' % si, 'cmask'], ['PT%d' % si])

                def emit_pv(jb):
                    h = jb['h']
                    si = jb['si']
                    if jb['t'] == 's':
                        if jb['first']:
                            o_i = cnt['o'] % 2
                            cnt['o'] += 1
                            hstate['o'] = o_i
                        o_i = hstate['o']
                        po, pon = psO[o_i], 'psM%d' % (3 + o_i)
                        s_ = jb['s_']
                        P.op('pe', lambda: nc.tensor.matmul(po[0:65, :], lhsT=VA[s_][:, jb['kt'], :], rhs=PT[si][:, :],
                                                            start=jb['first'], stop=False),
                             ['VA%d' % s_, 'PT%d' % si], [pon])
                    else:
                        o_i = hstate['o']
                        po, pon = psO[o_i], 'psM%d' % (3 + o_i)
                        tt = jb['tt']
                        X = tt // 2
                        P.op('pe', lambda: nc.tensor.matmul(po[0:65, X * 256:(X + 1) * 256],
                                                            lhsT=VD[:, tt, h * 65:(h + 1) * 65], rhs=PT[si][:, 0:256],
                                                            start=False, stop=jb['last']),
                             ['VD', 'PT%d' % si], [pon])
                        if jb['last']:
                            P.op('dve', lambda: nc.vector.reciprocal(out=rden[64:65, :], in_=po[64:65, :]),
                                 [pon], ['rden'])
                            P.op('pe', lambda: nc.tensor.matmul(psX[0:64, :], lhsT=onesf[64:65, 0:64],
                                                                rhs=rden[64:65, :], start=True, stop=True),
                                 ['onesf', 'rden'], ['psM5'])
                            P.op('act', lambda: nc.scalar.copy(out=ON[:, :], in_=po[0:64, :]), [pon], ['ON'])
                            P.op('dve', lambda: nc.vector.tensor_tensor(out=AT[:, h, :], in0=psX[0:64, :], in1=ON[:, :],
                                                                        op=ALU.mult), ['psM5', 'ON'], ['AT'])

                for idx in range(len(jobs) + LOOK):
                    if idx < len(jobs):
                        emit_qk(jobs[idx])
                    if idx >= LOOK:
                        emit_pv(jobs[idx - LOOK])
                for j in range(4):
                    for hf in range(2):
                        for h in range(NH):
                            P.op('pe', lambda: nc.tensor.matmul(psX[:, :], lhsT=AT[0:64, h, j * 128:(j + 1) * 128],
                                                                rhs=woH[0:64, h, hf * 512:(hf + 1) * 512],
                                                                start=(h == 0), stop=(h == NH - 1)),
                                 ['AT', 'woH'], ['psM5'])
                        P.op('dve', lambda: nc.vector.tensor_tensor(out=xt[:, j, hf * 512:(hf + 1) * 512],
                                                                    in0=psX[:, :], in1=xt[:, j, hf * 512:(hf + 1) * 512],
                                                                    op=ALU.add), ['psM5', 'xtD'], ['xtD'])
                P.dma('sp', 'xtDs', lambda: nc.sync.dma_start(
                    out=X3[g * 512:(g + 1) * 512, :].rearrange("(j p) f -> p j f", p=128), in_=xt[:, :, :]),
                    ['xtD'], [])
            P.barrier()

        phase_mlp(1, [(X3, y_out, 512 * G, 256), (XS3, ys_out, 128, 128)], 4, final_idx=5)

        P.barrier()
    return nc


_NC_CACHE = {}


def _bf16(a):
    return np.asarray(a, dtype=np.float32).astype(ml_dtypes.bfloat16)


def _fm(v):
    return np.ascontiguousarray(np.asarray(v, np.float32).reshape(8, 128).T)


def kernel(x_prompt, x_sample, state_conv, state_h, cache_k, cache_v, page_table,
           norm_mix, norm_mlp, w_ff1, w_ff2, w_rg_in, b_rg_in, conv_w, conv_b,
           w_gate_a, b_gate_a, w_gate_i, b_gate_i, lru_lambda, w_rg_out, b_rg_out,
           norm_kv, w_kv, w_q, w_o, norm_out):
    f32 = np.float32
    x_prompt = np.asarray(x_prompt, f32)
    B, T, _ = x_prompt.shape
    G = T // 2048
    nphys = int(np.asarray(cache_k).shape[0])
    if (G, nphys) not in _NC_CACHE:
        _NC_CACHE[(G, nphys)] = build(G, nphys)
    nc = _NC_CACHE[(G, nphys)]

    vecs = np.zeros((128, NV), f32)
    b_in = np.asarray(b_rg_in, f32)[0]
    vecs[:, 0:8] = _fm(b_in[:D])
    vecs[:, 8:16] = _fm(b_in[D:])
    for k in range(4):
        vecs[:, 16 + 8 * k:24 + 8 * k] = _fm(np.asarray(conv_w, f32)[0, k])
    vecs[:, 48:56] = _fm(np.asarray(conv_b, f32)[0])
    vecs[:, 56:64] = _fm(np.asarray(b_gate_a, f32)[0].reshape(-1))
    vecs[:, 64:72] = _fm(np.asarray(b_gate_i, f32)[0].reshape(-1))
    vecs[:, 72:80] = _fm(np.asarray(lru_lambda, f32)[0])
    rows = np.stack([np.asarray(norm_mix, f32)[0], np.asarray(norm_mlp, f32)[0], np.asarray(norm_kv, f32),
                     np.asarray(norm_mix, f32)[1], np.asarray(norm_mlp, f32)[1], np.asarray(norm_out, f32),
                     np.asarray(b_rg_out, f32)[0]])
    rows = np.concatenate([rows, b_in[None, D:]], axis=0)
    rows = np.ascontiguousarray(np.broadcast_to(rows[:, None, :], (8, 128, D)))
    ind = _bf16((np.arange(T)[None, :] // 256) == np.arange(32)[:, None])
    kk = np.arange(128)[:, None, None]
    ii = np.arange(2)[None, :, None]
    qq = np.arange(256)[None, None, :]
    cmask = _bf16(qq >= ii * 128 + kk)
    ident = _bf16(np.eye(128))
    common = dict(
        w_ff1=np.asarray(w_ff1, f32), w_ff2=np.asarray(w_ff2, f32), w_rg_in=np.asarray(w_rg_in, f32)[0],
        w_ga=np.asarray(w_gate_a, f32)[0], w_gi=np.asarray(w_gate_i, f32)[0],
        w_rg_out=np.asarray(w_rg_out, f32)[0], w_kv=np.asarray(w_kv, f32), w_q=np.asarray(w_q, f32)[0],
        w_o=np.asarray(w_o, f32)[0], vecs=vecs, rows=rows, ind=ind, cmask=cmask, ident=ident)
    common.update(
        xs=np.ascontiguousarray(np.asarray(x_sample, f32)[:, 0, :]),
        sconv=np.ascontiguousarray(np.asarray(state_conv, f32)[0]),
        sh=np.ascontiguousarray(np.asarray(state_h, f32)[0]),
        iop=np.arange(128, dtype=f32).reshape(128, 1),
        ck=np.asarray(cache_k, f32).reshape(-1, D), cv=np.asarray(cache_v, f32).reshape(-1, D))
    ptab_all = np.asarray(page_table).astype(np.int32)
    in_maps = []
    chunks = []
    p = np.arange(128)
    for c in range(8):
        s, j = c // 4, c % 4
        cgs = [4 * g + (j if g % 2 == 0 else 3 - j) for g in range(G)]
        chunks.append(cgs)
        qidx = np.zeros((128, 4 * G), np.int32)
        ktidx = np.zeros((128, 16 * G), np.int32)
        for g, cg in enumerate(cgs):
            for jj in range(4):
                qidx[:, g * 4 + jj] = cg * 512 + jj * 128 + p
            for h in range(NH):
                ktidx[:, g * 16 + h] = (cg * NH + h) * 128 + p
        qblk = qidx // 256
        past01 = (np.arange(32)[None, None, :] < qblk[:, :, None]).astype(f32)
        pastb = ((past01 - 1.0) * 1e30).astype(f32)
        m = dict(common)
        ownidx = np.zeros((128, 1), np.int32)
        ownidx[:NS, 0] = NS * c + np.arange(NS)
        ownbc = np.ascontiguousarray(np.broadcast_to((NS * c + np.arange(NS, dtype=np.int32))[None, :], (128, NS)))
        m.update(xp=np.ascontiguousarray(x_prompt[s]), qidx=qidx, ktidx=ktidx, past01=past01, pastb=pastb,
                 ptab=np.ascontiguousarray(ptab_all[NS * c:NS * (c + 1)]), ownidx=ownidx, ownbc=ownbc)
        in_maps.append(m)
    res = run_bass_kernel_spmd(nc, in_maps, core_ids=list(range(8))).results

    y_prompt = np.zeros((B, T, D), f32)
    for c in range(8):
        s = c // 4
        for g, cg in enumerate(chunks[c]):
            y_prompt[s, cg * 512:(cg + 1) * 512] = res[c]["y"][g * 512:(g + 1) * 512]
    k_p = np.stack([res[0]["k"], res[4]["k"]]).reshape(B, T, NH, HD)
    v_p = np.stack([res[0]["v"], res[4]["v"]]).reshape(B, T, NH, HD)
    conv_p = np.stack([res[0]["convp"], res[4]["convp"]])[None]
    h_p = np.stack([res[0]["hp"], res[4]["hp"]])[None]
    y_s = np.concatenate([res[c]["ys"][:NS] for c in range(8)], axis=0)[:, None, :]
    conv_s = res[0]["convs"][None]
    h_s = res[0]["hs"][None]
    k_s = res[0]["ks"].reshape(128, 1, NH, HD)
    v_s = res[0]["vs"].reshape(128, 1, NH, HD)
    return (y_prompt, y_s, conv_p, h_p, k_p, v_p, conv_s, h_s, k_s, v_s)
```

```python
import contextlib
import numpy as np
import ml_dtypes
import concourse.bass as bass
import concourse.mybir as mybir
from concourse.bass_utils import run_bass_kernel_spmd

F32 = mybir.dt.float32
BF16 = mybir.dt.bfloat16
I32 = mybir.dt.int32
AF = mybir.ActivationFunctionType
ALU = mybir.AluOpType
AX = mybir.AxisListType

D = 1024
DFF = 4096
NH = 16
HD = 64
EPS = 1e-6
NS = 16
NPG = 16
NEGB = -30000.0
NV = 80


class Prog:
    def __init__(self, nc):
        self.nc = nc
        self.E = dict(pe=nc.tensor, act=nc.scalar, dve=nc.vector, pool=nc.gpsimd, sp=nc.sync)
        self.sem = {}
        self.cnt = {}
        for e in self.E:
            self.sem[e] = nc.alloc_semaphore("s_" + e)
            self.cnt[e] = 0
        self.known = {e: {} for e in self.E}
        self.lastw = {}
        self.readers = {}
        self.dk = {}
        self.nps = 0

    def _wait(self, eng, deps):
        best = {}
        for t in deps:
            if t is None:
                continue
            k, v = t
            if eng == 'pe' and k == 'pe':
                continue
            if v > best.get(k, 0):
                best[k] = v
        for k, v in best.items():
            if self.known[eng].get(k, 0) >= v:
                continue
            sem = self.sem[k] if k in self.sem else self.dk[k][0]
            self.E[eng].wait_ge(sem, v)
            self.known[eng][k] = v

    def _deps(self, reads, writes):
        deps = []
        for b in reads:
            deps.append(self.lastw.get(b))
        for b in writes:
            deps.append(self.lastw.get(b))
            deps.extend(self.readers.get(b, {}).items())
        return deps

    def _record(self, tok, reads, writes):
        k, v = tok
        for b in reads:
            self.readers.setdefault(b, {})[k] = v
        for b in writes:
            self.lastw[b] = tok
            self.readers[b] = {}

    def op(self, eng, fn, reads=(), writes=()):
        self._wait(eng, self._deps(reads, writes))
        ins = fn()
        self.cnt[eng] += 1
        ins.then_inc(self.sem[eng], 1)
        self._record((eng, self.cnt[eng]), reads, writes)

    def dma(self, q, key, fn, reads=(), writes=()):
        self._wait(q, self._deps(reads, writes))
        ins = fn()
        if key not in self.dk:
            self.dk[key] = [self.nc.alloc_semaphore("d_" + key), 0]
        self.dk[key][1] += 16
        ins.then_inc(self.dk[key][0], 16)
        self._record((key, self.dk[key][1]), reads, writes)

    def barrier(self):
        for e in self.E:
            deps = [(k, self.cnt[k]) for k in self.E if self.cnt[k] > 0]
            deps += [(k, v[1]) for k, v in self.dk.items()]
            self._wait(e, deps)
        self.lastw = {}
        self.readers = {}


def build(G, NPHYS=2560):
    T = 2048 * G
    NCH = T // 512
    nc = bass.Bass("TRN2", target_bir_lowering=False)
    P = Prog(nc)

    def din(name, shape, dt=F32):
        return nc.dram_tensor(name, list(shape), dt, kind="ExternalInput").ap()

    def dout(name, shape, dt=F32):
        return nc.dram_tensor(name, list(shape), dt, kind="ExternalOutput").ap()

    def dscr(name, shape, dt=F32):
        return nc.dram_tensor(name, list(shape), dt, kind="Internal").ap()

    xp = din("xp", [T, D])
    w_ff1 = din("w_ff1", [2, D, DFF])
    w_ff2 = din("w_ff2", [2, DFF, D])
    w_rg_in = din("w_rg_in", [D, 2 * D])
    w_ga = din("w_ga", [4, 256, 256])
    w_gi = din("w_gi", [4, 256, 256])
    w_rg_out = din("w_rg_out", [D, D])
    w_kv = din("w_kv", [D, 2 * D])
    w_q = din("w_q", [D, D])
    w_o = din("w_o", [D, D])
    vecs_d = din("vecs", [128, NV])
    rows_d = din("rows", [8, 128, D])
    qidx_d = din("qidx", [128, 4 * G], I32)
    ktidx_d = din("ktidx", [128, 16 * G], I32)
    pastb_d = din("pastb", [128, 4 * G, 32])
    past01_d = din("past01", [128, 4 * G, 32])
    ind_d = din("ind", [32, T], BF16)
    cmask_d = din("cmask", [128, 2, 256], BF16)
    ident_d = din("ident", [128, 128], BF16)
    xs_d = din("xs", [128, D])
    sconv_d = din("sconv", [128, 3, D])
    sh_d = din("sh", [128, D])
    ck_d = din("ck", [NPHYS * 64, 2 * D])
    cv_d = din("cv", [NPHYS * 64, 2 * D])
    ptab_d = din("ptab", [NS, NPG], I32)
    ownidx_d = din("ownidx", [128, 1], I32)
    ownbc_d = din("ownbc", [128, NS], I32)
    iop_d = din("iop", [128, 1])
    y_out = dout("y", [512 * G, D])
    k_out = dout("k", [T, D])
    v_out = dout("v", [T, D])
    conv_out = dout("convp", [3, D])
    h_out = dout("hp", [D])
    ys_out = dout("ys", [128, D])
    convs_out = dout("convs", [128, 3, D])
    hs_out = dout("hs", [128, D])
    ks_out = dout("ks", [128, D])
    vs_out = dout("vs", [128, D])
    XS1 = dscr("XS1", [128, D])
    XS2 = dscr("XS2", [128, D])
    XS3 = dscr("XS3", [128, D])
    KVS = dscr("KVS", [128, 2 * D])
    Qd = dscr("Qd", [128, D])
    Dn = dscr("Dn", [NS, 16])
    X1 = dscr("X1", [T, D])
    X2 = dscr("X2", [T, D])
    X3 = dscr("X3", [512 * G, D])
    KTd = dscr("KTd", [NCH * NH * 128, 512], BF16)
    Vd = dscr("Vd", [NH, T, 65], BF16)
    Vd2 = dscr("Vd2", [T, NH * 65], BF16)

    es = contextlib.ExitStack()

    uniq = [0]

    def sb(stack, name, shape, dt):
        uniq[0] += 1
        return stack.enter_context(nc.sbuf_tensor("sb%d_%s" % (uniq[0], name), list(shape), dt))

    def pst(stack, name, shape, dt):
        return stack.enter_context(nc.psum_tensor(name, list(shape), dt))

    with es:
        ident = sb(es, "ident", [128, 128], BF16)
        vecs = sb(es, "vecs", [128, NV], F32)
        grow = sb(es, "grow", [128, D], F32)
        ssq = sb(es, "ssq", [128, 4], F32)
        msq = sb(es, "msq", [128, 4], F32)
        rstd = sb(es, "rstd", [128, 4], F32)
        junk = sb(es, "junk", [128, D], BF16)
        P.dma('sp', 'c_ident', lambda: nc.sync.dma_start(out=ident[:, :], in_=ident_d[:, :]), [], ['ident'])
        P.dma('sp', 'c_vecs', lambda: nc.sync.dma_start(out=vecs[:, :], in_=vecs_d[:, :]), [], ['vecs'])

        psT = [pst(es, "psT%d" % i, [128, 1024], BF16) for i in range(2)]
        psM = [pst(es, "psM%d" % i, [128, 512], F32) for i in range(6)]
        st = dict(t=0, m=0)

        def next_psT():
            st['t'] += 1
            i = st['t'] % 2
            return psT[i], "psT%d" % i

        def next_psM(nm=4, base=0):
            st['m'] += 1
            i = base + st['m'] % nm
            return psM[i], "psM%d" % i

        def load_grow(idx):
            P.dma('sp', 'c_grow', lambda: nc.sync.dma_start(out=grow[:, :], in_=rows_d[idx, :, :]), [], ['grow'])

        def norm_T(xsub, xname, nsub, nt, xn, xnT, do_T=True, gname='grow', gt=None, xnn='xn', xnTn='xnT'):
            g_t = grow if gt is None else gt
            for j in range(nsub):
                P.op('act', lambda: nc.scalar.activation(out=junk[:nt, :], in_=xsub(j), func=AF.Square,
                                                          accum_out=ssq[:nt, j:j + 1]), [xname], ['junk', 'ssq'])
            P.op('dve', lambda: nc.vector.tensor_scalar(out=msq[:nt, :nsub], in0=ssq[:nt, :nsub], scalar1=1.0 / D,
                                                        scalar2=EPS, op0=ALU.mult, op1=ALU.add), ['ssq'], ['msq'])
            P.op('act', lambda: nc.scalar.activation(out=msq[:nt, :nsub], in_=msq[:nt, :nsub], func=AF.Sqrt),
                 ['msq'], ['msq'])
            P.op('dve', lambda: nc.vector.reciprocal(out=rstd[:nt, :nsub], in_=msq[:nt, :nsub]), ['msq'], ['rstd'])
            for j in range(nsub):
                P.op('dve', lambda: nc.vector.scalar_tensor_tensor(out=xn[:nt, j, :], in0=xsub(j),
                                                                    scalar=rstd[:nt, j:j + 1], in1=g_t[:nt, :],
                                                                    op0=ALU.mult, op1=ALU.mult),
                     [xname, 'rstd', gname], [xnn])
            if not do_T:
                return
            for c in range(8):
                ps, pn = next_psT()
                for j in range(nsub):
                    P.op('pe', lambda: nc.tensor.transpose(out=ps[:, j * nt:(j + 1) * nt],
                                                           in_=xn[:nt, j, c * 128:(c + 1) * 128],
                                                           identity=ident[:nt, :nt]), [xnn, 'ident'], [pn])
                if c % 2 == 0:
                    P.op('act', lambda: nc.scalar.copy(out=xnT[:, c, :nsub * nt], in_=ps[:, :nsub * nt]), [pn], [xnTn])
                else:
                    P.op('dve', lambda: nc.vector.tensor_copy(out=xnT[:, c, :nsub * nt], in_=ps[:, :nsub * nt]),
                         [pn], [xnTn])

        def load_w(stack_w, name, src, shape_kc, ncols, splits=1):
            t = sb(stack_w, name, [128, shape_kc, ncols], BF16)
            w = ncols // splits
            for s_ in range(splits):
                P.dma('pool', 'w_' + name, lambda: nc.gpsimd.dma_start(
                    out=t[:, :, s_ * w:(s_ + 1) * w],
                    in_=src[:, s_ * w:(s_ + 1) * w].rearrange("(kc p) f -> p kc f", p=128)), [], [name])
            return t

        TA = 256
        with contextlib.ExitStack() as sa:
            w_in_sb = load_w(sa, "w_in", w_rg_in, 8, 2048)
            wga = sb(sa, "wga", [128, 4, 2, 256], BF16)
            wgi = sb(sa, "wgi", [128, 4, 2, 256], BF16)
            P.dma('pool', 'w_wga', lambda: nc.gpsimd.dma_start(
                out=wga[:, :, :, :], in_=w_ga.rearrange("n (kc p) f -> p n kc f", p=128)), [], ['wga'])
            P.dma('pool', 'w_wgi', lambda: nc.gpsimd.dma_start(
                out=wgi[:, :, :, :], in_=w_gi.rearrange("n (kc p) f -> p n kc f", p=128)), [], ['wgi'])
            wout_sb = load_w(sa, "w_out", w_rg_out, 8, 1024)
            brow = sb(sa, "brow", [128, D], F32)
            P.dma('sp', 'c_brow', lambda: nc.sync.dma_start(out=brow[:, :], in_=rows_d[6, :, :]), [], ['brow'])
            load_grow(0)
            sc = sb(sa, "sc", [128, 8], F32)
            sc2 = sb(sa, "sc2", [128, 8], F32)
            P.op('act', lambda: nc.scalar.activation(out=sc[:, :], in_=vecs[:, 72:80], func=AF.Exp, scale=-1.0),
                 ['vecs'], ['sc'])
            P.op('act', lambda: nc.scalar.activation(out=sc[:, :], in_=sc[:, :], func=AF.Ln, bias=1.0), ['sc'], ['sc'])
            P.op('dve', lambda: nc.vector.tensor_scalar(out=sc2[:, :], in0=sc[:, :], scalar1=-16.0, scalar2=None,
                                                        op0=ALU.mult), ['sc'], ['sc2'])
            P.op('dve', lambda: nc.vector.tensor_scalar(out=sc[:, :], in0=sc[:, :], scalar1=-8.0, scalar2=None,
                                                        op0=ALU.mult), ['sc'], ['sc'])
            xt3 = [sb(sa, "xtA%d" % i, [128, 2, D], F32) for i in range(3)]
            xn2 = [sb(sa, "xnA%d" % i, [128, 2, D], BF16) for i in range(2)]
            xnT2 = [sb(sa, "xnTA%d" % i, [128, 8, TA], BF16) for i in range(2)]
            uT2 = [sb(sa, "uT%d" % i, [128, 8, TA + 3], F32) for i in range(2)]
            GX2 = [sb(sa, "GX%d" % i, [128, 8, TA], F32) for i in range(2)]
            Z = sb(sa, "Z", [128, 8, TA], F32)
            C = sb(sa, "C", [128, 8, TA], F32)
            Cbf = sb(sa, "Cbf", [128, 8, TA], BF16)
            R = sb(sa, "R", [128, 8, TA], F32)
            I_ = sb(sa, "I", [128, 8, TA], F32)
            E_ = sb(sa, "E", [128, 8, TA], F32)
            H = sb(sa, "H", [128, 8, TA], F32)
            HG = sb(sa, "HG", [128, 8, TA], BF16)
            hst = sb(sa, "hst", [128, 8], F32)
            UTM = sb(sa, "UTM", [128, D], F32)
            urow = sb(sa, "urow", [128, D], F32)
            id32 = sb(sa, "id32", [128, 128], F32)
            P.dma('sp', 'c_urow', lambda: nc.sync.dma_start(out=urow[:, :], in_=rows_d[7, :, :]), [], ['urow'])
            P.op('dve', lambda: nc.vector.tensor_copy(out=id32[:, :], in_=ident[:, :]), ['ident'], ['id32'])
            P.op('dve', lambda: nc.vector.memset(uT2[0][:, :, 0:3], 0.0), [], ['uTh0'])
            P.op('dve', lambda: nc.vector.memset(hst[:, :], 0.0), [], ['hst'])

            jobs = [('p', i) for i in range(T // TA)] + [('s', 0)]

            def run_merged(gens):
                pend = []
                for gi, g in enumerate(gens):
                    try:
                        pend.append([next(g), gi, g])
                    except StopIteration:
                        pass
                while pend:
                    pend.sort(key=lambda x_: (x_[0], x_[1]))
                    it = pend[0]
                    try:
                        it[0] = next(it[2])
                    except StopIteration:
                        pend.pop(0)

            def jvars(k):
                kind, i = jobs[k]
                smp = (kind == 's')
                NCl = 128 if smp else TA
                nsub = 1 if smp else 2
                t0 = i * TA
                sl2 = k % 2
                xs = xt3[k % 3]
                xsn = "xtA%d" % (k % 3)
                return kind, i, smp, NCl, nsub, t0, sl2, xs, xsn

            def loadA(k):
                kind, i, smp, NCl, nsub, t0, sl2, xs, xsn = jvars(k)
                if smp:
                    P.dma('sp', xsn, lambda: nc.sync.dma_start(out=xs[:, 0, :], in_=xs_d[:, :]), [], [xsn])
                    for k3 in range(2):
                        P.dma('sp', 'o_cs', lambda: nc.sync.dma_start(out=convs_out[:, k3, :],
                                                                       in_=sconv_d[:, k3 + 1, :]), [], [])
                else:
                    P.dma('sp', xsn, lambda: nc.sync.dma_start(
                        out=xs[:, :, :], in_=xp[t0:t0 + TA, :].rearrange("(j p) f -> p j f", p=128)), [], [xsn])

            def front(k):
                kind, i, smp, NCl, nsub, t0, sl2, xs, xsn = jvars(k)
                xn, xnT, uT, GX = xn2[sl2], xnT2[sl2], uT2[sl2], GX2[sl2]
                XN, XNT, UT, GXn = 'xn%d' % sl2, 'xnT%d' % sl2, 'uT%d' % sl2, 'GX%d' % sl2
                yield 0.0
                norm_T(lambda j: xs[:, j, :], xsn, nsub, 128, xn, None, do_T=False, xnn=XN)
                for c in range(8):
                    yield 4.0 + 0.5 * c
                    ps, pn = next_psT()
                    for j in range(nsub):
                        P.op('pe', lambda: nc.tensor.transpose(out=ps[:, j * 128:(j + 1) * 128],
                                                               in_=xn[:128, j, c * 128:(c + 1) * 128],
                                                               identity=ident[:128, :128]), [XN, 'ident'], [pn])
                    if c % 2 == 0:
                        P.op('act', lambda: nc.scalar.copy(out=xnT[:, c, :nsub * 128], in_=ps[:, :nsub * 128]),
                             [pn], [XNT])
                    else:
                        P.op('dve', lambda: nc.vector.tensor_copy(out=xnT[:, c, :nsub * 128], in_=ps[:, :nsub * 128]),
                             [pn], [XNT])
                for oc in range(16):
                    yield 8.0 + 2.1 * oc
                    ps, pn = next_psM()
                    for kc in range(8):
                        P.op('pe', lambda: nc.tensor.matmul(ps[:, :NCl], lhsT=w_in_sb[:, kc, oc * 128:(oc + 1) * 128],
                                                            rhs=xnT[:, kc, :NCl], start=(kc == 0), stop=(kc == 7)),
                             ['w_in', XNT], [pn])
                    if oc < 8:
                        P.op('act', lambda: nc.scalar.activation(out=GX[:, oc, :NCl], in_=ps[:, :NCl], func=AF.Identity,
                                                                  bias=vecs[:, oc:oc + 1]), [pn, 'vecs'], [GXn])
                    else:
                        c = oc - 8
                        P.op('dve', lambda: nc.vector.tensor_scalar(out=uT[:, c, 3:3 + NCl], in0=ps[:, :NCl],
                                                                    scalar1=vecs[:, 8 + c:9 + c], scalar2=None,
                                                                    op0=ALU.add), [pn, 'vecs'], [UT])
                if smp:
                    for hf in range(2):
                        yield 43.0 + hf
                        ps, pn = next_psM()
                        for kc in range(8):
                            P.op('pe', lambda: nc.tensor.matmul(ps[:, :], lhsT=xnT[:, kc, 0:128],
                                                                rhs=w_in_sb[:, kc, D + hf * 512:D + (hf + 1) * 512],
                                                                start=(kc == 0), stop=(kc == 7)), ['w_in', XNT], [pn])
                        P.op('dve', lambda: nc.vector.tensor_tensor(out=UTM[:, hf * 512:(hf + 1) * 512], in0=ps[:, :],
                                                                    in1=urow[:, hf * 512:(hf + 1) * 512], op=ALU.add),
                             [pn, 'urow'], ['UTM'])
                    P.dma('sp', 'o_cs2', lambda: nc.sync.dma_start(out=convs_out[:, 2, :], in_=UTM[:, :]), ['UTM'], [])

            def gelu(k):
                kind, i, smp, NCl, nsub, t0, sl2, xs, xsn = jvars(k)
                GX = GX2[sl2]
                GXn = 'GX%d' % sl2
                yield 25.0
                P.op('pool', lambda: nc.gpsimd.tensor_tensor(out=Z[:, :, :NCl], in0=GX[:, :, :NCl],
                                                             in1=GX[:, :, :NCl], op=ALU.mult), [GXn], ['Z'])
                yield 29.0
                P.op('dve', lambda: nc.vector.tensor_scalar(out=Z[:, :, :NCl], in0=Z[:, :, :NCl],
                                                            scalar1=0.044715, scalar2=1.0, op0=ALU.mult,
                                                            op1=ALU.add), ['Z'], ['Z'])
                yield 31.0
                P.op('pool', lambda: nc.gpsimd.tensor_tensor(out=Z[:, :, :NCl], in0=Z[:, :, :NCl],
                                                             in1=GX[:, :, :NCl], op=ALU.mult), ['Z', GXn], ['Z'])
                yield 35.0
                P.op('act', lambda: nc.scalar.activation(out=Z[:, :, :NCl], in_=Z[:, :, :NCl], func=AF.Sigmoid,
                                                          scale=1.5957691216057308), ['Z'], ['Z'])
                yield 36.5
                P.op('pool', lambda: nc.gpsimd.tensor_tensor(out=GX[:, :, :NCl], in0=GX[:, :, :NCl],
                                                             in1=Z[:, :, :NCl], op=ALU.mult), ['Z', GXn], [GXn])

            def back(k):
                kind, i, smp, NCl, nsub, t0, sl2, xs, xsn = jvars(k)
                xn, xnT, uT, GX = xn2[sl2], xnT2[sl2], uT2[sl2], GX2[sl2]
                XN, XNT, UT, GXn = 'xn%d' % sl2, 'xnT%d' % sl2, 'uT%d' % sl2, 'GX%d' % sl2
                UTH = 'uTh%d' % sl2
                uTn = uT2[1 - sl2]
                UTHn = 'uTh%d' % (1 - sl2)
                yield 0.5
                if smp:
                    for k3 in range(4):
                        if k3 < 3:
                            P.dma('sp', xsn, lambda: nc.sync.dma_start(out=xs[:, 1, :], in_=sconv_d[:, k3, :]),
                                  [], [xsn])
                        else:
                            P.dma('sp', xsn, lambda: nc.sync.dma_start(out=xs[:, 1, :], in_=sh_d[:, :]), [], [xsn])
                        for c in range(8):
                            ps, pn = next_psM()
                            P.op('pe', lambda: nc.tensor.transpose(out=ps[:, 0:128], in_=xs[:, 1, c * 128:(c + 1) * 128],
                                                                   identity=id32[:, :]), [xsn, 'id32'], [pn])
                            dstt, dn, o_ = ((Z, 'Z', 0), (Z, 'Z', 128), (E_, 'E', 0), (E_, 'E', 128))[k3]
                            P.op('act', lambda: nc.scalar.copy(out=dstt[:, c, o_:o_ + 128], in_=ps[:, 0:128]),
                                 [pn], [dn])
                sdeps = ['Z', 'E'] if smp else []
                for c in range(8):
                    if c > 0:
                        yield 0.5 + 1.4 * c
                    if smp:
                        taps = [Z[:, c, 0:128], Z[:, c, 128:256], E_[:, c, 0:128], uT[:, c, 3:3 + 128]]
                    else:
                        taps = [uT[:, c, kk:kk + TA] for kk in range(4)]
                    P.op('dve', lambda: nc.vector.tensor_scalar(out=C[:, c, :NCl], in0=taps[0],
                                                                scalar1=vecs[:, 16 + c:17 + c],
                                                                scalar2=vecs[:, 48 + c:49 + c],
                                                                op0=ALU.mult, op1=ALU.add), [UT, UTH, 'vecs'] + sdeps, ['C'])
                    for kk in range(1, 4):
                        P.op('dve', lambda: nc.vector.scalar_tensor_tensor(
                            out=C[:, c, :NCl], in0=taps[kk], scalar=vecs[:, 16 + 8 * kk + c:17 + 8 * kk + c],
                            in1=C[:, c, :NCl], op0=ALU.mult, op1=ALU.add), [UT, UTH, 'vecs', 'C'] + sdeps, ['C'])
                yield 11.5
                P.op('act', lambda: nc.scalar.copy(out=Cbf[:, :, :NCl], in_=C[:, :, :NCl]), ['C'], ['Cbf'])
                if kind == 'p' and i == T // TA - 1:
                    with nc.allow_non_contiguous_dma(reason="tiny state output"):
                        for k3 in range(3):
                            P.dma('sp', 'o_conv', lambda: nc.sync.dma_start(
                                out=conv_out[k3].rearrange("(c p) -> p c", p=128), in_=uT[:, :, TA + k3]),
                                [UT], [])
                if not smp:
                    P.op('pool', lambda: nc.gpsimd.tensor_copy(out=uTn[:, :, 0:3], in_=uT[:, :, TA:TA + 3]),
                         [UT], [UTHn])
                for c in range(8):
                    yield 13.0 + 1.0 * c
                    n, dc = c // 2, c % 2
                    for (wg, dst, dn, bo) in ((wga, R, 'R', 56), (wgi, I_, 'I', 64)):
                        ps, pn = next_psM()
                        for kc in range(2):
                            P.op('pe', lambda: nc.tensor.matmul(ps[:, :NCl], lhsT=wg[:, n, kc, dc * 128:(dc + 1) * 128],
                                                                rhs=Cbf[:, 2 * n + kc, :NCl], start=(kc == 0),
                                                                stop=(kc == 1)), ['wga', 'wgi', 'Cbf'], [pn])
                        P.op('act', lambda: nc.scalar.activation(out=dst[:, c, :NCl], in_=ps[:, :NCl], func=AF.Sigmoid,
                                                                  bias=vecs[:, bo + c:bo + c + 1]), [pn, 'vecs'], [dn])
                E2 = H if smp else E_
                E2n = 'H' if smp else 'E'
                for c in range(8):
                    yield 21.0 + 0.4 * c
                    P.op('act', lambda: nc.scalar.activation(out=E2[:, c, :NCl], in_=R[:, c, :NCl], func=AF.Exp,
                                                              scale=sc2[:, c:c + 1]), ['R', 'sc2'], [E2n])
                for c in range(8):
                    yield 24.2 + 0.4 * c
                    P.op('act', lambda: nc.scalar.activation(out=R[:, c, :NCl], in_=R[:, c, :NCl], func=AF.Exp,
                                                              scale=sc[:, c:c + 1]), ['R', 'sc'], ['R'])
                yield 27.5
                P.op('act', lambda: nc.scalar.activation(out=E2[:, :, :NCl], in_=E2[:, :, :NCl], func=AF.Sqrt,
                                                          scale=-1.0, bias=1.0), [E2n], [E2n])
                yield 28.0
                P.op('pool', lambda: nc.gpsimd.tensor_tensor(out=I_[:, :, :NCl], in0=I_[:, :, :NCl], in1=E2[:, :, :NCl],
                                                             op=ALU.mult), ['I', E2n], ['I'])
                yield 32.0
                P.op('pool', lambda: nc.gpsimd.tensor_tensor(out=I_[:, :, :NCl], in0=I_[:, :, :NCl], in1=C[:, :, :NCl],
                                                             op=ALU.mult), ['I', 'C'], ['I'])
                if smp:
                    yield 36.0
                    P.op('dve', lambda: nc.vector.tensor_tensor(out=H[:, :, :128], in0=R[:, :, :128],
                                                                in1=E_[:, :, 128:256], op=ALU.mult),
                         ['R', 'E', 'I'], ['H'])
                    P.op('dve', lambda: nc.vector.tensor_tensor(out=H[:, :, :128], in0=H[:, :, :128],
                                                                in1=I_[:, :, :128], op=ALU.add), ['H', 'I'], ['H'])
                    for c in range(8):
                        ps, pn = next_psM()
                        P.op('pe', lambda: nc.tensor.transpose(out=ps[:, 0:128], in_=H[:, c, 0:128],
                                                               identity=id32[:, :]), ['H', 'id32'], [pn])
                        P.op('act', lambda: nc.scalar.copy(out=UTM[:, c * 128:(c + 1) * 128], in_=ps[:, 0:128]),
                             [pn], ['UTM'])
                    P.dma('sp', 'o_hs', lambda: nc.sync.dma_start(out=hs_out[:, :], in_=UTM[:, :]), ['UTM'], [])
                else:
                    for c in range(8):
                        yield 36.0 + 0.6 * c
                        P.op('dve', lambda: nc.vector.tensor_tensor_scan(out=H[:, c, :], data0=R[:, c, :],
                                                                         data1=I_[:, c, :], initial=hst[:, c:c + 1],
                                                                         op0=ALU.mult, op1=ALU.add),
                             ['R', 'I', 'hst'], ['H'])
                    yield 40.8
                    P.op('dve', lambda: nc.vector.tensor_copy(out=hst[:, :], in_=H[:, :, TA - 1]), ['H'], ['hst'])
                yield 41.0
                P.op('pool', lambda: nc.gpsimd.tensor_tensor(out=HG[:, :, :NCl], in0=H[:, :, :NCl], in1=GX[:, :, :NCl],
                                                             op=ALU.mult), ['H', GXn], ['HG'])
                for j in range(nsub):
                    for hf in range(2):
                        yield 44.0 + 1.5 * (2 * j + hf)
                        ps, pn = next_psM()
                        for kc in range(8):
                            P.op('pe', lambda: nc.tensor.matmul(ps[:, :], lhsT=HG[:, kc, j * 128:(j + 1) * 128],
                                                                rhs=wout_sb[:, kc, hf * 512:(hf + 1) * 512],
                                                                start=(kc == 0), stop=(kc == 7)), ['HG', 'w_out'], [pn])
                        P.op('dve', lambda: nc.vector.tensor_tensor(out=xs[:, j, hf * 512:(hf + 1) * 512],
                                                                    in0=ps[:, :], in1=xs[:, j, hf * 512:(hf + 1) * 512],
                                                                    op=ALU.add), [pn, xsn], [xsn])
                    P.op('pool', lambda: nc.gpsimd.tensor_tensor(out=xs[:, j, :], in0=xs[:, j, :], in1=brow[:, :],
                                                                 op=ALU.add), [xsn, 'brow'], [xsn])
                yield 50.0
                if smp:
                    P.dma('sp', xsn + 's', lambda: nc.sync.dma_start(out=XS1[:, :], in_=xs[:, 0, :]), [xsn], [])
                else:
                    P.dma('sp', xsn + 's', lambda: nc.sync.dma_start(
                        out=X1[t0:t0 + TA, :].rearrange("(j p) f -> p j f", p=128), in_=xs[:, :, :]), [xsn], [])
                    if i == T // TA - 1:
                        with nc.allow_non_contiguous_dma(reason="tiny state output"):
                            P.dma('sp', 'o_h', lambda: nc.sync.dma_start(out=h_out.rearrange("(c p) -> p c", p=128),
                                                                         in_=hst[:, :]), ['hst'], [])

            loadA(0)
            loadA(1)
            run_merged([front(0), gelu(0)])
            for k in range(len(jobs)):
                if k + 2 < len(jobs):
                    loadA(k + 2)
                gens = [back(k)]
                if k + 1 < len(jobs):
                    gens.append(front(k + 1))
                    gens.append(gelu(k + 1))
                run_merged(gens)
            P.barrier()

        def phase_mlp(layer, jobs, grow_idx, final_idx=None):
            TBm = 256
            with contextlib.ExitStack() as sm:
                ff1 = load_w(sm, "ff1", w_ff1[layer], 8, DFF, splits=2)
                ff2 = load_w(sm, "ff2", w_ff2[layer], 32, D)
                load_grow(grow_idx)
                if final_idx is not None:
                    grow2 = sb(sm, "grow2", [128, D], F32)
                    P.dma('sp', 'c_grow2', lambda: nc.sync.dma_start(out=grow2[:, :], in_=rows_d[final_idx, :, :]),
                          [], ['grow2'])
                    yt = sb(sm, "yt", [128, 2, D], F32)
                xt = [sb(sm, "xtB%d" % i, [128, 2, D], F32) for i in range(2)]
                xn = sb(sm, "xnB", [128, 2, D], BF16)
                xnT = sb(sm, "xnTB", [128, 8, TBm], BF16)
                RL = [sb(sm, "RL%d" % i, [128, TBm], F32) for i in range(2)]
                HT = sb(sm, "HT", [128, 32, TBm], BF16)
                ti = 0
                for (src, dst, ntok, TB) in jobs:
                    nsub = TB // 128
                    for i in range(ntok // TB):
                        t0 = i * TB
                        xs = xt[ti % 2]
                        xsn = "xtB%d" % (ti % 2)
                        ti += 1
                        P.dma('sp', xsn, lambda: nc.sync.dma_start(
                            out=xs[:, 0:nsub, :], in_=src[t0:t0 + TB, :].rearrange("(j p) f -> p j f", p=128)),
                            [], [xsn])
                        norm_T(lambda j: xs[:, j, :], xsn, nsub, 128, xn, xnT)
                        for oc in range(32):
                            ps, pn = next_psM()
                            for kc in range(8):
                                P.op('pe', lambda: nc.tensor.matmul(ps[:, :TB], lhsT=ff1[:, kc, oc * 128:(oc + 1) * 128],
                                                                    rhs=xnT[:, kc, :TB], start=(kc == 0),
                                                                    stop=(kc == 7)), ['ff1', 'xnT'], [pn])
                            rl = RL[oc % 2]
                            rln = "RL%d" % (oc % 2)
                            P.op('act', lambda: nc.scalar.activation(out=rl[:, :TB], in_=ps[:, :TB], func=AF.Relu),
                                 [pn], [rln])
                            P.op('pool', lambda: nc.gpsimd.tensor_tensor(out=HT[:, oc, :TB], in0=rl[:, :TB],
                                                                         in1=rl[:, :TB], op=ALU.mult), [rln], ['HT'])
                        for j in range(nsub):
                            for hf in range(2):
                                ps, pn = next_psM()
                                for kc in range(32):
                                    P.op('pe', lambda: nc.tensor.matmul(ps[:, :], lhsT=HT[:, kc, j * 128:(j + 1) * 128],
                                                                        rhs=ff2[:, kc, hf * 512:(hf + 1) * 512],
                                                                        start=(kc == 0), stop=(kc == 31)),
                                         ['HT', 'ff2'], [pn])
                                P.op('dve', lambda: nc.vector.tensor_tensor(out=xs[:, j, hf * 512:(hf + 1) * 512],
                                                                            in0=ps[:, :],
                                                                            in1=xs[:, j, hf * 512:(hf + 1) * 512],
                                                                            op=ALU.add), [pn, xsn], [xsn])
                        if final_idx is None:
                            P.dma('sp', xsn + 's', lambda: nc.sync.dma_start(
                                out=dst[t0:t0 + TB, :].rearrange("(j p) f -> p j f", p=128), in_=xs[:, 0:nsub, :]),
                                [xsn], [])
                        else:
                            norm_T(lambda j: xs[:, j, :], xsn, nsub, 128, yt, None, do_T=False, gname='grow2', gt=grow2)
                            P.dma('sp', 'yts', lambda: nc.sync.dma_start(
                                out=dst[t0:t0 + TB, :].rearrange("(j p) f -> p j f", p=128), in_=yt[:, 0:nsub, :]),
                                ['xn'], [])
                P.barrier()

        phase_mlp(0, [(X1, X2, T, 256), (XS1, XS2, 128, 128)], 1)

        TC = 256
        with contextlib.ExitStack() as sc_:
            wkv = load_w(sc_, "wkv", w_kv, 8, 2048)
            load_grow(2)
            xt = [sb(sc_, "xtC%d" % i, [128, 2, D], F32) for i in range(2)]
            xnC2 = [sb(sc_, "xnC%d" % i, [128, 2, D], BF16) for i in range(2)]
            xnTC2 = [sb(sc_, "xnTC%d" % i, [128, 8, TC], BF16) for i in range(2)]
            KTs2 = [sb(sc_, "KTs%d" % i, [128, 8, TC], BF16) for i in range(2)]
            KVo2 = [sb(sc_, "KVo%d" % i, [128, 2, 2 * D], F32) for i in range(2)]
            VAs2 = [sb(sc_, "VAs%d" % i, [128, 2, NH, 65], BF16) for i in range(2)]
            for i in range(2):
                P.op('dve', lambda: nc.vector.memset(VAs2[i][:, :, :, :], 1.0), [], ['VAs%d' % i])
            KTv = KTd.rearrange("(c h r) t -> c h r t", h=NH, r=128)
            nC = T // TC

            def loadC(i):
                t0 = i * TC
                xs = xt[i % 2]
                xsn = "xtC%d" % (i % 2)
                P.dma('sp', xsn, lambda: nc.sync.dma_start(
                    out=xs[:, :, :], in_=X2[t0:t0 + TC, :].rearrange("(j p) f -> p j f", p=128)), [], [xsn])

            def frontC(i):
                sl = i % 2
                xs = xt[sl]
                xsn = "xtC%d" % sl
                xn, xnT = xnC2[sl], xnTC2[sl]
                XN, XNT = 'xnC%d' % sl, 'xnTC%d' % sl
                yield 0.0
                norm_T(lambda j: xs[:, j, :], xsn, 2, 128, xn, None, do_T=False, xnn=XN)
                for c in range(8):
                    yield 5.0 + 0.5 * c
                    ps, pn = next_psT()
                    for j in range(2):
                        P.op('pe', lambda: nc.tensor.transpose(out=ps[:, j * 128:(j + 1) * 128],
                                                               in_=xn[:128, j, c * 128:(c + 1) * 128],
                                                               identity=ident[:128, :128]), [XN, 'ident'], [pn])
                    if c % 2 == 0:
                        P.op('act', lambda: nc.scalar.copy(out=xnT[:, c, :], in_=ps[:, :TC]), [pn], [XNT])
                    else:
                        P.op('dve', lambda: nc.vector.tensor_copy(out=xnT[:, c, :], in_=ps[:, :TC]), [pn], [XNT])

            def backC(i):
                sl = i % 2
                t0 = i * TC
                xnT, XNT = xnTC2[sl], 'xnTC%d' % sl
                KTs, KTn = KTs2[sl], 'KTs%d' % sl
                KVo, KVn = KVo2[sl], 'KVo%d' % sl
                VAs, VAn = VAs2[sl], 'VAs%d' % sl
                for oc in range(8):
                    yield 0.2 + 1.3 * oc
                    ps, pn = next_psM()
                    for kc in range(8):
                        P.op('pe', lambda: nc.tensor.matmul(ps[:, :TC], lhsT=wkv[:, kc, oc * 128:(oc + 1) * 128],
                                                            rhs=xnT[:, kc, :], start=(kc == 0), stop=(kc == 7)),
                             ['wkv', XNT], [pn])
                    P.op('act', lambda: nc.scalar.copy(out=KTs[:, oc, :], in_=ps[:, :TC]), [pn], [KTn])
                yield 10.7
                ch, off = t0 // 512, t0 % 512
                for hh in range(2):
                    P.dma('sp', 'KTs%d' % hh, lambda: nc.sync.dma_start(
                        out=KTv[ch, hh::2, 0:64, off:off + TC].rearrange("h r t -> r h t"),
                        in_=KTs[hh * 64:(hh + 1) * 64, :, :]), [KTn], [])
                for j in range(2):
                    for q4 in range(4):
                        yield 11.0 + 2.2 * (4 * j + q4)
                        ps, pn = next_psM()
                        for kc in range(8):
                            P.op('pe', lambda: nc.tensor.matmul(ps[:, :], lhsT=xnT[:, kc, j * 128:(j + 1) * 128],
                                                                rhs=wkv[:, kc, q4 * 512:(q4 + 1) * 512],
                                                                start=(kc == 0), stop=(kc == 7)), ['wkv', XNT], [pn])
                        if q4 % 2 == 0:
                            P.op('act', lambda: nc.scalar.copy(out=KVo[:, j, q4 * 512:(q4 + 1) * 512], in_=ps[:, :]),
                                 [pn], [KVn])
                        else:
                            P.op('dve', lambda: nc.vector.tensor_copy(out=KVo[:, j, q4 * 512:(q4 + 1) * 512],
                                                                      in_=ps[:, :]), [pn], [KVn])
                    P.op('pool', lambda: nc.gpsimd.tensor_copy(
                        out=VAs[:, j, :, 0:64], in_=KVo[:, j, D:2 * D].rearrange("p (h e) -> p h e", e=64)),
                        [KVn], [VAn])
                yield 30.0
                P.dma('sp', 'KVo_k', lambda: nc.sync.dma_start(
                    out=k_out[t0:t0 + TC, :].rearrange("(j p) f -> p j f", p=128), in_=KVo[:, :, 0:D]), [KVn], [])
                P.dma('sp', 'KVo_v', lambda: nc.sync.dma_start(
                    out=v_out[t0:t0 + TC, :].rearrange("(j p) f -> p j f", p=128), in_=KVo[:, :, D:2 * D]), [KVn], [])
                for j in range(2):
                    P.dma('sp', 'VAs_1', lambda: nc.sync.dma_start(
                        out=Vd[:, t0 + j * 128:t0 + (j + 1) * 128, :].rearrange("h p e -> p h e"),
                        in_=VAs[:, j, :, :]), [VAn], [])
                P.dma('sp', 'VAs_2', lambda: nc.sync.dma_start(
                    out=Vd2[t0:t0 + TC, :].rearrange("(j p) f -> p j f", p=128),
                    in_=VAs[:, :, :, :].rearrange("p j h e -> p j (h e)")), [VAn], [])

            loadC(0)
            if nC > 1:
                loadC(1)
            run_merged([frontC(0)])
            for i in range(nC):
                if i + 2 < nC:
                    loadC(i + 2)
                gens = [backC(i)]
                if i + 1 < nC:
                    gens.append(frontC(i + 1))
                run_merged(gens)
            xs = xt[0]
            xn, xnT, KVo = xnC2[0], xnTC2[0], KVo2[0]
            P.dma('sp', 'xtC0', lambda: nc.sync.dma_start(out=xs[:, 0, :], in_=XS2[:, :]), [], ['xtC0'])
            norm_T(lambda j: xs[:, j, :], 'xtC0', 1, 128, xn, xnT, xnn='xnC0', xnTn='xnTC0')
            for q4 in range(4):
                ps, pn = next_psM()
                for kc in range(8):
                    P.op('pe', lambda: nc.tensor.matmul(ps[:, :], lhsT=xnT[:, kc, 0:128],
                                                        rhs=wkv[:, kc, q4 * 512:(q4 + 1) * 512],
                                                        start=(kc == 0), stop=(kc == 7)), ['wkv', 'xnTC0'], [pn])
                P.op('act', lambda: nc.scalar.copy(out=KVo[:, 0, q4 * 512:(q4 + 1) * 512], in_=ps[:, :]),
                     [pn], ['KVo0'])
            P.dma('sp', 'KVo_k', lambda: nc.sync.dma_start(out=ks_out[:, :], in_=KVo[:, 0, 0:D]), ['KVo0'], [])
            P.dma('sp', 'KVo_v', lambda: nc.sync.dma_start(out=vs_out[:, :], in_=KVo[:, 0, D:2 * D]), ['KVo0'], [])
            P.dma('sp', 'KVo_s', lambda: nc.sync.dma_start(out=KVS[:, :], in_=KVo[:, 0, :]), ['KVo0'], [])
            P.barrier()


        PB = 4
        with contextlib.ExitStack() as ss:
            wqS = load_w(ss, "wqS", w_q, 8, 1024)
            woS = load_w(ss, "woS", w_o, 8, 1024)
            load_grow(3)
            xts = sb(ss, "xtS", [128, 1, D], F32)
            xns = sb(ss, "xnS", [128, 1, D], BF16)
            xnTs = sb(ss, "xnTS", [128, 8, 128], BF16)
            Qs = sb(ss, "Qs", [128, D], F32)
            id32 = sb(ss, "id32S", [128, 128], F32)
            ones32 = sb(ss, "ones32", [128, 128], F32)
            onesb = sb(ss, "onesb", [128, 128], BF16)
            ownidx = sb(ss, "ownidx", [128, 1], I32)
            ownbc = sb(ss, "ownbc", [128, NS], I32)
            iop = sb(ss, "iop", [128, 1], F32)
            ptb_i = sb(ss, "ptb_i", [128, NS * NPG], I32)
            ptb_f = sb(ss, "ptb_f", [128, NS * NPG], F32)
            pidx = sb(ss, "pidx", [128, NS * NPG // 2], I32)
            pidx_f = sb(ss, "pidx_f", [128, NS * NPG // 2], F32)
            Qo = sb(ss, "Qo", [NS, D], F32)
            KVo_ = sb(ss, "KVown", [NS, 2 * D], F32)
            x2o = sb(ss, "x2o", [128, 1, D], F32)
            Kp = [sb(ss, "Kp%d" % i, [128, PB, D], BF16) for i in range(2)]
            Vp = [sb(ss, "Vp%d" % i, [128, NPG, D], BF16) for i in range(2)]
            qb = [sb(ss, "qb%d" % i, [128, D], BF16) for i in range(2)]
            prod = sb(ss, "prod", [128, PB, D], F32)
            Sq = sb(ss, "Sq", [128, NPG, 16], F32)
            Eq = sb(ss, "Eq", [128, NPG, 16], BF16)
            gateS = sb(ss, "gateS", [128, 8, 16], F32)
            top8s = sb(ss, "top8s", [128, 16, 8], F32)
            selS = sb(ss, "selS", [128, 8, 16], F32)
            numT = sb(ss, "numT", [128, 8, NS], F32)
            denr = sb(ss, "denr", [128, 16], F32)
            num = sb(ss, "num", [NS, D], F32)
            dens = sb(ss, "dens", [NS, 16], F32)
            snew = sb(ss, "snew", [NS, 16], F32)
            pr0 = sb(ss, "pr0", [NS, D], F32)
            attnb = sb(ss, "attnb", [128, 1, D], BF16)
            attnT = sb(ss, "attnT", [128, 8, 128], BF16)
            P.op('dve', lambda: nc.vector.tensor_copy(out=id32[:, :], in_=ident[:, :]), ['ident'], ['id32S'])
            P.op('dve', lambda: nc.vector.memset(ones32[:, :], 1.0), [], ['ones32'])
            P.op('dve', lambda: nc.vector.memset(onesb[:, :], 1.0), [], ['onesb'])
            P.op('dve', lambda: nc.vector.memset(x2o[:, :, :], 0.0), [], ['x2o'])
            P.op('pool', lambda: nc.gpsimd.memset(attnb[:, :, :], 0.0), [], ['attnb'])
            P.dma('sp', 'c_own', lambda: nc.sync.dma_start(out=ownidx[:, :], in_=ownidx_d[:, :]), [], ['ownidx'])
            P.dma('sp', 'c_ownbc', lambda: nc.sync.dma_start(out=ownbc[:, :], in_=ownbc_d[:, :]), [], ['ownbc'])
            P.dma('sp', 'c_iop', lambda: nc.sync.dma_start(out=iop[:, :], in_=iop_d[:, :]), [], ['iop'])
            P.dma('sp', 'c_ptab', lambda: nc.sync.dma_start(
                out=ptb_i[:, :], in_=ptab_d.rearrange("b s -> (b s)").partition_broadcast(128)), [], ['ptb_i'])
            P.op('dve', lambda: nc.vector.tensor_copy(out=ptb_f[:, :], in_=ptb_i[:, :]), ['ptb_i'], ['ptb_f'])
            ptb_v = ptb_f[:, :].rearrange("p (j t) -> p j t", t=2)
            for hp_ in range(2):
                P.op('dve', lambda: nc.vector.tensor_scalar(out=pidx_f[hp_ * 64:(hp_ + 1) * 64, :],
                                                            in0=ptb_v[hp_ * 64:(hp_ + 1) * 64, :, hp_], scalar1=64.0,
                                                            scalar2=iop[hp_ * 64:(hp_ + 1) * 64, 0:1], op0=ALU.mult,
                                                            op1=ALU.add), ['ptb_f', 'iop'], ['pidx_f'])
            P.op('dve', lambda: nc.vector.tensor_copy(out=pidx[:, :], in_=pidx_f[:, :]), ['pidx_f'], ['pidx'])
            P.dma('sp', 'xtS', lambda: nc.sync.dma_start(out=xts[:, 0, :], in_=XS2[:, :]), [], ['xtS'])
            norm_T(lambda j: xts[:, j, :], 'xtS', 1, 128, xns, xnTs)
            for hf in range(2):
                ps, pn = next_psM()
                for kc in range(8):
                    P.op('pe', lambda: nc.tensor.matmul(ps[:, :], lhsT=xnTs[:, kc, :],
                                                        rhs=wqS[:, kc, hf * 512:(hf + 1) * 512],
                                                        start=(kc == 0), stop=(kc == 7)), ['wqS', 'xnT'], [pn])
                P.op('act', lambda: nc.scalar.copy(out=Qs[:, hf * 512:(hf + 1) * 512], in_=ps[:, :]), [pn], ['Qs'])
            P.dma('sp', 'Qd', lambda: nc.sync.dma_start(out=Qd[:, :], in_=Qs[:, :]), ['Qs'], ['Qd'])
            P.dma('pool', 'Qo', lambda: nc.gpsimd.indirect_dma_start(
                out=Qo[:, :], out_offset=None, in_=Qd[:, :],
                in_offset=bass.IndirectOffsetOnAxis(ap=ownidx[0:NS, 0:1], axis=0)), ['ownidx', 'Qd'], ['Qo'])
            P.dma('pool', 'KVown', lambda: nc.gpsimd.indirect_dma_start(
                out=KVo_[:, :], out_offset=None, in_=KVS[:, :],
                in_offset=bass.IndirectOffsetOnAxis(ap=ownidx[0:NS, 0:1], axis=0)), ['ownidx'], ['KVown'])
            P.dma('pool', 'x2o', lambda: nc.gpsimd.indirect_dma_start(
                out=x2o[0:NS, 0, :], out_offset=None, in_=XS2[:, :],
                in_offset=bass.IndirectOffsetOnAxis(ap=ownidx[0:NS, 0:1], axis=0)), ['ownidx'], ['x2o'])
            for i in range(NS):
                sl = i % 2
                P.dma('pool', 'qb%d' % sl, lambda: nc.gpsimd.indirect_dma_start(
                    out=qb[sl][:, :], out_offset=None, in_=Qd[:, :],
                    in_offset=bass.IndirectOffsetOnAxis(ap=ownbc[:, i:i + 1], axis=0)), ['ownbc', 'Qd'], ['qb%d' % sl])
                NPJ = NPG // 2
                for j_ in range(NPJ):
                    P.dma('pool', 'Vp%d' % sl, lambda: nc.gpsimd.indirect_dma_start(
                        out=Vp[sl][:, 2 * j_:2 * j_ + 2, :].rearrange("p t f -> p (t f)"), out_offset=None,
                        in_=cv_d[:, :],
                        in_offset=bass.IndirectOffsetOnAxis(ap=pidx[:, i * NPJ + j_:i * NPJ + j_ + 1], axis=0)),
                        ['pidx'], ['Vp%d' % sl])
                for bq in range(NPG // PB):
                    ksl = (i * (NPG // PB) + bq) % 2
                    for g_ in range(PB // 2):
                        j_ = bq * (PB // 2) + g_
                        P.dma('pool', 'Kp%d' % ksl, lambda: nc.gpsimd.indirect_dma_start(
                            out=Kp[ksl][:, 2 * g_:2 * g_ + 2, :].rearrange("p t f -> p (t f)"), out_offset=None,
                            in_=ck_d[:, :],
                            in_offset=bass.IndirectOffsetOnAxis(ap=pidx[:, i * NPJ + j_:i * NPJ + j_ + 1], axis=0)),
                            ['pidx'], ['Kp%d' % ksl])
                    P.op('dve', lambda: nc.vector.tensor_tensor(
                        out=prod[:, :, :], in0=Kp[ksl][:, :, :],
                        in1=qb[sl][:, :].unsqueeze(1).broadcast_to([128, PB, D]), op=ALU.mult),
                        ['Kp%d' % ksl, 'qb%d' % sl], ['prod'])
                    P.op('dve', lambda: nc.vector.tensor_reduce(
                        out=Sq[:, bq * PB:(bq + 1) * PB, :], in_=prod[:, :, :].rearrange("p g (h e) -> p g h e", e=64),
                        axis=AX.X, op=ALU.add), ['prod'], ['Sq'])
                ps, pn = next_psM()
                P.op('pe', lambda: nc.tensor.matmul(ps[:, 0:256], lhsT=ones32[:, :],
                                                    rhs=Sq[:, :, :].rearrange("p s h -> p (s h)"), start=True, stop=True),
                     ['ones32', 'Sq'], [pn])
                gv = ps[:, 0:256].rearrange("p (n t h) -> p n t h", t=2, h=16)
                P.op('dve', lambda: nc.vector.tensor_copy(out=gateS[:, :, :], in_=gv[:, :, 0, :]), [pn], ['gateS'])
                P.op('dve', lambda: nc.vector.tensor_tensor(out=gateS[:, :, :], in0=gv[:, :, 1, :], in1=gateS[:, :, :],
                                                            op=ALU.add), [pn, 'gateS'], ['gateS'])
                for h in range(NH):
                    P.op('dve', lambda: nc.vector.max(out=top8s[:, h, :], in_=gateS[:, :, h]), ['gateS'], ['top8s'])
                P.op('dve', lambda: nc.vector.tensor_tensor(
                    out=selS[:, :, :], in0=gateS[:, :, :],
                    in1=top8s[:, :, 2:3].rearrange("p h o -> p o h").broadcast_to([128, 8, 16]), op=ALU.is_ge),
                    ['gateS', 'top8s'], ['selS'])
                P.op('act', lambda: nc.scalar.activation(out=Sq[:, :, :], in_=Sq[:, :, :], func=AF.Exp, scale=0.125),
                     ['Sq'], ['Sq'])
                for t_ in range(2):
                    P.op('dve', lambda: nc.vector.tensor_tensor(
                        out=Eq[:, :, :].rearrange("p (n t) h -> p n t h", t=2)[:, :, t_, :],
                        in0=Sq[:, :, :].rearrange("p (n t) h -> p n t h", t=2)[:, :, t_, :], in1=selS[:, :, :],
                        op=ALU.mult), ['Sq', 'selS'], ['Eq'])
                ps, pn = next_psM()
                P.op('pe', lambda: nc.tensor.matmul(ps[:, 0:256], lhsT=onesb[:, :],
                                                    rhs=Eq[:, :, :].rearrange("p s h -> p (s h)"), start=True, stop=True),
                     ['onesb', 'Eq'], [pn])
                P.op('dve', lambda: nc.vector.tensor_reduce(out=denr[:, :],
                                                            in_=ps[:, 0:256].rearrange("p (s h) -> p h s", h=16),
                                                            axis=AX.X, op=ALU.add), [pn], ['denr'])
                P.dma('sp', 'Dn', lambda: nc.sync.dma_start(out=Dn[i:i + 1, :], in_=denr[0:1, :]), ['denr'], ['Dn'])
                ps, pn = next_psM()
                for c in range(8):
                    for s_ in range(NPG):
                        P.op('pe', lambda: nc.tensor.matmul(ps[:, c * 16:(c + 1) * 16],
                                                            lhsT=Vp[sl][:, s_, c * 128:(c + 1) * 128], rhs=Eq[:, s_, :],
                                                            start=(s_ == 0), stop=(s_ == NPG - 1)),
                             ['Vp%d' % sl, 'Eq'], [pn])
                P.op('act', lambda: nc.scalar.copy(out=numT[0:64, :, i], in_=ps[0:64, 0:127:18]), [pn], ['numT'])
                P.op('dve', lambda: nc.vector.tensor_copy(out=numT[64:128, :, i], in_=ps[64:128, 1:128:18]),
                     [pn], ['numT'])
            ps, pn = next_psM()
            ps2, pn2 = next_psM()
            for c in range(8):
                pp_ = ps if c < 4 else ps2
                P.op('pe', lambda: nc.tensor.transpose(out=pp_[0:NS, (c % 4) * 128:(c % 4 + 1) * 128],
                                                       in_=numT[:, c, :], identity=id32[:, :]),
                     ['numT', 'id32S'], [pn if c < 4 else pn2])
            P.op('act', lambda: nc.scalar.copy(out=num[:, 0:512], in_=ps[0:NS, :]), [pn], ['num'])
            P.op('dve', lambda: nc.vector.tensor_copy(out=num[:, 512:1024], in_=ps2[0:NS, :]), [pn2], ['num'])
            P.dma('sp', 'dens', lambda: nc.sync.dma_start(out=dens[:, :], in_=Dn[:, :]), ['Dn'], ['dens'])
            P.op('dve', lambda: nc.vector.tensor_tensor(out=pr0[:, :], in0=Qo[:, :], in1=KVo_[:, 0:D], op=ALU.mult),
                 ['Qo', 'KVown'], ['pr0'])
            P.op('dve', lambda: nc.vector.tensor_reduce(out=snew[:, :], in_=pr0[:, :].rearrange("p (h e) -> p h e", e=64),
                                                        axis=AX.X, op=ALU.add), ['pr0'], ['snew'])
            P.op('act', lambda: nc.scalar.activation(out=snew[:, :], in_=snew[:, :], func=AF.Exp, scale=0.125),
                 ['snew'], ['snew'])
            P.op('dve', lambda: nc.vector.tensor_tensor(
                out=pr0[:, :].rearrange("p (h e) -> p h e", e=64), in0=KVo_[:, D:2 * D].rearrange("p (h e) -> p h e", e=64),
                in1=snew[:, :].unsqueeze(2).broadcast_to([NS, 16, 64]), op=ALU.mult), ['KVown', 'snew'], ['pr0'])
            P.op('dve', lambda: nc.vector.tensor_tensor(out=num[:, :], in0=num[:, :], in1=pr0[:, :], op=ALU.add),
                 ['num', 'pr0'], ['num'])
            P.op('dve', lambda: nc.vector.tensor_tensor(out=dens[:, :], in0=dens[:, :], in1=snew[:, :], op=ALU.add),
                 ['dens', 'snew'], ['dens'])
            P.op('dve', lambda: nc.vector.reciprocal(out=dens[:, :], in_=dens[:, :]), ['dens'], ['dens'])
            P.op('dve', lambda: nc.vector.tensor_tensor(
                out=attnb[0:NS, 0, :].rearrange("p (h e) -> p h e", e=64), in0=num[:, :].rearrange("p (h e) -> p h e", e=64),
                in1=dens[:, :].unsqueeze(2).broadcast_to([NS, 16, 64]), op=ALU.mult), ['num', 'dens', 'attnb'], ['attnb'])
            for c in range(8):
                pt_, ptn = next_psT()
                P.op('pe', lambda: nc.tensor.transpose(out=pt_[:, 0:128], in_=attnb[:, 0, c * 128:(c + 1) * 128],
                                                       identity=ident[:, :]), ['attnb', 'ident'], [ptn])
                P.op('act', lambda: nc.scalar.copy(out=attnT[:, c, :], in_=pt_[:, 0:128]), [ptn], ['attnT'])
            for hf in range(2):
                ps, pn = next_psM()
                for c in range(8):
                    P.op('pe', lambda: nc.tensor.matmul(ps[:, :], lhsT=attnT[:, c, :], rhs=woS[:, c, hf * 512:(hf + 1) * 512],
                                                        start=(c == 0), stop=(c == 7)), ['attnT', 'woS'], [pn])
                P.op('dve', lambda: nc.vector.tensor_tensor(out=x2o[:, 0, hf * 512:(hf + 1) * 512], in0=ps[:, :],
                                                            in1=x2o[:, 0, hf * 512:(hf + 1) * 512], op=ALU.add),
                     [pn, 'x2o'], ['x2o'])
            P.dma('sp', 'x2os', lambda: nc.sync.dma_start(out=XS3[:, :], in_=x2o[:, 0, :]), ['x2o'], [])
            P.barrier()

        SEGL = 4096
        with contextlib.ExitStack() as sd:
            wq = load_w(sd, "wq", w_q, 8, 1024)
            woH = sb(sd, "woH", [64, NH, D], BF16)
            P.dma('pool', 'w_woH', lambda: nc.gpsimd.dma_start(
                out=woH[:, :, :], in_=w_o.rearrange("(h d) f -> d h f", d=64)), [], ['woH'])
            load_grow(3)
            qidx = sb(sd, "qidx", [128, 4 * G], I32)
            ktidx = sb(sd, "ktidx", [128, 16 * G], I32)
            pastb = sb(sd, "pastb", [128, 4 * G, 32], F32)
            past01 = sb(sd, "past01", [128, 4 * G, 32], F32)
            cmask = sb(sd, "cmask", [128, 2, 256], BF16)
            onesf = sb(sd, "onesf", [128, 64], F32)
            P.dma('sp', 'c_qidx', lambda: nc.sync.dma_start(out=qidx[:, :], in_=qidx_d[:, :]), [], ['qidx'])
            P.dma('sp', 'c_ktidx', lambda: nc.sync.dma_start(out=ktidx[:, :], in_=ktidx_d[:, :]), [], ['ktidx'])
            P.dma('sp', 'c_pastb', lambda: nc.sync.dma_start(out=pastb[:, :, :], in_=pastb_d[:, :, :]), [], ['pastb'])
            P.dma('sp', 'c_past01', lambda: nc.sync.dma_start(out=past01[:, :, :], in_=past01_d[:, :, :]), [], ['past01'])
            P.dma('sp', 'c_cmask', lambda: nc.sync.dma_start(out=cmask[:, :, :], in_=cmask_d[:, :, :]), [], ['cmask'])
            P.op('dve', lambda: nc.vector.memset(onesf[:, :], 1.0), [], ['onesf'])
            KA = [sb(sd, "KA%d" % i, [96, SEGL], BF16) for i in range(2)]
            VA = [sb(sd, "VA%d" % i, [128, SEGL // 128, 65], BF16) for i in range(2)]
            KD = [sb(sd, "KD%d" % i, [128, 512], BF16) for i in range(2)]
            VD = sb(sd, "VD", [128, 4, NH * 65], BF16)
            xt = sb(sd, "xtD", [128, 4, D], F32)
            xn = sb(sd, "xnD", [128, 4, D], BF16)
            xnT = sb(sd, "xnTD", [128, 8, 512], BF16)
            QA = sb(sd, "QA", [96, NH, 512], BF16)
            selp = sb(sd, "selp", [128, NH, 96], BF16)
            gm = sb(sd, "gm", [128, NH, 32], F32)
            selw = sb(sd, "selw", [128, NH, 32], F32)
            top8 = sb(sd, "top8", [128, NH, 8], F32)
            kmT = sb(sd, "kmT", [64, NH, 32], BF16)
            kms = sb(sd, "kms", [64, 32], F32)
            PT = [sb(sd, "PT%d" % i, [128, 512], BF16) for i in range(3)]
            PTd = [sb(sd, "PTd%d" % i, [128, 256], BF16) for i in range(2)]
            ON = sb(sd, "ON", [64, 512], F32)
            rden = sb(sd, "rden", [65, 512], F32)
            AT = sb(sd, "AT", [64, NH, 512], BF16)
            P.op('pool', lambda: nc.gpsimd.memset(selp[:, :, :], 0.0), [], ['selp'])
            KTv = KTd.rearrange("(c h r) t -> c h r t", h=NH, r=128)
            psS = psM[0:3]
            psO = psM[3:5]
            psX = psM[5]

            nseg_all = T // SEGL if T >= SEGL else 1
            segl_all = min(T, SEGL)
            for h in range(NH):
                for sgi in range(nseg_all):
                    base = sgi * segl_all
                    s_ = (h * nseg_all + sgi) % 2
                    P.dma('sp', 'KA%d' % s_, lambda: nc.sync.dma_start(
                        out=KA[s_][0:64, 0:segl_all].rearrange("r (c t) -> r c t", t=512),
                        in_=KTv[base // 512:(base + segl_all) // 512, h, 0:64, :].rearrange("c r t -> r c t")),
                        [], ['KA%d' % s_])
                    nb = segl_all // 256
                    P.op('dve', lambda: nc.vector.tensor_reduce(
                        out=kms[:, 0:nb], in_=KA[s_][0:64, 0:segl_all].rearrange("r (n k) -> r n k", k=256),
                        axis=AX.X, op=ALU.add), ['KA%d' % s_], ['kms'])
                    P.op('dve', lambda: nc.vector.tensor_scalar(
                        out=kmT[:, h, base // 256:base // 256 + nb], in0=kms[:, 0:nb], scalar1=1.0 / 256.0,
                        scalar2=None, op0=ALU.mult), ['kms'], ['kmT'])
            if T < 32 * 256:
                P.op('dve', lambda: nc.vector.memset(kmT[:, :, T // 256:32], 0.0), [], ['kmT'])

            cnt = dict(s=0, seg=0, kd=0, o=0, pd=0)
            for g in range(G):
                L = 2048 * (g + 1)
                for j in range(4):
                    P.dma('pool', 'xtD', lambda: nc.gpsimd.indirect_dma_start(
                        out=xt[:, j, :], out_offset=None, in_=X2[:, :],
                        in_offset=bass.IndirectOffsetOnAxis(ap=qidx[:, g * 4 + j:g * 4 + j + 1], axis=0)),
                        ['qidx'], ['xtD'])
                    P.dma('pool', 'VD', lambda: nc.gpsimd.indirect_dma_start(
                        out=VD[:, j, :], out_offset=None, in_=Vd2[:, :],
                        in_offset=bass.IndirectOffsetOnAxis(ap=qidx[:, g * 4 + j:g * 4 + j + 1], axis=0)),
                        ['qidx'], ['VD'])
                norm_T(lambda j: xt[:, j, :], 'xtD', 4, 128, xn, xnT)
                for h in range(NH):
                    for kc in range(8):
                        P.op('pe', lambda: nc.tensor.matmul(psX[0:64, :], lhsT=wq[:, kc, h * 64:(h + 1) * 64],
                                                            rhs=xnT[:, kc, :], start=(kc == 0), stop=(kc == 7)),
                             ['wq', 'xnT'], ['psM5'])
                    if h % 2 == 0:
                        P.op('act', lambda: nc.scalar.copy(out=QA[0:64, h, :], in_=psX[0:64, :]), ['psM5'], ['QA'])
                    else:
                        P.op('dve', lambda: nc.vector.tensor_copy(out=QA[0:64, h, :], in_=psX[0:64, :]), ['psM5'], ['QA'])
                for j in range(4):
                    gj = g * 4 + j
                    for h in range(NH):
                        P.op('pe', lambda: nc.tensor.matmul(psX[:, h * 32:(h + 1) * 32],
                                                            lhsT=QA[0:64, h, j * 128:(j + 1) * 128],
                                                            rhs=kmT[0:64, h, :], start=True, stop=True),
                             ['QA', 'kmT'], ['psM5'])
                    P.op('dve', lambda: nc.vector.tensor_tensor(
                        out=gm[:, :, :], in0=psX[:, :].rearrange("p (h n) -> p h n", n=32),
                        in1=pastb[:, gj:gj + 1, :].broadcast_to([128, NH, 32]), op=ALU.add), ['psM5', 'pastb'], ['gm'])
                    for h in range(NH):
                        P.op('dve', lambda: nc.vector.max(out=top8[:, h, :], in_=gm[:, h, :]), ['gm'], ['top8'])
                    P.op('dve', lambda: nc.vector.tensor_tensor(
                        out=selw[:, :, :], in0=gm[:, :, :], in1=top8[:, :, 2:3].broadcast_to([128, NH, 32]),
                        op=ALU.is_ge), ['gm', 'top8'], ['selw'])
                    P.op('dve', lambda: nc.vector.tensor_tensor(
                        out=selw[:, :, :], in0=selw[:, :, :], in1=past01[:, gj:gj + 1, :].broadcast_to([128, NH, 32]),
                        op=ALU.mult), ['selw', 'past01'], ['selw'])
                    P.op('dve', lambda: nc.vector.tensor_scalar(
                        out=selp[:, :, 64:96], in0=selw[:, :, :], scalar1=-NEGB, scalar2=NEGB, op0=ALU.mult,
                        op1=ALU.add), ['selw'], ['selp'])
                    for h in range(NH):
                        k4 = h % 4
                        if k4 == 0:
                            oi = (h // 4) % 2
                            ps4, pn4 = psO[oi], 'psM%d' % (3 + oi)
                        P.op('pe', lambda: nc.tensor.matmul(ps4[0:96, k4 * 128:(k4 + 1) * 128],
                                                            lhsT=selp[:, h, :], rhs=ident[:, :], start=True, stop=True),
                             ['selp', 'ident'], [pn4])
                        if k4 == 3:
                            h0 = h - 3
                            P.op('act', lambda: nc.scalar.copy(
                                out=QA[64:96, h0:h0 + 4, j * 128:(j + 1) * 128],
                                in_=ps4[64:96, 0:512].rearrange("p (a q) -> p a q", q=128)), [pn4], ['QA'])
                segs = [(b0, min(SEGL, L - b0)) for b0 in range(0, L, SEGL)]
                LOOK = 2
                jobs = []
                for h in range(NH):
                    for sgi, (base, sl) in enumerate(segs):
                        for kt in range(sl // 128):
                            jobs.append(dict(t='s', h=h, base=base, sl=sl, kt=kt, segfirst=(kt == 0),
                                             first=(sgi == 0 and kt == 0)))
                    for tt in range(4):
                        jobs.append(dict(t='d', h=h, tt=tt, dfirst=(tt == 0), last=(tt == 3)))
                hstate = {}

                def emit_qk(jb):
                    h = jb['h']
                    si = cnt['s'] % 3
                    cnt['s'] += 1
                    jb['si'] = si
                    if jb['t'] == 's':
                        if jb['segfirst']:
                            s_ = cnt['seg'] % 2
                            cnt['seg'] += 1
                            base, sl = jb['base'], jb['sl']
                            nkt = sl // 128
                            P.dma('sp', 'KA%d' % s_, lambda: nc.sync.dma_start(
                                out=KA[s_][0:64, 0:sl].rearrange("r (c t) -> r c t", t=512),
                                in_=KTv[base // 512:(base + sl) // 512, h, 0:64, :].rearrange("c r t -> r c t")),
                                [], ['KA%d' % s_])
                            P.dma('sp', 'KAi%d' % s_, lambda: nc.sync.dma_start(
                                out=KA[s_][64:96, 0:sl], in_=ind_d[:, base:base + sl]), [], ['KAi%d' % s_])
                            P.dma('sp', 'VA%d' % s_, lambda: nc.sync.dma_start(
                                out=VA[s_][:, 0:nkt, :],
                                in_=Vd[h, base:base + sl, :].rearrange("(p k) e -> p k e", k=nkt)), [], ['VA%d' % s_])
                            hstate['seg'] = s_
                        s_ = hstate['seg']
                        jb['s_'] = s_
                        sl = jb['sl']
                        nkt = sl // 128
                        kav = KA[s_][0:96, 0:sl].rearrange("r (p k) -> r k p", k=nkt)
                        P.op('pe', lambda: nc.tensor.matmul(psS[si][:, :], lhsT=kav[:, jb['kt'], :], rhs=QA[0:96, h, :],
                                                            start=True, stop=True),
                             ['KA%d' % s_, 'KAi%d' % s_, 'QA'], ['psM%d' % si])
                        P.op('act', lambda: nc.scalar.activation(out=PT[si][:, :], in_=psS[si][:, :], func=AF.Exp,
                                                                  scale=0.125), ['psM%d' % si], ['PT%d' % si])
                    else:
                        if jb['dfirst']:
                            kd_i = cnt['kd'] % 2
                            cnt['kd'] += 1
                            P.dma('pool', 'KD%d' % kd_i, lambda: nc.gpsimd.indirect_dma_start(
                                out=KD[kd_i][:, :], out_offset=None, in_=KTd[:, :],
                                in_offset=bass.IndirectOffsetOnAxis(ap=ktidx[:, g * 16 + h:g * 16 + h + 1], axis=0)),
                                ['ktidx'], ['KD%d' % kd_i])
                            hstate['kd'] = kd_i
                        kd_i = hstate['kd']
                        tt = jb['tt']
                        X, i2 = tt // 2, tt % 2
                        P.op('pe', lambda: nc.tensor.matmul(psS[si][:, 0:256],
                                                            lhsT=KD[kd_i][0:64, tt * 128:(tt + 1) * 128],
                                                            rhs=QA[0:64, h, X * 256:(X + 1) * 256],
                                                            start=True, stop=True),
                             ['KD%d' % kd_i, 'QA'], ['psM%d' % si])
                        P.op('act', lambda: nc.scalar.activation(out=PT[si][:, 0:256], in_=psS[si][:, 0:256],
                                                                  func=AF.Exp, scale=0.125),
                             ['psM%d' % si], ['PT%d' % si])
                        P.op('pool', lambda: nc.gpsimd.tensor_tensor(out=PT[si][:, 0:256], in0=PT[si][:, 0:256],
                                                                     in1=cmask[:, i2, :], op=ALU.mult),
                             ['PT%d' % si, 'cmask'], ['PT%d' % si])

                def emit_pv(jb):
                    h = jb['h']
                    si = jb['si']
                    if jb['t'] == 's':
                        if jb['first']:
                            o_i = cnt['o'] % 2
                            cnt['o'] += 1
                            hstate['o'] = o_i
                        o_i = hstate['o']
                        po, pon = psO[o_i], 'psM%d' % (3 + o_i)
                        s_ = jb['s_']
                        P.op('pe', lambda: nc.tensor.matmul(po[0:65, :], lhsT=VA[s_][:, jb['kt'], :], rhs=PT[si][:, :],
                                                            start=jb['first'], stop=False),
                             ['VA%d' % s_, 'PT%d' % si], [pon])
                    else:
                        o_i = hstate['o']
                        po, pon = psO[o_i], 'psM%d' % (3 + o_i)
                        tt = jb['tt']
                        X = tt // 2
                        P.op('pe', lambda: nc.tensor.matmul(po[0:65, X * 256:(X + 1) * 256],
                                                            lhsT=VD[:, tt, h * 65:(h + 1) * 65], rhs=PT[si][:, 0:256],
                                                            start=False, stop=jb['last']),
                             ['VD', 'PT%d' % si], [pon])
                        if jb['last']:
                            P.op('dve', lambda: nc.vector.reciprocal(out=rden[64:65, :], in_=po[64:65, :]),
                                 [pon], ['rden'])
                            P.op('pe', lambda: nc.tensor.matmul(psX[0:64, :], lhsT=onesf[64:65, 0:64],
                                                                rhs=rden[64:65, :], start=True, stop=True),
                                 ['onesf', 'rden'], ['psM5'])
                            P.op('act', lambda: nc.scalar.copy(out=ON[:, :], in_=po[0:64, :]), [pon], ['ON'])
                            P.op('dve', lambda: nc.vector.tensor_tensor(out=AT[:, h, :], in0=psX[0:64, :], in1=ON[:, :],
                                                                        op=ALU.mult), ['psM5', 'ON'], ['AT'])

                for idx in range(len(jobs) + LOOK):
                    if idx < len(jobs):
                        emit_qk(jobs[idx])
                    if idx >= LOOK:
                        emit_pv(jobs[idx - LOOK])
                for j in range(4):
                    for hf in range(2):
                        for h in range(NH):
                            P.op('pe', lambda: nc.tensor.matmul(psX[:, :], lhsT=AT[0:64, h, j * 128:(j + 1) * 128],
                                                                rhs=woH[0:64, h, hf * 512:(hf + 1) * 512],
                                                                start=(h == 0), stop=(h == NH - 1)),
                                 ['AT', 'woH'], ['psM5'])
                        P.op('dve', lambda: nc.vector.tensor_tensor(out=xt[:, j, hf * 512:(hf + 1) * 512],
                                                                    in0=psX[:, :], in1=xt[:, j, hf * 512:(hf + 1) * 512],
                                                                    op=ALU.add), ['psM5', 'xtD'], ['xtD'])
                P.dma('sp', 'xtDs', lambda: nc.sync.dma_start(
                    out=X3[g * 512:(g + 1) * 512, :].rearrange("(j p) f -> p j f", p=128), in_=xt[:, :, :]),
                    ['xtD'], [])
            P.barrier()

        phase_mlp(1, [(X3, y_out, 512 * G, 256), (XS3, ys_out, 128, 128)], 4, final_idx=5)

        P.barrier()
    return nc


_NC_CACHE = {}


def _bf16(a):
    return np.asarray(a, dtype=np.float32).astype(ml_dtypes.bfloat16)


def _fm(v):
    return np.ascontiguousarray(np.asarray(v, np.float32).reshape(8, 128).T)


def kernel(x_prompt, x_sample, state_conv, state_h, cache_k, cache_v, page_table,
           norm_mix, norm_mlp, w_ff1, w_ff2, w_rg_in, b_rg_in, conv_w, conv_b,
           w_gate_a, b_gate_a, w_gate_i, b_gate_i, lru_lambda, w_rg_out, b_rg_out,
           norm_kv, w_kv, w_q, w_o, norm_out):
    f32 = np.float32
    x_prompt = np.asarray(x_prompt, f32)
    B, T, _ = x_prompt.shape
    G = T // 2048
    nphys = int(np.asarray(cache_k).shape[0])
    if (G, nphys) not in _NC_CACHE:
        _NC_CACHE[(G, nphys)] = build(G, nphys)
    nc = _NC_CACHE[(G, nphys)]

    vecs = np.zeros((128, NV), f32)
    b_in = np.asarray(b_rg_in, f32)[0]
    vecs[:, 0:8] = _fm(b_in[:D])
    vecs[:, 8:16] = _fm(b_in[D:])
    for k in range(4):
        vecs[:, 16 + 8 * k:24 + 8 * k] = _fm(np.asarray(conv_w, f32)[0, k])
    vecs[:, 48:56] = _fm(np.asarray(conv_b, f32)[0])
    vecs[:, 56:64] = _fm(np.asarray(b_gate_a, f32)[0].reshape(-1))
    vecs[:, 64:72] = _fm(np.asarray(b_gate_i, f32)[0].reshape(-1))
    vecs[:, 72:80] = _fm(np.asarray(lru_lambda, f32)[0])
    rows = np.stack([np.asarray(norm_mix, f32)[0], np.asarray(norm_mlp, f32)[0], np.asarray(norm_kv, f32),
                     np.asarray(norm_mix, f32)[1], np.asarray(norm_mlp, f32)[1], np.asarray(norm_out, f32),
                     np.asarray(b_rg_out, f32)[0]])
    rows = np.concatenate([rows, b_in[None, D:]], axis=0)
    rows = np.ascontiguousarray(np.broadcast_to(rows[:, None, :], (8, 128, D)))
    ind = _bf16((np.arange(T)[None, :] // 256) == np.arange(32)[:, None])
    kk = np.arange(128)[:, None, None]
    ii = np.arange(2)[None, :, None]
    qq = np.arange(256)[None, None, :]
    cmask = _bf16(qq >= ii * 128 + kk)
    ident = _bf16(np.eye(128))
    common = dict(
        w_ff1=np.asarray(w_ff1, f32), w_ff2=np.asarray(w_ff2, f32), w_rg_in=np.asarray(w_rg_in, f32)[0],
        w_ga=np.asarray(w_gate_a, f32)[0], w_gi=np.asarray(w_gate_i, f32)[0],
        w_rg_out=np.asarray(w_rg_out, f32)[0], w_kv=np.asarray(w_kv, f32), w_q=np.asarray(w_q, f32)[0],
        w_o=np.asarray(w_o, f32)[0], vecs=vecs, rows=rows, ind=ind, cmask=cmask, ident=ident)
    common.update(
        xs=np.ascontiguousarray(np.asarray(x_sample, f32)[:, 0, :]),
        sconv=np.ascontiguousarray(np.asarray(state_conv, f32)[0]),
        sh=np.ascontiguousarray(np.asarray(state_h, f32)[0]),
        iop=(np.arange(128) % 64).astype(f32).reshape(128, 1),
        ck=np.asarray(cache_k, f32).reshape(-1, 2 * D), cv=np.asarray(cache_v, f32).reshape(-1, 2 * D))
    ptab_all = np.asarray(page_table).astype(np.int32)
    in_maps = []
    chunks = []
    p = np.arange(128)
    for c in range(8):
        s, j = c // 4, c % 4
        cgs = [4 * g + (j if g % 2 == 0 else 3 - j) for g in range(G)]
        chunks.append(cgs)
        qidx = np.zeros((128, 4 * G), np.int32)
        ktidx = np.zeros((128, 16 * G), np.int32)
        for g, cg in enumerate(cgs):
            for jj in range(4):
                qidx[:, g * 4 + jj] = cg * 512 + jj * 128 + p
            for h in range(NH):
                ktidx[:, g * 16 + h] = (cg * NH + h) * 128 + p
        qblk = qidx // 256
        past01 = (np.arange(32)[None, None, :] < qblk[:, :, None]).astype(f32)
        pastb = ((past01 - 1.0) * 1e30).astype(f32)
        m = dict(common)
        ownidx = np.zeros((128, 1), np.int32)
        ownidx[:NS, 0] = NS * c + np.arange(NS)
        ownbc = np.ascontiguousarray(np.broadcast_to((NS * c + np.arange(NS, dtype=np.int32))[None, :], (128, NS)))
        m.update(xp=np.ascontiguousarray(x_prompt[s]), qidx=qidx, ktidx=ktidx, past01=past01, pastb=pastb,
                 ptab=np.ascontiguousarray(ptab_all[NS * c:NS * (c + 1)]), ownidx=ownidx, ownbc=ownbc)
        in_maps.append(m)
    res = run_bass_kernel_spmd(nc, in_maps, core_ids=list(range(8))).results

    y_prompt = np.zeros((B, T, D), f32)
    for c in range(8):
        s = c // 4
        for g, cg in enumerate(chunks[c]):
            y_prompt[s, cg * 512:(cg + 1) * 512] = res[c]["y"][g * 512:(g + 1) * 512]
    k_p = np.stack([res[0]["k"], res[4]["k"]]).reshape(B, T, NH, HD)
    v_p = np.stack([res[0]["v"], res[4]["v"]]).reshape(B, T, NH, HD)
    conv_p = np.stack([res[0]["convp"], res[4]["convp"]])[None]
    h_p = np.stack([res[0]["hp"], res[4]["hp"]])[None]
    y_s = np.concatenate([res[c]["ys"][:NS] for c in range(8)], axis=0)[:, None, :]
    conv_s = res[0]["convs"][None]
    h_s = res[0]["hs"][None]
    k_s = res[0]["ks"].reshape(128, 1, NH, HD)
    v_s = res[0]["vs"].reshape(128, 1, NH, HD)
    return (y_prompt, y_s, conv_p, h_p, k_p, v_p, conv_s, h_s, k_s, v_s)
```
